# Optimizing a Trainium2 kernel written in Bass

```python
import math
import jax, jax.numpy as jnp
from jax import lax
import numpy as np

D_MODEL = 2048
BATCH = 4
SEQ = 2048
DEPTH = 2

N_MIXERS = 2
N_A = (DEPTH + 1) // 2
N_B = DEPTH // 2
S5_WIDTH = D_MODEL
S5_GROUP = 16
S5_GROUPS = S5_WIDTH // S5_GROUP
S5_STATE = 64
S5_DT_MIN = 1e-3
S5_DT_MAX = 1e-1
LRU_WIDTH = ((4 * D_MODEL // 3 + 255) // 256) * 256
LRU_BLOCKS = 16
LRU_BLOCK = LRU_WIDTH // LRU_BLOCKS
LRU_C = 8.0
CONV_WIDTH = 4
FFN_HIDDEN = ((8 * D_MODEL // 3 + 255) // 256) * 256
N_MOD = 6
EPS = 1e-6

kernel_name = "adaln_hybrid_s5_rglru_swiglu"


def rmsnorm(x, gain):
    xf = x.astype(jnp.float32)
    y = xf * lax.rsqrt(jnp.mean(xf * xf, axis=-1, keepdims=True) + EPS) * gain.astype(jnp.float32)
    return y.astype(x.dtype)


def _complex_affine_combine(left, right):
    a1r, a1i, b1r, b1i = left
    a2r, a2i, b2r, b2i = right
    return (a2r * a1r - a2i * a1i,
            a2r * a1i + a2i * a1r,
            a2r * b1r - a2i * b1i + b2r,
            a2r * b1i + a2i * b1r + b2i)


def _real_affine_combine(left, right):
    a1, b1 = left
    a2, b2 = right
    return (a2 * a1, a2 * b1 + b2)


def s5_mixer(h, w_in, lam_re, lam_im, log_dt, b_re, b_im, c_re, c_im, d_skip, w_glu):
    f32 = jnp.float32
    bsz, seq, _ = h.shape
    u = h @ w_in
    uf = u.astype(f32)
    ug = uf.reshape(bsz, seq, S5_GROUPS, S5_GROUP)
    dt = jnp.exp(log_dt.astype(f32))[:, None]
    lr = lam_re.astype(f32)
    li = lam_im.astype(f32)
    mag = jnp.exp(lr * dt)
    ab_re = mag * jnp.cos(li * dt)
    ab_im = mag * jnp.sin(li * dt)
    nr, ni = ab_re - 1.0, ab_im
    den = lr * lr + li * li
    f_re = (nr * lr + ni * li) / den
    f_im = (ni * lr - nr * li) / den
    br, bi = b_re.astype(f32), b_im.astype(f32)
    bb_re = f_re[..., None] * br - f_im[..., None] * bi
    bb_im = f_re[..., None] * bi + f_im[..., None] * br
    bu_re = jnp.einsum('blgc,gpc->blgp', ug, bb_re)
    bu_im = jnp.einsum('blgc,gpc->blgp', ug, bb_im)
    a_re = jnp.broadcast_to(ab_re[None, None], (1, seq, S5_GROUPS, S5_STATE))
    a_im = jnp.broadcast_to(ab_im[None, None], (1, seq, S5_GROUPS, S5_STATE))
    _, _, s_re, s_im = lax.associative_scan(
        _complex_affine_combine, (a_re, a_im, bu_re, bu_im), axis=1)
    y = (jnp.einsum('blgp,gcp->blgc', s_re, c_re.astype(f32))
         - jnp.einsum('blgp,gcp->blgc', s_im, c_im.astype(f32)))
    y = y.reshape(bsz, seq, S5_WIDTH) + d_skip.astype(f32) * uf
    y = jax.nn.gelu(y).astype(h.dtype)
    val, gate = jnp.split(y @ w_glu, 2, axis=-1)
    return val * jax.nn.sigmoid(gate)


def causal_depthwise_conv(x, w, b):
    y = lax.conv_general_dilated(
        x, w.astype(x.dtype), window_strides=(1,), padding=[(CONV_WIDTH - 1, 0)],
        dimension_numbers=('NWC', 'WIO', 'NWC'), feature_group_count=x.shape[-1])
    return y + b.astype(x.dtype)


def rglru_mixer(h, w_in, conv_w, conv_b, w_rg, b_rg, w_ig, b_ig, lam, w_out):
    f32 = jnp.float32
    bsz, seq, _ = h.shape
    gate_branch, xb = jnp.split(h @ w_in, 2, axis=-1)
    xb = causal_depthwise_conv(xb, conv_w, conv_b).astype(f32)
    xblk = xb.reshape(bsz, seq, LRU_BLOCKS, LRU_BLOCK)
    r = jax.nn.sigmoid(jnp.einsum('blhi,hij->blhj', xblk, w_rg.astype(f32)).reshape(bsz, seq, LRU_WIDTH)
                       + b_rg.astype(f32))
    ig = jax.nn.sigmoid(jnp.einsum('blhi,hij->blhj', xblk, w_ig.astype(f32)).reshape(bsz, seq, LRU_WIDTH)
                        + b_ig.astype(f32))
    log_a = -LRU_C * r * jax.nn.softplus(-lam.astype(f32))
    a = jnp.exp(log_a)
    mult = jnp.sqrt(-jnp.expm1(2.0 * log_a))
    _, hs = lax.associative_scan(_real_affine_combine, (a, mult * (ig * xb)), axis=1)
    y = hs * jax.nn.gelu(gate_branch.astype(f32))
    return y.astype(h.dtype) @ w_out


def swiglu(h, w_gu, w_down):
    g, u = jnp.split(h @ w_gu, 2, axis=-1)
    return (jax.nn.silu(g) * u) @ w_down


def setup_inputs(seed: int = 0) -> dict:
    key = jax.random.key(seed)
    ks = jax.random.split(key, 32)
    f32 = jnp.float32
    nrm = lambda k, shape, s: jax.random.normal(k, shape, f32) * s
    D = D_MODEL
    x = nrm(ks[0], (BATCH, SEQ, D), 1.0)
    c = nrm(ks[1], (BATCH, D), 1.0)
    norm_g = 1.0 + nrm(ks[2], (DEPTH, 2, D), 0.02)
    w_ada = nrm(ks[3], (DEPTH, D, N_MOD * D), 0.5 * D ** -0.5)
    b_ada = nrm(ks[4], (DEPTH, N_MOD * D), 0.02)
    s5_w_in = nrm(ks[5], (N_A, D, S5_WIDTH), D ** -0.5)
    n = jnp.arange(S5_STATE, dtype=f32)
    s5_lam_re = -0.5 + nrm(ks[6], (N_A, S5_GROUPS, S5_STATE), 0.01)
    s5_lam_im = math.pi * n + nrm(ks[7], (N_A, S5_GROUPS, S5_STATE), 0.01)
    s5_log_dt = jax.random.uniform(ks[8], (N_A, S5_GROUPS), f32,
                                   math.log(S5_DT_MIN), math.log(S5_DT_MAX))
    s5_b_re = nrm(ks[9], (N_A, S5_GROUPS, S5_STATE, S5_GROUP), (2 * S5_GROUP) ** -0.5)
    s5_b_im = nrm(ks[10], (N_A, S5_GROUPS, S5_STATE, S5_GROUP), (2 * S5_GROUP) ** -0.5)
    s5_c_re = nrm(ks[11], (N_A, S5_GROUPS, S5_GROUP, S5_STATE), (2 * S5_STATE) ** -0.5)
    s5_c_im = nrm(ks[12], (N_A, S5_GROUPS, S5_GROUP, S5_STATE), (2 * S5_STATE) ** -0.5)
    s5_d = nrm(ks[13], (N_A, S5_WIDTH), 1.0)
    s5_w_glu = nrm(ks[14], (N_A, S5_WIDTH, 2 * D), S5_WIDTH ** -0.5)
    lru_w_in = nrm(ks[15], (N_B, D, 2 * LRU_WIDTH), D ** -0.5)
    lru_conv_w = nrm(ks[16], (N_B, CONV_WIDTH, 1, LRU_WIDTH), CONV_WIDTH ** -0.5)
    lru_conv_b = nrm(ks[17], (N_B, LRU_WIDTH), 0.02)
    lru_w_rg = nrm(ks[18], (N_B, LRU_BLOCKS, LRU_BLOCK, LRU_BLOCK), LRU_BLOCK ** -0.5)
    lru_b_rg = nrm(ks[19], (N_B, LRU_WIDTH), 0.1)
    lru_w_ig = nrm(ks[20], (N_B, LRU_BLOCKS, LRU_BLOCK, LRU_BLOCK), LRU_BLOCK ** -0.5)
    lru_b_ig = nrm(ks[21], (N_B, LRU_WIDTH), 0.1)
    a_pow = jax.random.uniform(ks[22], (N_B, LRU_WIDTH), f32, 0.9, 0.999)
    a0 = a_pow ** (1.0 / LRU_C)
    lru_lam = jnp.log(a0) - jnp.log1p(-a0)
    lru_w_out = nrm(ks[23], (N_B, LRU_WIDTH, D), LRU_WIDTH ** -0.5)
    ffn_w_gu = nrm(ks[24], (DEPTH, D, 2 * FFN_HIDDEN), D ** -0.5)
    ffn_w_down = nrm(ks[25], (DEPTH, FFN_HIDDEN, D), FFN_HIDDEN ** -0.5)
    final_g = 1.0 + nrm(ks[26], (D,), 0.02)
    return {"x": x, "c": c, "norm_g": norm_g, "w_ada": w_ada, "b_ada": b_ada,
            "s5_w_in": s5_w_in, "s5_lam_re": s5_lam_re, "s5_lam_im": s5_lam_im,
            "s5_log_dt": s5_log_dt, "s5_b_re": s5_b_re, "s5_b_im": s5_b_im,
            "s5_c_re": s5_c_re, "s5_c_im": s5_c_im, "s5_d": s5_d, "s5_w_glu": s5_w_glu,
            "lru_w_in": lru_w_in, "lru_conv_w": lru_conv_w, "lru_conv_b": lru_conv_b,
            "lru_w_rg": lru_w_rg, "lru_b_rg": lru_b_rg, "lru_w_ig": lru_w_ig,
            "lru_b_ig": lru_b_ig, "lru_lam": lru_lam, "lru_w_out": lru_w_out,
            "ffn_w_gu": ffn_w_gu, "ffn_w_down": ffn_w_down, "final_g": final_g}


def reference(x, c, norm_g, w_ada, b_ada,
              s5_w_in, s5_lam_re, s5_lam_im, s5_log_dt, s5_b_re, s5_b_im,
              s5_c_re, s5_c_im, s5_d, s5_w_glu,
              lru_w_in, lru_conv_w, lru_conv_b, lru_w_rg, lru_b_rg, lru_w_ig,
              lru_b_ig, lru_lam, lru_w_out,
              ffn_w_gu, ffn_w_down, final_g):
    cond = jax.nn.silu(c)
    for i in range(DEPTH):
        mod = cond @ w_ada[i] + b_ada[i]
        sh1, sc1, g1, sh2, sc2, g2 = [m[:, None, :] for m in jnp.split(mod, N_MOD, axis=-1)]
        h = rmsnorm(x, norm_g[i, 0]) * (1.0 + sc1) + sh1
        j = i // N_MIXERS
        if i % N_MIXERS == 0:
            y = s5_mixer(h, s5_w_in[j], s5_lam_re[j], s5_lam_im[j], s5_log_dt[j],
                         s5_b_re[j], s5_b_im[j], s5_c_re[j], s5_c_im[j], s5_d[j], s5_w_glu[j])
        else:
            y = rglru_mixer(h, lru_w_in[j], lru_conv_w[j], lru_conv_b[j], lru_w_rg[j],
                            lru_b_rg[j], lru_w_ig[j], lru_b_ig[j], lru_lam[j], lru_w_out[j])
        x = x + g1 * y
        h = rmsnorm(x, norm_g[i, 1]) * (1.0 + sc2) + sh2
        x = x + g2 * swiglu(h, ffn_w_gu[i], ffn_w_down[i])
    return rmsnorm(x, final_g)
```

```python
import os
from contextlib import ExitStack
import numpy as np
import concourse.bass as bass
import concourse.mybir as mybir
from concourse.bass_utils import run_bass_kernel_spmd

F32 = mybir.dt.float32
BF16 = mybir.dt.bfloat16
AF = mybir.ActivationFunctionType
ALU = mybir.AluOpType
AX = mybir.AxisListType

D = 2048
T = 2048
TB = 512
NTB = T // TB
KD = D // 128
FH = 5632
KF = FH // 128
LW = 2816
KL = LW // 128
EPS = 1e-6
NCORES = 4


class Buf:
    def __init__(self, name):
        self.name = name
        self.w = None
        self.r = []
        self.sem = None
        self.cnt = 0


class Prog:
    ENGS = ["sync", "scalar", "vector", "gpsimd", "tensor"]

    def __init__(self, nc, es):
        self.nc = nc
        self.es = es
        self.q = {e: [] for e in self.ENGS}
        self.esem = {}
        self.ecnt = {}
        for e in ["scalar", "vector", "gpsimd", "tensor"]:
            self.esem[e] = es.enter_context(nc.semaphore("es_" + e))
            self.ecnt[e] = 0
        self.known = {e: {} for e in self.ENGS}
        self.nsem = 4
        self.semobj = {}

    def _waits(self, eng, toks):
        need = {}
        for t in toks:
            if t is None:
                continue
            s, v = t
            if self.known[eng].get(id(s), 0) >= v:
                continue
            if need.get(id(s), (s, 0))[1] < v:
                need[id(s)] = (s, v)
        out = []
        for k, (s, v) in need.items():
            self.known[eng][k] = v
            out.append((s, v))
        return out

    def _deps(self, reads, writes):
        toks = []
        for b in reads:
            toks.append(b.w)
        for b in writes:
            toks.append(b.w)
            toks.extend(b.r)
        return toks

    def op(self, eng, fn, reads=(), writes=(), pe_same_ok=False):
        toks = self._deps(reads, writes)
        if pe_same_ok:
            toks = [t for t in toks if t is None or t[0] is not self.esem["tensor"]]
        w = self._waits(eng, toks)
        self.ecnt[eng] += 1
        tok = (self.esem[eng], self.ecnt[eng])
        self.q[eng].append((w, fn, (self.esem[eng], 1)))
        for b in writes:
            b.w = tok
            b.r = []
        for b in reads:
            if b not in writes:
                b.r.append(tok)
        return tok

    def mm(self, fns, reads=(), writes=()):
        toks = self._deps(reads, writes)
        toks = [t for t in toks if t is None or t[0] is not self.esem["tensor"]]
        w = self._waits("tensor", toks)
        n = len(fns)
        for i, fn in enumerate(fns):
            if i == n - 1:
                self.ecnt["tensor"] += 1
                tok = (self.esem["tensor"], self.ecnt["tensor"])
                self.q["tensor"].append((w if i == 0 else [], fn, (self.esem["tensor"], 1)))
            else:
                self.q["tensor"].append((w if i == 0 else [], fn, None))
        for b in writes:
            b.w = tok
            b.r = []
        for b in reads:
            if b not in writes:
                b.r.append(tok)
        return tok

    def dma(self, eng, out_ap, in_ap, src, dst, sembuf=None):
        sb = sembuf if sembuf is not None else dst
        if sb.sem is None:
            sb.sem = self.es.enter_context(self.nc.semaphore("d_" + sb.name))
            self.nsem += 1
        toks = self._deps([src], [dst])
        w = self._waits(eng, toks)
        sb.cnt += 16
        tok = (sb.sem, sb.cnt)
        self.q[eng].append((w, lambda e, o=out_ap, i=in_ap: e.dma_start(out=o, in_=i), (sb.sem, 16)))
        dst.w = tok
        dst.r = []
        src.r.append(tok)
        return tok

    def fence(self, eng, bufs):
        toks = []
        for b in bufs:
            toks.append(b.w)
            toks.extend(b.r)
        w = self._waits(eng, toks)
        if w:
            self.q[eng].append((w, None, None))

    def emit(self):
        nc = self.nc
        with nc.Block() as block:
            def run(name):
                def body(eng):
                    for (w, fn, inc) in self.q[name]:
                        for (s, v) in w:
                            eng.wait_ge(s, v)
                        if fn is None:
                            continue
                        ins = fn(eng)
                        if inc is not None:
                            ins.then_inc(inc[0], inc[1])
                return body
            block.sync(run("sync"))
            block.scalar(run("scalar"))
            block.vector(run("vector"))
            block.gpsimd(run("gpsimd"))
            block.tensor(run("tensor"))


class Ring:
    def __init__(self, nc, es, name, n, shape, dt):
        self.t = [es.enter_context(nc.sbuf_tensor(f"sb_{name}{i}", shape, dt)) for i in range(n)]
        self.b = [Buf(f"{name}{i}") for i in range(n)]
        self.i = 0
        self.n = n

    def next(self):
        k = self.i % self.n
        self.i += 1
        return self.t[k], self.b[k]


def build(stop_after=99, dbg=None):
    nc = bass.Bass("TRN2", target_bir_lowering=False)
    es = ExitStack()
    P = Prog(nc, es)

    def din(name, shape, dt=F32):
        return nc.dram_tensor(name, list(shape), dt, kind="ExternalInput").ap()

    def dscr(name, shape, dt):
        return nc.dram_tensor(name, list(shape), dt).ap()

    x_d = din("x", [T, D])
    c_d = din("c", [KD, 128])
    ident_d = din("ident", [128, 128])
    out_d = nc.dram_tensor("out", [T, D], F32, kind="ExternalOutput").ap()

    XT = dscr("XT", [D, T], F32)
    XTb = Buf("XT")
    HT = dscr("HT", [D, T], BF16)
    HTb = Buf("HT")

    def sb(name, shape, dt):
        return es.enter_context(nc.sbuf_tensor("sb_" + name, list(shape), dt))

    w_ada_d = din("w_ada", [2, D, 6 * D])
    vecA_d = din("vecA", [128, 128])
    vecB_d = din("vecB", [2, 96, 128])
    vecC_d = din("vecC", [128, 128])
    vecD_d = din("vecD", [128, 128])
    ffn_gu_d = din("ffn_w_gu", [2, D, 2 * FH])
    ffn_dn_d = din("ffn_w_down", [2, FH, D])
    wbuf_d = Buf("wdram")

    HID = dscr("HID", [FH, T], BF16)
    HIDb = Buf("HID")

    ident = sb("ident", [128, 128], F32)
    identb = Buf("ident")
    P.dma("sync", ident[:], ident_d, Buf("identd"), identb)
    ones_bf = sb("ones_bf", [128, 128], BF16)
    onesb = Buf("ones")
    P.op("vector", lambda e: e.memset(ones_bf[:], 1.0), writes=[onesb])

    psum = [es.enter_context(nc.psum_tensor(f"ps{i}", [128, 512], F32)) for i in range(8)]
    psb = [Buf(f"ps{i}") for i in range(8)]
    pctr = [0]

    def next_ps():
        k = pctr[0] % 7
        pctr[0] += 1
        return psum[k], psb[k]

    big = Ring(nc, es, "big", 2, [128, 2048], F32)
    xto = Ring(nc, es, "xto", 2, [128, KD, 128], F32)
    hbo = Ring(nc, es, "hbo", 2, [128, KD, 128], BF16)
    tmpf = Ring(nc, es, "tmpf", 4, [128, 512], F32)
    tmpb = Ring(nc, es, "tmpb", 3, [128, 512], BF16)
    wring = Ring(nc, es, "wr", 2, [128, KL, 512], BF16)
    act = sb("act", [128, KL, T], BF16)
    actb = Buf("act")

    colsA = sb("colsA", [128, 128], F32)
    colsB = sb("colsB", [128, 2, 96], F32)
    colsC = sb("colsC", [128, 128], F32)
    colsD = sb("colsD", [128, 128], F32)
    colsAb, colsBb, colsCb, colsDb = Buf("cA"), Buf("cB"), Buf("cC"), Buf("cD")
    vdb = Buf("vecd")

    def load_cols(src_ap, nrows, dst_ap, dstb):
        st, stb = big.next()
        P.dma("sync", st[:nrows, :128], src_ap, vdb, stb)
        ps, pb = next_ps()
        P.mm([lambda e: e.transpose(ps[:, :nrows], st[:nrows, :128], ident[:nrows, :nrows])],
             reads=[stb, identb], writes=[pb])
        P.op("vector", lambda e: e.tensor_copy(dst_ap, ps[:, :nrows]), reads=[pb], writes=[dstb])

    load_cols(vecA_d, 128, colsA[:, :], colsAb)
    load_cols(vecB_d[0], 96, colsB[:, 0, :], colsBb)
    load_cols(vecB_d[1], 96, colsB[:, 1, :], colsBb)
    load_cols(vecC_d, 128, colsC[:, :], colsCb)
    load_cols(vecD_d, 128, colsD[:, :], colsDb)
    condT = sb("condT", [128, KD], BF16)
    condb = Buf("cond")
    P.op("scalar", lambda e: e.activation(condT[:], colsA[:, 0:16], AF.Silu), reads=[colsAb], writes=[condb])

    mod = sb("mod", [128, 2, 96], F32)
    modb = Buf("mod")
    gs = sb("gs", [128, 2, 2, KD], F32)
    gsb = Buf("gs")
    ada_steps = []

    def ada_step(L, j):
        psm, pbm = psum[7], psb[7]
        wt, wb = wring.next()
        P.dma("gpsimd", wt[:, :KD, :],
              w_ada_d[L, :, j * 512:(j + 1) * 512].rearrange("(k p) n -> p k n", p=128), wbuf_d, wb)
        for m in range(4):
            col = j * 4 + m
            fns = []
            for k in range(KD):
                fns.append(lambda e, wt=wt, m=m, k=k, col=col, psm=psm: e.matmul(
                    psm[:, col:col + 1], lhsT=wt[:, k, m * 128:(m + 1) * 128], rhs=condT[:, k:k + 1],
                    start=(k == 0), stop=(k == KD - 1)))
            P.mm(fns, reads=[wb, condb], writes=[pbm])
        if j == 23:
            P.op("vector", lambda e, L=L, psm=psm: e.tensor_tensor(
                out=mod[:, L, :], in0=psm[:, 0:96], in1=colsB[:, L, :], op=ALU.add),
                reads=[pbm, colsBb], writes=[modb])
            for sub in range(2):
                sc0 = (sub * 3 + 1) * 16
                P.op("vector", lambda e, L=L, sub=sub, sc0=sc0: e.scalar_tensor_tensor(
                    out=gs[:, L, sub, :], in0=mod[:, L, sc0:sc0 + 16], scalar=1.0,
                    in1=colsA[:, 16 + (L * 2 + sub) * 16: 16 + (L * 2 + sub) * 16 + 16],
                    op0=ALU.add, op1=ALU.mult), reads=[modb, colsAb], writes=[gsb])

    for L in range(2):
        for j in range(24):
            ada_steps.append((L, j))

    def mcol(L, idx, k):
        return mod[:, L, idx * 16 + k: idx * 16 + k + 1]

    xdb = Buf("xd")
    XTv = XT.rearrange("(k p) t -> p k t", p=128)
    HTv = HT.rearrange("(k p) t -> p k t", p=128)
    def phase0_tile(tt):
        xt_, xb_ = big.next()
        P.dma("sync", xt_[:, :D], x_d[tt * 128:(tt + 1) * 128, :], xdb, xb_)
        ot, ob = xto.next()
        for k4 in range(KD // 4):
            ps, pb = next_ps()
            for j in range(4):
                k = k4 * 4 + j
                P.mm([lambda e, ps=ps, j=j, k=k, xt_=xt_: e.transpose(
                    ps[:, j * 128:(j + 1) * 128], xt_[:, k * 128:(k + 1) * 128], ident[:])],
                    reads=[xb_, identb], writes=[pb])
            if k4 % 2 == 0:
                P.op("vector", lambda e, ps=ps, ot=ot, k4=k4: e.tensor_copy(
                    ot[:, k4 * 4:(k4 + 1) * 4, :], ps[:].rearrange("p (j t) -> p j t", j=4)),
                    reads=[pb], writes=[ob])
            else:
                P.op("scalar", lambda e, ps=ps, ot=ot, k4=k4: e.copy(
                    ot[:, k4 * 4:(k4 + 1) * 4, :], ps[:].rearrange("p (j t) -> p j t", j=4)),
                    reads=[pb], writes=[ob])
        P.dma("sync", XTv[:, :, tt * 128:(tt + 1) * 128], ot[:], ob, XTb, sembuf=ob)

    for i, (L_, j_) in enumerate(ada_steps):
        ada_step(L_, j_)
        if i % 3 == 2:
            phase0_tile(i // 3)
    P.fence("sync", xto.b)

    def norm_phase(L, sub):
        P.fence("sync", [XTb])
        for q in range(T // 128):
            xt_, xb_ = big.next()
            xv = xt_[:].rearrange("p (k t) -> p k t", k=KD)
            P.dma("sync", xv, XTv[:, :, q * 128:(q + 1) * 128], XTb, xb_)
            ht, hb = hbo.next()
            P.op("scalar", lambda e, ht=ht, xv=xv: e.activation(ht[:], xv, AF.Square), reads=[xb_], writes=[hb])
            ps, pb = next_ps()
            fns = []
            for k in range(KD):
                fns.append(lambda e, ps=ps, ht=ht, k=k: e.matmul(
                    ps[:, :128], lhsT=ones_bf[:], rhs=ht[:, k, :], start=(k == 0), stop=(k == KD - 1)))
            P.mm(fns, reads=[hb, onesb], writes=[pb])
            rs, rb = tmpf.next()
            P.op("vector", lambda e, rs=rs, ps=ps: e.tensor_scalar(
                out=rs[:, :128], in0=ps[:, :128], scalar1=1.0 / D, scalar2=EPS, op0=ALU.mult, op1=ALU.add),
                reads=[pb], writes=[rb])
            P.op("scalar", lambda e, rs=rs: e.activation(rs[:, :128], rs[:, :128], AF.Sqrt),
                 reads=[rb], writes=[rb])
            P.op("vector", lambda e, rs=rs: e.reciprocal(rs[:, :128], rs[:, :128]),
                 reads=[rb], writes=[rb])
            for k in range(KD):
                P.op("vector", lambda e, xv=xv, k=k, rs=rs: e.scalar_tensor_tensor(
                    out=xv[:, k, :], in0=xv[:, k, :], scalar=gs[:, L, sub, k:k + 1], in1=rs[:, :128],
                    op0=ALU.mult, op1=ALU.mult), reads=[xb_, rb, gsb], writes=[xb_])
                P.op("scalar", lambda e, xv=xv, k=k, ht=ht: e.activation(
                    ht[:, k, :], xv[:, k, :], AF.Identity, bias=mcol(L, sub * 3, k), scale=1.0),
                    reads=[xb_, modb], writes=[hb])
            P.dma("sync", HTv[:, :, q * 128:(q + 1) * 128], ht[:], hb, HTb, sembuf=hb)
        P.fence("sync", hbo.b)

    def load_act(src_v, KT, srcb):
        P.fence("sync", [srcb])
        for k in range(KT):
            P.dma("sync", act[:, k, :], src_v[:, k, :], srcb, actb)

    def gemm(steps, KT, epilogue, krange=None):
        for si, wlist in enumerate(steps):
            wts = []
            for wap in wlist:
                wt, wb = wring.next()
                P.dma("gpsimd", wt[:, :KT, :], wap.rearrange("(k p) n -> p k n", p=128), wbuf_d, wb)
                wts.append((wt, wb))
            for m in range(4):
                for tb in range(NTB):
                    pss = []
                    for (wt, wb) in wts:
                        ps, pb = next_ps()
                        fns = []
                        k0, k1 = (0, KT - 1) if krange is None else krange(si, m)
                        for k in range(k0, k1 + 1):
                            fns.append(lambda e, ps=ps, wt=wt, m=m, k=k, tb=tb, k0=k0, k1=k1: e.matmul(
                                ps[:], lhsT=wt[:, k, m * 128:(m + 1) * 128],
                                rhs=act[:, k, tb * TB:(tb + 1) * TB], start=(k == k0), stop=(k == k1)))
                        P.mm(fns, reads=[wb, actb], writes=[pb])
                        pss.append((ps, pb))
                    epilogue(si, m, tb, pss)

    def ffn(L):
        norm_phase(L, 1)
        load_act(HTv, KD, HTb)
        HIDv = HID.rearrange("(k p) t -> p k t", p=128)

        def ep_swiglu(si, m, tb, pss):
            (pg, pgb), (pu, pub) = pss
            sg, sgb = tmpf.next()
            P.op("scalar", lambda e: e.activation(sg[:], pg[:], AF.Silu), reads=[pgb], writes=[sgb])
            ho, hob = tmpb.next()
            P.op("vector", lambda e: e.tensor_tensor(out=ho[:], in0=pu[:], in1=sg[:], op=ALU.mult),
                 reads=[pub, sgb], writes=[hob])
            P.dma("sync", HIDv[:, si * 4 + m, tb * TB:(tb + 1) * TB], ho[:], hob, HIDb, sembuf=hob)

        steps = [[ffn_gu_d[L, :, j * 512:(j + 1) * 512], ffn_gu_d[L, :, FH + j * 512: FH + (j + 1) * 512]]
                 for j in range(FH // 512)]
        gemm(steps, KD, ep_swiglu)
        P.fence("sync", tmpb.b)

        def ep_res(si, m, tb, pss):
            (ps, pb), = pss
            kk = si * 4 + m
            xt_, xb_ = tmpf.next()
            P.dma("sync", xt_[:], XTv[:, kk, tb * TB:(tb + 1) * TB], XTb, xb_)
            P.op("vector", lambda e: e.scalar_tensor_tensor(
                out=xt_[:], in0=ps[:], scalar=mcol(L, 5, kk), in1=xt_[:], op0=ALU.mult, op1=ALU.add),
                reads=[pb, xb_, modb], writes=[xb_])
            P.dma("sync", XTv[:, kk, tb * TB:(tb + 1) * TB], xt_[:], xb_, XTb, sembuf=xb_)

        for half in range(2):
            load_act(HIDv[:, half * KL:(half + 1) * KL, :], KL, HIDb)
            steps = [[ffn_dn_d[L, half * LW:(half + 1) * LW, j * 512:(j + 1) * 512]] for j in range(D // 512)]
            gemm(steps, KL, ep_res)
            P.fence("sync", tmpf.b)


    lru_in_d = din("lru_w_in", [D, 2 * LW])
    lru_rg_d = din("lru_rg_dense", [LW, LW])
    lru_ig_d = din("lru_ig_dense", [LW, LW])
    lru_out_d = din("lru_w_out", [LW, D])
    GB = dscr("GB", [LW, T], BF16); GBb = Buf("GB")
    XB = dscr("XB", [LW, T], F32); XBb = Buf("XB")
    XC = dscr("XC", [LW, T], F32); XCb_ = Buf("XC")
    XCh = dscr("XCh", [LW, T], BF16); XChb = Buf("XCh")
    AA = dscr("AA", [LW, T], F32); AAb = Buf("AA")
    BT = dscr("BT", [LW, T], F32); BTb = Buf("BT")
    YL = dscr("YL", [LW, T], BF16); YLb = Buf("YL")
    GBv, XBv, XCv, XChv, AAv, BTv, YLv = [a.rearrange("(k p) t -> p k t", p=128) for a in (GB, XB, XC, XCh, AA, BT, YL)]

    def gelu_tile(src_ap, srcb, n, eng="vector"):
        t1, t1b = tmpf.next()
        P.op("scalar", lambda e: e.activation(t1[:, :n], src_ap, AF.Square), reads=[srcb], writes=[t1b])
        P.op(eng, lambda e: e.tensor_scalar(out=t1[:, :n], in0=t1[:, :n], scalar1=0.044715, scalar2=1.0,
                                            op0=ALU.mult, op1=ALU.add), reads=[t1b], writes=[t1b])
        P.op(eng, lambda e: e.tensor_tensor(out=t1[:, :n], in0=t1[:, :n], in1=src_ap, op=ALU.mult),
             reads=[t1b, srcb], writes=[t1b])
        P.op("scalar", lambda e: e.activation(t1[:, :n], t1[:, :n], AF.Sigmoid, scale=1.5957691216057308),
             reads=[t1b], writes=[t1b])
        ho, hob = tmpb.next()
        P.op(eng, lambda e: e.tensor_tensor(out=ho[:, :n], in0=t1[:, :n], in1=src_ap, op=ALU.mult),
             reads=[t1b, srcb], writes=[hob])
        return ho, hob

    def lru_mixer(L):
        norm_phase(L, 0)
        load_act(HTv, KD, HTb)

        def ep_in(si, m, tb, pss):
            (ps, pb), = pss
            kk = si * 4 + m
            if kk < KL:
                ho, hob = gelu_tile(ps[:], pb, TB)
                P.dma("sync", GBv[:, kk, tb * TB:(tb + 1) * TB], ho[:], hob, GBb, sembuf=hob)
            else:
                xo, xob = tmpf.next()
                P.op("scalar", lambda e: e.copy(xo[:], ps[:]), reads=[pb], writes=[xob])
                P.dma("sync", XBv[:, kk - KL, tb * TB:(tb + 1) * TB], xo[:], xob, XBb, sembuf=xob)

        gemm([[lru_in_d[:, j * 512:(j + 1) * 512]] for j in range(2 * LW // 512)], KD, ep_in)
        P.fence("sync", tmpf.b + tmpb.b)

        spc = sb("spc", [128, KL], F32); spb = Buf("spc")
        ex = sb("spx", [128, KL], F32); exb = Buf("spx")
        P.op("scalar", lambda e: e.activation(ex[:], colsD[:, 44:66], AF.Exp, scale=-1.0), reads=[colsDb], writes=[exb])
        P.op("vector", lambda e: e.tensor_scalar(out=spc[:], in0=ex[:], scalar1=-0.25, scalar2=1.0 / 3.0,
                                                 op0=ALU.mult, op1=ALU.add), reads=[exb], writes=[spb])
        P.op("vector", lambda e: e.tensor_tensor(out=spc[:], in0=spc[:], in1=ex[:], op=ALU.mult), reads=[exb, spb], writes=[spb])
        P.op("vector", lambda e: e.tensor_scalar(out=spc[:], in0=spc[:], scalar1=-0.5, scalar2=None, op0=ALU.add),
             reads=[spb], writes=[spb])
        P.op("vector", lambda e: e.tensor_tensor(out=spc[:], in0=spc[:], in1=ex[:], op=ALU.mult), reads=[exb, spb], writes=[spb])
        P.op("vector", lambda e: e.tensor_scalar(out=spc[:], in0=spc[:], scalar1=1.0, scalar2=None, op0=ALU.add),
             reads=[spb], writes=[spb])
        P.op("vector", lambda e: e.tensor_tensor(out=spc[:], in0=spc[:], in1=ex[:], op=ALU.mult), reads=[exb, spb], writes=[spb])
        P.op("vector", lambda e: e.tensor_scalar(out=spc[:], in0=spc[:], scalar1=-8.0, scalar2=None, op0=ALU.mult),
             reads=[spb], writes=[spb])

        for kt in range(KL):
            xt_, xb_ = big.next()
            P.dma("sync", xt_[:, :T], XBv[:, kt, :], XBb, xb_)
            ot, ob = xto.next()
            ov = ot[:].rearrange("p k t -> p (k t)")
            P.op("vector", lambda e, xt_=xt_, ov=ov, kt=kt: e.tensor_scalar(
                out=ov, in0=xt_[:, :T], scalar1=colsC[:, 3 * KL + kt: 3 * KL + kt + 1],
                scalar2=colsC[:, 88 + kt: 88 + kt + 1], op0=ALU.mult, op1=ALU.add),
                reads=[xb_, colsCb], writes=[ob])
            for sh in (1, 2, 3):
                P.op("vector", lambda e, xt_=xt_, ov=ov, kt=kt, sh=sh: e.scalar_tensor_tensor(
                    out=ov[:, sh:], in0=xt_[:, :T - sh], scalar=colsC[:, (3 - sh) * KL + kt: (3 - sh) * KL + kt + 1],
                    in1=ov[:, sh:], op0=ALU.mult, op1=ALU.add), reads=[xb_, colsCb, ob], writes=[ob])
            hb16, hb16b = hbo.next()
            hv = hb16[:].rearrange("p k t -> p (k t)")
            P.op("scalar", lambda e, hv=hv, ov=ov: e.copy(hv, ov), reads=[ob], writes=[hb16b])
            P.dma("sync", XCv[:, kt, :], ov, ob, XCb_, sembuf=ob)
            P.dma("sync", XChv[:, kt, :], hv, hb16b, XChb, sembuf=hb16b)
        P.fence("sync", xto.b + hbo.b)

        load_act(XChv, KL, XChb)

        def kr(si, m):
            kk = si * 4 + m
            h0 = (128 * kk) // 176
            h1 = (128 * kk + 127) // 176
            return (176 * h0) // 128, min(KL - 1, (176 * h1 + 175) // 128)

        def ep_gate(si, m, tb, pss):
            (pr, prb), (pi, pib) = pss
            kk = si * 4 + m
            r_, rb_ = tmpf.next()
            P.op("scalar", lambda e: e.activation(r_[:], pr[:], AF.Sigmoid, bias=colsD[:, kk:kk + 1], scale=1.0),
                 reads=[prb, colsDb], writes=[rb_])
            i_, ib_ = tmpf.next()
            P.op("scalar", lambda e: e.activation(i_[:], pi[:], AF.Sigmoid, bias=colsD[:, 22 + kk:22 + kk + 1], scale=1.0),
                 reads=[pib, colsDb], writes=[ib_])
            xc_, xcb = tmpf.next()
            P.dma("sync", xc_[:], XCv[:, kk, tb * TB:(tb + 1) * TB], XCb_, xcb)
            P.op("scalar", lambda e: e.activation(r_[:], r_[:], AF.Exp, scale=spc[:, kk:kk + 1]),
                 reads=[rb_, spb], writes=[rb_])
            P.op("vector", lambda e: e.tensor_tensor(out=i_[:], in0=i_[:], in1=xc_[:], op=ALU.mult),
                 reads=[ib_, xcb], writes=[ib_])
            P.op("vector", lambda e: e.tensor_tensor(out=xc_[:], in0=r_[:], in1=r_[:], op=ALU.mult),
                 reads=[rb_, xcb], writes=[xcb])
            P.op("vector", lambda e: e.tensor_scalar(out=xc_[:], in0=xc_[:], scalar1=-1.0, scalar2=1.0,
                                                     op0=ALU.mult, op1=ALU.add), reads=[xcb], writes=[xcb])
            P.op("scalar", lambda e: e.activation(xc_[:], xc_[:], AF.Sqrt), reads=[xcb], writes=[xcb])
            P.op("vector", lambda e: e.tensor_tensor(out=i_[:], in0=i_[:], in1=xc_[:], op=ALU.mult),
                 reads=[ib_, xcb], writes=[ib_])
            P.dma("sync", AAv[:, kk, tb * TB:(tb + 1) * TB], r_[:], rb_, AAb, sembuf=rb_)
            P.dma("sync", BTv[:, kk, tb * TB:(tb + 1) * TB], i_[:], ib_, BTb, sembuf=ib_)

        steps = [[lru_rg_d[:, j * 512: min(LW, (j + 1) * 512)], lru_ig_d[:, j * 512: min(LW, (j + 1) * 512)]]
                 for j in range((LW + 511) // 512)]
        gemm_ragged(steps, KL, ep_gate, kr)
        P.fence("sync", tmpf.b)

        P.fence("sync", [AAb, BTb, GBb])
        for kt in range(KL):
            at, atb = big.next()
            P.dma("sync", at[:, :T], AAv[:, kt, :], AAb, atb)
            bt, btb = xto.next()
            bv = bt[:].rearrange("p k t -> p (k t)")
            P.dma("sync", bv, BTv[:, kt, :], BTb, btb)
            gt, gtb = hbo.next()
            gv = gt[:].rearrange("p k t -> p (k t)")
            P.dma("sync", gv, GBv[:, kt, :], GBb, gtb)
            P.op("vector", lambda e, at=at, bv=bv: e.tensor_tensor_scan(
                out=bv, data0=at[:, :T], data1=bv, initial=0.0, op0=ALU.mult, op1=ALU.add),
                reads=[atb, btb], writes=[btb])
            P.op("vector", lambda e, gv=gv, bv=bv: e.tensor_tensor(out=gv, in0=bv, in1=gv, op=ALU.mult),
                 reads=[btb, gtb], writes=[gtb])
            P.dma("sync", YLv[:, kt, :], gv, gtb, YLb, sembuf=gtb)
        P.fence("sync", hbo.b)

        load_act(YLv, KL, YLb)

        def ep_res1(si, m, tb, pss):
            (ps, pb), = pss
            kk = si * 4 + m
            xt_, xb_ = tmpf.next()
            P.dma("sync", xt_[:], XTv[:, kk, tb * TB:(tb + 1) * TB], XTb, xb_)
            P.op("vector", lambda e: e.scalar_tensor_tensor(
                out=xt_[:], in0=ps[:], scalar=mcol(L, 2, kk), in1=xt_[:], op0=ALU.mult, op1=ALU.add),
                reads=[pb, xb_, modb], writes=[xb_])
            P.dma("sync", XTv[:, kk, tb * TB:(tb + 1) * TB], xt_[:], xb_, XTb, sembuf=xb_)

        gemm([[lru_out_d[:, j * 512:(j + 1) * 512]] for j in range(D // 512)], KL, ep_res1)
        P.fence("sync", tmpf.b)

    def gemm_ragged(steps, KT, epilogue, krange):
        for si, wlist in enumerate(steps):
            wts = []
            ncol = wlist[0].shape[1]
            for wap in wlist:
                wt, wb = wring.next()
                P.dma("gpsimd", wt[:, :KT, :ncol], wap.rearrange("(k p) n -> p k n", p=128), wbuf_d, wb)
                wts.append((wt, wb))
            for m in range(ncol // 128):
                for tb in range(NTB):
                    pss = []
                    for (wt, wb) in wts:
                        ps, pb = next_ps()
                        fns = []
                        k0, k1 = krange(si, m)
                        for k in range(k0, k1 + 1):
                            fns.append(lambda e, ps=ps, wt=wt, m=m, k=k, tb=tb, k0=k0, k1=k1: e.matmul(
                                ps[:], lhsT=wt[:, k, m * 128:(m + 1) * 128],
                                rhs=act[:, k, tb * TB:(tb + 1) * TB], start=(k == k0), stop=(k == k1)))
                        P.mm(fns, reads=[wb, actb], writes=[pb])
                        pss.append((ps, pb))
                    epilogue(si, m, tb, pss)

    if dbg == "LRU":
        lru_mixer(1)

    PI = 3.141592653589793
    s5_in_d = din("s5_w_in", [D, D])
    s5_glu_d = din("s5_w_glu", [D, 2 * D])
    lamre_d = din("s5_lam_re", [128, 64]); lamim_d = din("s5_lam_im", [128, 64]); logdt_d = din("s5_log_dt", [128, 1])
    bre_d = din("s5_b_re", [128, 1024]); bim_d = din("s5_b_im", [128, 1024])
    cre_d = din("s5_c_re", [128, 1024]); cim_d = din("s5_c_im", [128, 1024])
    s5d_d = din("s5_d_row", [1, D])
    WST = dscr("WST", [128, 128, 128], BF16); WINT = dscr("WINT", [128, 128, 128], BF16)
    WOR = dscr("WOR", [128, 64, 128], BF16); WOI = dscr("WOI", [128, 64, 128], BF16)
    WSTb, WINTb, WORb, WOIb = Buf("WST"), Buf("WINT"), Buf("WOR"), Buf("WOI")
    YT = dscr("YT", [D, T], BF16); YTb = Buf("YT")
    YTv = YT.rearrange("(k p) t -> p k t", p=128)
    identh = sb("identh", [128, 128], BF16); identhb = Buf("identh")
    P.op("vector", lambda e: e.tensor_copy(identh[:], ident[:]), reads=[identb], writes=[identhb])
    cA4 = sb("cA4", [64, 2, 2, 128], F32); cAb = Buf("cA")
    cA1 = cA4[:, 0, :, :]; cA2 = cA4[:, 1, :, :]
    vpb = [Buf("vst0"), Buf("vst1")]
    dtab = sb("dtab", [128, D], F32); dtabb = Buf("dtab")

    def V(fn, r, w):
        return P.op("vector", fn, reads=r, writes=w)

    def A(fn, r, w):
        return P.op("scalar", fn, reads=r, writes=w)

    def tt(out, a, b, op):
        return lambda e: e.tensor_tensor(out=out, in0=a, in1=b, op=op)

    def s5_prep():
        nat = sb("s5nat", [128, 130], F32); natb = Buf("s5nat")
        P.dma("sync", nat[:, 0:64], lamre_d, vdb, natb)
        P.dma("sync", nat[:, 64:128], lamim_d, vdb, natb)
        P.dma("sync", nat[:, 128:129], logdt_d, vdb, natb)
        P.dma("sync", dtab[:], s5d_d.partition_broadcast(128), vdb, dtabb)
        Lr = sb("s5Lr", [128, 9, 64], F32); Li = sb("s5Li", [128, 9, 64], F32); Lb = Buf("s5L")
        sm = sb("s5sm", [128, 12, 64], F32); smb = Buf("s5sm")
        lr = nat[:, 0:64]; li = nat[:, 64:128]
        dtc = sm[:, 11, 0:1]
        A(lambda e: e.activation(dtc, nat[:, 128:129], AF.Exp), [natb], [smb])
        lrdt, lidt, mag, sa, ca, t1, t2, fr, fi, nr, den = [sm[:, i, :] for i in range(11)]
        V(lambda e: e.tensor_scalar(out=lrdt, in0=lr, scalar1=dtc, scalar2=None, op0=ALU.mult), [natb, smb], [smb])
        V(lambda e: e.tensor_scalar(out=lidt, in0=li, scalar1=dtc, scalar2=None, op0=ALU.mult), [natb, smb], [smb])
        A(lambda e: e.activation(mag, lrdt, AF.Exp), [smb], [smb])
        A(lambda e: e.activation(sa, lidt, AF.Sin, scale=1.0 / 16.0), [smb], [smb])
        V(lambda e: e.tensor_scalar(out=ca, in0=lidt, scalar1=1.0 / 16.0, scalar2=PI / 2, op0=ALU.mult, op1=ALU.add), [smb], [smb])
        A(lambda e: e.activation(ca, ca, AF.Sin), [smb], [smb])
        for _ in range(4):
            V(tt(t1, ca, ca, ALU.mult), [smb], [smb])
            V(tt(t2, sa, sa, ALU.mult), [smb], [smb])
            V(tt(sa, ca, sa, ALU.mult), [smb], [smb])
            V(lambda e: e.tensor_scalar(out=sa, in0=sa, scalar1=2.0, scalar2=None, op0=ALU.mult), [smb], [smb])
            V(tt(ca, t1, t2, ALU.subtract), [smb], [smb])
        V(lambda e: e.memset(Lr[:, 0, :], 1.0), [], [Lb])
        V(lambda e: e.memset(Li[:, 0, :], 0.0), [], [Lb])
        V(tt(Lr[:, 1, :], mag, ca, ALU.mult), [smb], [Lb])
        V(tt(Li[:, 1, :], mag, sa, ALU.mult), [smb], [Lb])
        for k in range(2, 9):
            V(tt(t1, Lr[:, k - 1, :], Lr[:, 1, :], ALU.mult), [Lb], [smb])
            V(tt(t2, Li[:, k - 1, :], Li[:, 1, :], ALU.mult), [Lb], [smb])
            V(tt(Lr[:, k, :], t1, t2, ALU.subtract), [smb], [Lb])
            V(tt(t1, Lr[:, k - 1, :], Li[:, 1, :], ALU.mult), [Lb], [smb])
            V(tt(t2, Li[:, k - 1, :], Lr[:, 1, :], ALU.mult), [Lb], [smb])
            V(tt(Li[:, k, :], t1, t2, ALU.add), [smb], [Lb])
        V(lambda e: e.tensor_scalar(out=nr, in0=Lr[:, 1, :], scalar1=-1.0, scalar2=None, op0=ALU.add), [Lb], [smb])
        V(tt(den, lr, lr, ALU.mult), [natb], [smb])
        V(tt(t1, li, li, ALU.mult), [natb], [smb])
        V(tt(den, den, t1, ALU.add), [smb], [smb])
        V(lambda e: e.reciprocal(den, den), [smb], [smb])
        V(tt(t1, nr, lr, ALU.mult), [smb, natb], [smb])
        V(tt(t2, Li[:, 1, :], li, ALU.mult), [Lb, natb], [smb])
        V(tt(fr, t1, t2, ALU.add), [smb], [smb])
        V(tt(fr, fr, den, ALU.mult), [smb], [smb])
        V(tt(t1, Li[:, 1, :], lr, ALU.mult), [Lb, natb], [smb])
        V(tt(t2, nr, li, ALU.mult), [smb, natb], [smb])
        V(tt(fi, t1, t2, ALU.subtract), [smb], [smb])
        V(tt(fi, fi, den, ALU.mult), [smb], [smb])
        for (src, dst0, dst1, neg) in ((Lr[:, 8, :], cA1[:, 0, :], cA1[:, 1, :], False), (Li[:, 8, :], cA2[:, 1, :], cA2[:, 0, :], True)):
            ps, pb = next_ps()
            P.mm([lambda e, ps=ps, src=src: e.transpose(ps[0:64, 0:128], src, ident[:])], reads=[Lb, identb], writes=[pb])
            V(lambda e, ps=ps, dst0=dst0: e.tensor_copy(dst0, ps[0:64, 0:128]), [pb], [cAb])
            if neg:
                V(lambda e, ps=ps, dst1=dst1: e.tensor_scalar(out=dst1, in0=ps[0:64, 0:128], scalar1=-1.0, scalar2=None, op0=ALU.mult), [pb], [cAb])
            else:
                V(lambda e, ps=ps, dst1=dst1: e.tensor_copy(dst1, ps[0:64, 0:128]), [pb], [cAb])
        s5_prep.sm = sm; s5_prep.smb = smb

        w0, w0b = wring.t[0], wring.b[0]
        w1, w1b = wring.t[1], wring.b[1]
        f32v = w1[:].rearrange("p k n -> p (k n)").bitcast(F32)
        bre, bim, bbre, bbim, tA = [f32v[:, i * 1024:(i + 1) * 1024] for i in range(5)]
        g0, g0b = big.t[0], big.b[0]
        g1_, g1b = big.t[1], big.b[1]
        cre, cim = g0[:, 0:1024], g0[:, 1024:2048]
        tB, tC = g1_[:, 0:1024], g1_[:, 1024:2048]
        P.dma("sync", bre, bre_d, vdb, w1b)
        P.dma("sync", bim, bim_d, vdb, w1b)
        P.dma("sync", cre, cre_d, vdb, g0b)
        P.dma("sync", cim, cim_d, vdb, g0b)
        actf = act[:].rearrange("p k t -> p (k t)")
        stage = actf[:, 0:16384].rearrange("p (g j) -> p g j", j=128)
        Mre = actf[:, 16384:24576].rearrange("p (q s c) -> p q s c", q=64, s=8)
        Mim = actf[:, 24576:32768].rearrange("p (q s c) -> p q s c", q=64, s=8)
        Nre = actf[:, 32768:40960].rearrange("p (t o q) -> p t o q", t=8, o=16)
        Nim = w0[:].rearrange("p k n -> p (k n)")[:, 0:8192].rearrange("p (t o q) -> p t o q", t=8, o=16)
        Krev = xto.t[0][:].rearrange("p k t -> p (k t)").bitcast(BF16)[:, 0:4096].rearrange("p (s o c) -> p s o c", s=16, o=16)
        KT = xto.t[1][:].rearrange("p k t -> p (k t)").bitcast(BF16)[:, 0:2048].rearrange("p (o s c) -> p o s c", o=16, s=8)
        krb, ktb = xto.b[0], xto.b[1]
        b3 = lambda a: a.rearrange("p (q c) -> p q c", c=16)
        bc3 = lambda a: a.unsqueeze(2).broadcast_to([128, 64, 16])
        V(tt(b3(tA), b3(bre), bc3(fr), ALU.mult), [w1b, smb], [w1b])
        V(tt(b3(tB), b3(bim), bc3(fi), ALU.mult), [w1b, smb], [g1b])
        V(tt(bbre, tA, tB, ALU.subtract), [w1b, g1b], [w1b])
        V(tt(b3(tA), b3(bim), bc3(fr), ALU.mult), [w1b, smb], [w1b])
        V(tt(b3(tB), b3(bre), bc3(fi), ALU.mult), [w1b, smb], [g1b])
        V(tt(bbim, tA, tB, ALU.add), [w1b, g1b], [w1b])
        for s_ in range(8):
            k = 7 - s_
            V(tt(b3(tA), b3(bbre), bc3(Lr[:, k, :]), ALU.mult), [w1b, Lb], [w1b])
            V(tt(b3(tB), b3(bbim), bc3(Li[:, k, :]), ALU.mult), [w1b, Lb], [g1b])
            V(tt(Mre[:, :, s_, :], b3(tA), b3(tB), ALU.subtract), [w1b, g1b], [actb])
            V(tt(b3(tA), b3(bbim), bc3(Lr[:, k, :]), ALU.mult), [w1b, Lb], [w1b])
            V(tt(b3(tB), b3(bbre), bc3(Li[:, k, :]), ALU.mult), [w1b, Lb], [g1b])
            V(tt(Mim[:, :, s_, :], b3(tA), b3(tB), ALU.add), [w1b, g1b], [actb])
        V(lambda e: e.memset(Krev, 0.0), [], [krb])
        c3 = lambda a: a.rearrange("p (o q) -> p o q", q=64)
        bo3 = lambda a: a.unsqueeze(1).broadcast_to([128, 16, 64])
        nre_f = tC.rearrange("p (o q) -> p o q", q=64)
        nim_f = f32v[:, 5120:5632]
        nimb = hbo.b[1]
        nim_f = hbo.t[1][:].rearrange("p k t -> p (k t)").bitcast(F32).rearrange("p (o q) -> p o q", q=64)
        kred = sb("kred", [128, 16], F32); kredb = Buf("kred")
        bbreT = bbre.rearrange("p (q c) -> p c q", c=16)
        bbimT = bbim.rearrange("p (q c) -> p c q", c=16)
        for k in range(9):
            V(tt(c3(tA), c3(cre), bo3(Lr[:, k, :]), ALU.mult), [g0b, Lb], [w1b])
            V(tt(c3(tB), c3(cim), bo3(Li[:, k, :]), ALU.mult), [g0b, Lb], [g1b])
            V(tt(nre_f, c3(tA), c3(tB), ALU.subtract), [w1b, g1b], [g1b])
            V(tt(c3(tA), c3(cre), bo3(Li[:, k, :]), ALU.mult), [g0b, Lb], [w1b])
            V(tt(c3(tB), c3(cim), bo3(Lr[:, k, :]), ALU.mult), [g0b, Lb], [g1b])
            V(tt(nim_f, c3(tA), c3(tB), ALU.add), [w1b, g1b], [nimb])
            if k >= 1:
                V(lambda e, k=k: e.tensor_copy(Nre[:, k - 1, :, :], nre_f), [g1b], [actb])
                V(lambda e, k=k: e.tensor_scalar(out=Nim[:, k - 1, :, :], in0=nim_f, scalar1=-1.0, scalar2=None, op0=ALU.mult), [nimb], [w0b])
            if k <= 7:
                for o in range(16):
                    V(tt(tA.rearrange("p (c q) -> p c q", q=64), bbreT, nre_f[:, o, :].unsqueeze(1).broadcast_to([128, 16, 64]), ALU.mult), [w1b, g1b], [w1b])
                    V(tt(tB.rearrange("p (c q) -> p c q", q=64), bbimT, nim_f[:, o, :].unsqueeze(1).broadcast_to([128, 16, 64]), ALU.mult), [w1b, nimb, g1b], [g1b])
                    V(tt(tA, tA, tB, ALU.subtract), [w1b, g1b], [w1b])
                    V(lambda e, k=k, o=o: e.tensor_reduce(out=kred[:], in_=tA.rearrange("p (c q) -> p c q", q=64),
                                                            axis=AX.X, op=ALU.add), [w1b], [kredb])
                    V(lambda e, k=k, o=o: e.tensor_copy(Krev[:, 7 - k, o, :], kred[:]), [kredb], [krb])

        def family(n_in, src_fn, src_bufs, dram, dramb, rows):
            for j4 in range(32):
                ps, pb = next_ps()
                psh = ps[:].bitcast(BF16)
                for q in range(4):
                    j = j4 * 4 + q
                    P.mm([lambda e, psh=psh, q=q, j=j: e.transpose(psh[0:n_in, q * 128:(q + 1) * 128], src_fn(j), identh[:])],
                         reads=src_bufs + [identhb], writes=[pb])
                cp = (lambda e, psh=psh, j4=j4: e.tensor_copy(stage[0:n_in, :, j4 * 4:(j4 + 1) * 4],
                                                              psh[0:n_in, 0:512].rearrange("p (j g) -> p g j", j=4)))
                if j4 % 2 == 0:
                    V(cp, [pb], [actb])
                else:
                    A(lambda e, psh=psh, j4=j4: e.copy(stage[0:n_in, :, j4 * 4:(j4 + 1) * 4],
                                                       psh[0:n_in, 0:512].rearrange("p (j g) -> p g j", j=4)), [pb], [actb])
            P.dma("sync", dram.rearrange("g r j -> r g j"), stage[0:rows, :, :], actb, dramb, sembuf=actb)

        family(128, lambda j: (Mre if j < 64 else Mim)[:, j % 64, :, :].rearrange("p s c -> p (s c)"), [actb], WST, WSTb, 128)
        family(64, lambda j: Nre[:, j // 16, j % 16, :], [actb], WOR, WORb, 64)
        family(64, lambda j: Nim[:, j // 16, j % 16, :], [w0b], WOI, WOIb, 64)
        for t_ in range(8):
            V(lambda e, t_=t_: e.tensor_copy(KT, Krev[:, 7 - t_: 15 - t_, :, :].rearrange("p s o c -> p o s c")), [krb], [ktb])
            for j4 in range(4):
                ps, pb = next_ps()
                psh = ps[:].bitcast(BF16)
                for q in range(4):
                    o = j4 * 4 + q
                    P.mm([lambda e, psh=psh, q=q, o=o: e.transpose(psh[:, q * 128:(q + 1) * 128],
                                                                   KT[:, o, :, :].rearrange("p s c -> p (s c)"), identh[:])],
                         reads=[ktb, identhb], writes=[pb])
                jb = t_ * 16 + j4 * 4
                V(lambda e, psh=psh, jb=jb: e.tensor_copy(stage[:, :, jb:jb + 4],
                                                          psh[:, 0:512].rearrange("p (j g) -> p g j", j=4)), [pb], [actb])
        P.dma("sync", WINT.rearrange("g r j -> r g j"), stage[:, :, :], actb, WINTb, sembuf=actb)
        P.fence("sync", [actb])

    def s5_main():
        norm_phase(0, 0)
        P.fence("sync", [HTb])
        actf = act[:].rearrange("p k t -> p (k t)")
        hsb = actf[:, 0:16384].rearrange("p (k t) -> p k t", k=KD)
        wsec = actf[:, 16384:32768].rearrange("p (w g j) -> p w g j", w=4, g=32)
        wsecb = Buf("wsec")
        sp_ = actf[0:64, 32768:40960].rearrange("p (r g c) -> p r g c", r=2, g=32)
        spb_ = Buf("sprev")
        xb_ = actf[:, 40960:45056].rearrange("p (g c) -> p g c", c=128)
        xbb = Buf("xblk")
        ub = xto.t[0][:].rearrange("p k t -> p (k t)").bitcast(BF16).rearrange("p (g s c) -> p g s c", g=32, s=8)
        ubb = xto.b[0]
        ym = xto.t[1][:].rearrange("p k t -> p (k t)").bitcast(BF16).rearrange("p (t c) -> p t c", c=512)
        ymb = xto.b[1]
        yos = [hbo.t[i][:].rearrange("p k t -> p (k t)").rearrange("p (a t) -> p a t", a=2) for i in range(2)]
        yobs = [hbo.b[0], hbo.b[1]]
        slr = [big.t[ri][:].bitcast(BF16)[0:64, :].rearrange("p (g c) -> p g c", c=128) for ri in range(2)]
        slbs = [big.b[0], big.b[1]]
        tsc = sb("tsc", [64, 2, 2, 32], F32)
        usc = sb("usc", [64, 2, 32], F32)
        tscb = Buf("tsc")
        uscb = Buf("usc")
        vst = s5_prep.sm[0:64, :, :].rearrange("p a b -> p (a b)").rearrange("p (j q s g) -> p j q s g", j=4, q=2, s=3)
        V(lambda e: e.memset(vst, 0.0), [], [s5_prep.smb, vpb[0], vpb[1]])
        for SBi in range(2):
            for k in range(KD):
                P.dma("sync", hsb[:, k, :], HTv[:, k, SBi * 1024:(SBi + 1) * 1024], HTb, actb)
            for j in range(4):
                gb = j * 32
                wt, wb = wring.next()
                P.dma("gpsimd", wt[:, :KD, :], s5_in_d[:, j * 512:(j + 1) * 512].rearrange("(k p) n -> p k n", p=128), wbuf_d, wb)
                P.dma("sync", wsec[:, 0, :, :], WST[gb:gb + 32].rearrange("g r j -> r g j"), WSTb, wsecb)
                P.dma("sync", wsec[:, 1, :, :], WINT[gb:gb + 32].rearrange("g r j -> r g j"), WINTb, wsecb)
                P.dma("sync", wsec[0:64, 2, :, :], WOR[gb:gb + 32].rearrange("g r j -> r g j"), WORb, wsecb)
                P.dma("sync", wsec[0:64, 3, :, :], WOI[gb:gb + 32].rearrange("g r j -> r g j"), WOIb, wsecb)
                for s_ in range(8):
                    ps, pb = next_ps()
                    fns = []
                    for k in range(KD):
                        fns.append(lambda e, ps=ps, k=k, s_=s_, wt=wt: e.matmul(
                            ps[:], lhsT=hsb[:, k, s_:1024:8], rhs=wt[:, k, :],
                            start=(k == 0), stop=(k == KD - 1)))
                    P.mm(fns, reads=[actb, wb], writes=[pb])
                    src = ps[:].rearrange("p (g c) -> p g c", c=16)
                    A(lambda e, s_=s_, src=src: e.copy(ub[:, :, s_, :], src), [pb], [ubb])
                for g4 in range(8):
                    ps, pb = next_ps()
                    psh = ps[:].bitcast(BF16)
                    for q in range(4):
                        gl = g4 * 4 + q
                        P.mm([lambda e, psh=psh, q=q, gl=gl: e.transpose(
                            psh[:, q * 128:(q + 1) * 128],
                            ub[:, gl, :, :].rearrange("p s c -> p (s c)"), identh[:])],
                            reads=[ubb, identhb], writes=[pb])
                    A(lambda e, psh=psh, g4=g4: e.copy(
                        xb_[:, g4 * 4:(g4 + 1) * 4, :], psh[:, 0:512].rearrange("p (g c) -> p g c", g=4)), [pb], [xbb])
                for g2 in range(16):
                    ps, pb = next_ps()
                    for q in range(2):
                        gl = g2 * 2 + q
                        for ri in range(2):
                            P.mm([lambda e, ps=ps, q=q, ri=ri, gl=gl: e.matmul(
                                ps[0:64, (q * 2 + ri) * 128:(q * 2 + ri + 1) * 128],
                                lhsT=wsec[:, 0, gl, ri * 64:(ri + 1) * 64], rhs=xb_[:, gl, :], start=True, stop=True)],
                                reads=[wsecb, xbb], writes=[pb])
                    for ri in range(2):
                        A(lambda e, ps=ps, g2=g2, ri=ri: e.copy(
                            slr[ri][:, g2 * 2:(g2 + 1) * 2, :],
                            ps[0:64, :].rearrange("p (q r c) -> p r q c", q=2, r=2)[:, ri, :, :]), [pb], [slbs[ri]])
                c4 = cA4[:, :, :, gb:gb + 32]
                for c_ in range(128):
                    rd, wr = c_ % 2, 1 - (c_ % 2)
                    v3 = vst[:, j, rd, :, :]
                    w3 = vst[:, j, wr, :, :]
                    win = bass.AP(v3.tensor, v3.offset, [list(v3.ap[0]), [32, 2], [32, 2], [1, 32]])
                    w02 = bass.AP(w3.tensor, w3.offset, [list(w3.ap[0]), [64, 2], [1, 32]])
                    V(tt(tsc[:], c4, win, ALU.mult), [cAb, vpb[rd]], [tscb])
                    V(tt(usc[:], tsc[:, 0, :, :], tsc[:, 1, :, :], ALU.add), [tscb], [uscb])
                    A(lambda e, c_=c_, v3=v3: e.copy(sp_[:, :, :, c_], v3[:, 0:2, :]), [vpb[rd]], [spb_])
                    V(tt(w02, usc[:, 0, :].unsqueeze(1).broadcast_to([64, 2, 32]),
                         slr[0][:, :, c_].unsqueeze(1).broadcast_to([64, 2, 32]), ALU.add), [uscb, slbs[0]], [vpb[wr]])
                    V(tt(w3[:, 1, :], usc[:, 1, :], slr[1][:, :, c_], ALU.add), [uscb, slbs[1]], [vpb[wr]])
                for g4 in range(8):
                    ps, pb = next_ps()
                    for q in range(4):
                        gl = g4 * 4 + q
                        P.mm([lambda e, ps=ps, q=q, gl=gl: e.matmul(
                                ps[:, q * 128:(q + 1) * 128], lhsT=xb_[:, gl, :], rhs=wsec[:, 1, gl, :], start=True, stop=False),
                              lambda e, ps=ps, q=q, gl=gl: e.matmul(
                                ps[:, q * 128:(q + 1) * 128], lhsT=sp_[:, 0, gl, :], rhs=wsec[0:64, 2, gl, :], start=False, stop=False),
                              lambda e, ps=ps, q=q, gl=gl: e.matmul(
                                ps[:, q * 128:(q + 1) * 128], lhsT=sp_[:, 1, gl, :], rhs=wsec[0:64, 3, gl, :], start=False, stop=True)],
                             reads=[xbb, wsecb, spb_], writes=[pb])
                    yf_, yfb = tmpf.next()
                    yf = yf_[:].rearrange("p (g t c) -> p g t c", g=4, t=8)
                    ch0 = (gb + g4 * 4) * 16
                    P.op("gpsimd", tt(yf, ub[:, g4 * 4: g4 * 4 + 4, :, :],
                         dtab[:, ch0:ch0 + 64].rearrange("p (g c) -> p g c", c=16).unsqueeze(2).broadcast_to([128, 4, 8, 16]),
                         ALU.mult), reads=[ubb, dtabb], writes=[yfb])
                    V(tt(yf, yf, ps[:].rearrange("p (g t c) -> p g t c", g=4, t=8), ALU.add), [yfb, pb], [yfb])
                    ho, hob = gelu_tile(yf_[:], yfb, 512, eng="gpsimd")
                    P.op("gpsimd", lambda e, ho=ho, g4=g4: e.tensor_copy(
                        ym[:, :, g4 * 64:(g4 + 1) * 64].rearrange("p t (g c) -> p g t c", c=16),
                        ho[:, :512].rearrange("p (g t c) -> p g t c", g=4, t=8)), reads=[hob], writes=[ymb])
                for t4 in range(8):
                    ps, pb = next_ps()
                    psh = ps[:].bitcast(BF16)
                    for q in range(4):
                        idx = t4 * 4 + q
                        t_, ct = idx // 4, idx % 4
                        P.mm([lambda e, psh=psh, q=q, t_=t_, ct=ct: e.transpose(
                            psh[:, q * 128:(q + 1) * 128], ym[:, t_, ct * 128:(ct + 1) * 128], identh[:])],
                            reads=[ymb, identhb], writes=[pb])
                    for q in range(4):
                        idx = t4 * 4 + q
                        t_, ct = idx // 4, idx % 4
                        A(lambda e, psh=psh, q=q, t_=t_, ct=ct: e.copy(
                            yos[ct // 2][:, ct % 2, t_:1024:8], psh[:, q * 128:(q + 1) * 128]), [pb], [yobs[ct // 2]])
                kt0 = gb // 8
                for hh in range(2):
                    P.dma("sync", YTv[:, kt0 + 2 * hh: kt0 + 2 * hh + 2, SBi * 1024:(SBi + 1) * 1024], yos[hh], yobs[hh], YTb, sembuf=yobs[hh])
        P.fence("sync", hbo.b)
        load_act(YTv, KD, YTb)

        def ep_glu(si, m, tb, pss):
            (pv, pvb), (pg, pgb) = pss
            kk = si * 4 + m
            sg, sgb = tmpf.next()
            A(lambda e: e.activation(sg[:], pg[:], AF.Sigmoid), [pgb], [sgb])
            V(tt(sg[:], sg[:], pv[:], ALU.mult), [sgb, pvb], [sgb])
            xt_, xb2 = tmpf.next()
            P.dma("sync", xt_[:], XTv[:, kk, tb * TB:(tb + 1) * TB], XTb, xb2)
            V(lambda e: e.scalar_tensor_tensor(out=xt_[:], in0=sg[:], scalar=mcol(0, 2, kk), in1=xt_[:],
                                               op0=ALU.mult, op1=ALU.add), [sgb, xb2, modb], [xb2])
            P.dma("sync", XTv[:, kk, tb * TB:(tb + 1) * TB], xt_[:], xb2, XTb, sembuf=xb2)

        gemm([[s5_glu_d[:, jj * 512:(jj + 1) * 512], s5_glu_d[:, D + jj * 512: D + (jj + 1) * 512]] for jj in range(4)], KD, ep_glu)
        P.fence("sync", tmpf.b)

    def final_phase():
        P.fence("sync", [XTb])
        for q in range(T // 128):
            xt_, xb_ = big.next()
            xv = xt_[:].rearrange("p (k t) -> p k t", k=KD)
            P.dma("sync", xv, XTv[:, :, q * 128:(q + 1) * 128], XTb, xb_)
            ht, hb = hbo.next()
            A(lambda e, ht=ht, xv=xv: e.activation(ht[:], xv, AF.Square), [xb_], [hb])
            ps, pb = next_ps()
            fns = []
            for k in range(KD):
                fns.append(lambda e, ps=ps, ht=ht, k=k: e.matmul(
                    ps[:, :128], lhsT=ones_bf[:], rhs=ht[:, k, :], start=(k == 0), stop=(k == KD - 1)))
            P.mm(fns, reads=[hb, onesb], writes=[pb])
            rs, rb = tmpf.next()
            V(lambda e, rs=rs, ps=ps: e.tensor_scalar(out=rs[:, :128], in0=ps[:, :128], scalar1=1.0 / D, scalar2=EPS,
                                                      op0=ALU.mult, op1=ALU.add), [pb], [rb])
            A(lambda e, rs=rs: e.activation(rs[:, :128], rs[:, :128], AF.Sqrt), [rb], [rb])
            V(lambda e, rs=rs: e.reciprocal(rs[:, :128], rs[:, :128]), [rb], [rb])
            for k in range(KD):
                V(lambda e, xv=xv, k=k, rs=rs: e.scalar_tensor_tensor(
                    out=xv[:, k, :], in0=xv[:, k, :], scalar=colsA[:, 80 + k:81 + k], in1=rs[:, :128],
                    op0=ALU.mult, op1=ALU.mult), [xb_, rb, colsAb], [xb_])
            ot, ob = xto.next()
            otv = ot[:].rearrange("p k t -> p (k t)")
            for k4 in range(4):
                ps2, pb2 = next_ps()
                for jq in range(4):
                    k = k4 * 4 + jq
                    P.mm([lambda e, ps2=ps2, jq=jq, k=k, xv=xv: e.transpose(
                        ps2[:, jq * 128:(jq + 1) * 128], xv[:, k, :], ident[:])], reads=[xb_, identb], writes=[pb2])
                if k4 % 2 == 0:
                    V(lambda e, ps2=ps2, k4=k4, otv=otv: e.tensor_copy(otv[:, k4 * 512:(k4 + 1) * 512], ps2[:]), [pb2], [ob])
                else:
                    A(lambda e, ps2=ps2, k4=k4, otv=otv: e.copy(otv[:, k4 * 512:(k4 + 1) * 512], ps2[:]), [pb2], [ob])
            P.dma("sync", out_d[q * 128:(q + 1) * 128, :], otv, ob, Buf("outd"), sembuf=ob)
        P.fence("sync", xto.b)

    if dbg == "S5":
        s5_prep()
        s5_main()
    snaps = []

    def snap(name):
        d_ = nc.dram_tensor(name, [D, T], F32, kind="ExternalOutput").ap()
        P.fence("sync", [XTb] + tmpf.b)
        b_ = Buf(name)
        P.dma("sync", d_, XT, XTb, b_)
        snaps.append(b_)

    if dbg is None or dbg == "ALL":
        s5_prep()
        s5_main()
        if dbg: snap("dbg0")
        ffn(0)
        if dbg: snap("dbg1")
        lru_mixer(1)
        if dbg: snap("dbg2")
        ffn(1)
        if dbg: snap("dbg3")
        final_phase()
        P.fence("sync", snaps)

    if dbg == "FFN0":
        ffn(0)

    if dbg in ("XT", "FFN0", "LRU", "S5"):
        dbg_d = nc.dram_tensor("dbg", [D, T], F32, kind="ExternalOutput").ap()
        P.fence("sync", [XTb] + tmpf.b)
        dbgb = Buf("dbgb")
        P.dma("sync", dbg_d, XT, XTb, dbgb)
        P.fence("sync", [dbgb])
    P.emit()
    return nc, es


def kernel(**inputs):
    dbg = os.environ.get("KDBG")
    nc, es = build(dbg=dbg)
    f = lambda a: np.ascontiguousarray(a, dtype=np.float32)
    x = f(inputs["x"]); c = f(inputs["c"])
    ng = f(inputs["norm_g"]).reshape(64, 128)
    fg = f(inputs["final_g"]).reshape(16, 128)
    sd = f(inputs["s5_d"]).reshape(16, 128)
    vecB = f(inputs["b_ada"]).reshape(2, 96, 128)
    vecC = np.zeros((128, 128), np.float32)
    vecC[0:88] = f(inputs["lru_conv_w"]).reshape(88, 128)
    vecC[88:110] = f(inputs["lru_conv_b"]).reshape(22, 128)
    vecD = np.zeros((128, 128), np.float32)
    vecD[0:22] = f(inputs["lru_b_rg"]).reshape(22, 128)
    vecD[22:44] = f(inputs["lru_b_ig"]).reshape(22, 128)
    vecD[44:66] = f(inputs["lru_lam"]).reshape(22, 128)
    def dense_bd(w):
        w = f(w)[0]
        o = np.zeros((LW, LW), np.float32)
        for h in range(16):
            o[h * 176:(h + 1) * 176, h * 176:(h + 1) * 176] = w[h]
        return o
    rgd = dense_bd(inputs["lru_w_rg"]); igd = dense_bd(inputs["lru_w_ig"])
    in_maps = []
    for b in range(NCORES):
        vecA = np.zeros((128, 128), np.float32)
        vecA[0:16] = c[b].reshape(16, 128)
        vecA[16:80] = ng
        vecA[80:96] = fg
        vecA[96:112] = sd
        in_maps.append({"x": x[b], "c": c[b].reshape(KD, 128), "ident": np.eye(128, dtype=np.float32),
                        "w_ada": f(inputs["w_ada"]), "vecA": vecA, "vecB": vecB, "vecC": vecC, "vecD": vecD,
                        "ffn_w_gu": f(inputs["ffn_w_gu"]), "ffn_w_down": f(inputs["ffn_w_down"]),
                        "lru_w_in": f(inputs["lru_w_in"])[0], "lru_rg_dense": rgd, "lru_ig_dense": igd,
                        "lru_w_out": f(inputs["lru_w_out"])[0],
                        "s5_w_in": f(inputs["s5_w_in"])[0], "s5_w_glu": f(inputs["s5_w_glu"])[0],
                        "s5_lam_re": f(inputs["s5_lam_re"])[0], "s5_lam_im": f(inputs["s5_lam_im"])[0],
                        "s5_log_dt": f(inputs["s5_log_dt"]).reshape(128, 1),
                        "s5_b_re": f(inputs["s5_b_re"]).reshape(128, 1024), "s5_b_im": f(inputs["s5_b_im"]).reshape(128, 1024),
                        "s5_c_re": f(inputs["s5_c_re"]).reshape(128, 1024), "s5_c_im": f(inputs["s5_c_im"]).reshape(128, 1024),
                        "s5_d_row": f(inputs["s5_d"]).reshape(1, D)})
    res = run_bass_kernel_spmd(nc, in_maps, core_ids=list(range(NCORES)))
    es.close()
    if dbg == "ALL":
        return [{k: r[k] for k in ("dbg0", "dbg1", "dbg2", "dbg3", "out")} for r in res.results]
    if dbg:
        return [r["dbg"] for r in res.results]
    return np.stack([r["out"] for r in res.results], axis=0)
```

```python
import os
from contextlib import ExitStack
import numpy as np
import concourse.bass as bass
import concourse.mybir as mybir
from concourse.bass_utils import run_bass_kernel_spmd

F32 = mybir.dt.float32
BF16 = mybir.dt.bfloat16
AF = mybir.ActivationFunctionType
ALU = mybir.AluOpType
AX = mybir.AxisListType

D = 2048
T = 2048
TB = 512
NTB = T // TB
KD = D // 128
FH = 5632
KF = FH // 128
LW = 2816
KL = LW // 128
EPS = 1e-6
NCORES = 4


class Buf:
    def __init__(self, name):
        self.name = name
        self.w = None
        self.r = []
        self.sem = None
        self.cnt = 0


class Prog:
    ENGS = ["sync", "scalar", "vector", "gpsimd", "tensor"]

    def __init__(self, nc, es):
        self.nc = nc
        self.es = es
        self.q = {e: [] for e in self.ENGS}
        self.esem = {}
        self.ecnt = {}
        for e in ["scalar", "vector", "gpsimd", "tensor"]:
            self.esem[e] = es.enter_context(nc.semaphore("es_" + e))
            self.ecnt[e] = 0
        self.known = {e: {} for e in self.ENGS}
        self.nsem = 4
        self.semobj = {}

    def _waits(self, eng, toks):
        need = {}
        for t in toks:
            if t is None:
                continue
            s, v = t
            if self.known[eng].get(id(s), 0) >= v:
                continue
            if need.get(id(s), (s, 0))[1] < v:
                need[id(s)] = (s, v)
        out = []
        for k, (s, v) in need.items():
            self.known[eng][k] = v
            out.append((s, v))
        return out

    def _deps(self, reads, writes):
        toks = []
        for b in reads:
            toks.append(b.w)
        for b in writes:
            toks.append(b.w)
            toks.extend(b.r)
        return toks

    def op(self, eng, fn, reads=(), writes=(), pe_same_ok=False):
        toks = self._deps(reads, writes)
        if pe_same_ok:
            toks = [t for t in toks if t is None or t[0] is not self.esem["tensor"]]
        w = self._waits(eng, toks)
        self.ecnt[eng] += 1
        tok = (self.esem[eng], self.ecnt[eng])
        self.q[eng].append((w, fn, (self.esem[eng], 1)))
        for b in writes:
            b.w = tok
            b.r = []
        for b in reads:
            if b not in writes:
                b.r.append(tok)
        return tok

    def mm(self, fns, reads=(), writes=()):
        toks = self._deps(reads, writes)
        toks = [t for t in toks if t is None or t[0] is not self.esem["tensor"]]
        w = self._waits("tensor", toks)
        n = len(fns)
        for i, fn in enumerate(fns):
            if i == n - 1:
                self.ecnt["tensor"] += 1
                tok = (self.esem["tensor"], self.ecnt["tensor"])
                self.q["tensor"].append((w if i == 0 else [], fn, (self.esem["tensor"], 1)))
            else:
                self.q["tensor"].append((w if i == 0 else [], fn, None))
        for b in writes:
            b.w = tok
            b.r = []
        for b in reads:
            if b not in writes:
                b.r.append(tok)
        return tok

    def dma(self, eng, out_ap, in_ap, src, dst, sembuf=None):
        sb = sembuf if sembuf is not None else dst
        if sb.sem is None:
            sb.sem = self.es.enter_context(self.nc.semaphore("d_" + sb.name))
            self.nsem += 1
        toks = self._deps([src], [dst])
        w = self._waits(eng, toks)
        sb.cnt += 16
        tok = (sb.sem, sb.cnt)
        self.q[eng].append((w, lambda e, o=out_ap, i=in_ap: e.dma_start(out=o, in_=i), (sb.sem, 16)))
        dst.w = tok
        dst.r = []
        src.r.append(tok)
        return tok

    def fence(self, eng, bufs):
        toks = []
        for b in bufs:
            toks.append(b.w)
            toks.extend(b.r)
        w = self._waits(eng, toks)
        if w:
            self.q[eng].append((w, None, None))

    def emit(self):
        nc = self.nc
        with nc.Block() as block:
            def run(name):
                def body(eng):
                    for (w, fn, inc) in self.q[name]:
                        for (s, v) in w:
                            eng.wait_ge(s, v)
                        if fn is None:
                            continue
                        ins = fn(eng)
                        if inc is not None:
                            ins.then_inc(inc[0], inc[1])
                return body
            block.sync(run("sync"))
            block.scalar(run("scalar"))
            block.vector(run("vector"))
            block.gpsimd(run("gpsimd"))
            block.tensor(run("tensor"))


class Ring:
    def __init__(self, nc, es, name, n, shape, dt):
        self.t = [es.enter_context(nc.sbuf_tensor(f"sb_{name}{i}", shape, dt)) for i in range(n)]
        self.b = [Buf(f"{name}{i}") for i in range(n)]
        self.i = 0
        self.n = n

    def next(self):
        k = self.i % self.n
        self.i += 1
        return self.t[k], self.b[k]


def build(stop_after=99, dbg=None):
    nc = bass.Bass("TRN2", target_bir_lowering=False)
    es = ExitStack()
    P = Prog(nc, es)

    def din(name, shape, dt=F32):
        return nc.dram_tensor(name, list(shape), dt, kind="ExternalInput").ap()

    def dscr(name, shape, dt):
        return nc.dram_tensor(name, list(shape), dt).ap()

    x_d = din("x", [T, D])
    c_d = din("c", [KD, 128])
    ident_d = din("ident", [128, 128])
    out_d = nc.dram_tensor("out", [T, D], F32, kind="ExternalOutput").ap()

    XT = dscr("XT", [D, T], F32)
    XTb = Buf("XT")
    HT = dscr("HT", [D, T], BF16)
    HTb = Buf("HT")

    def sb(name, shape, dt):
        return es.enter_context(nc.sbuf_tensor("sb_" + name, list(shape), dt))

    w_ada_d = din("w_ada", [2, D, 6 * D])
    vecA_d = din("vecA", [128, 128])
    vecB_d = din("vecB", [2, 96, 128])
    vecC_d = din("vecC", [128, 128])
    vecD_d = din("vecD", [128, 128])
    ffn_gu_d = din("ffn_w_gu", [2, D, 2 * FH])
    ffn_dn_d = din("ffn_w_down", [2, FH, D])
    wbuf_d = Buf("wdram")

    HID = dscr("HID", [FH, T], BF16)
    HIDb = Buf("HID")

    ident = sb("ident", [128, 128], F32)
    identb = Buf("ident")
    P.dma("sync", ident[:], ident_d, Buf("identd"), identb)
    ones_bf = sb("ones_bf", [128, 128], BF16)
    onesb = Buf("ones")
    P.op("vector", lambda e: e.memset(ones_bf[:], 1.0), writes=[onesb])

    psum = [es.enter_context(nc.psum_tensor(f"ps{i}", [128, 512], F32)) for i in range(8)]
    psb = [Buf(f"ps{i}") for i in range(8)]
    pctr = [0]

    def next_ps():
        k = pctr[0] % 7
        pctr[0] += 1
        return psum[k], psb[k]

    big = Ring(nc, es, "big", 2, [128, 2048], F32)
    xto = Ring(nc, es, "xto", 2, [128, KD, 128], F32)
    hbo = Ring(nc, es, "hbo", 2, [128, KD, 128], BF16)
    tmpf = Ring(nc, es, "tmpf", 4, [128, 512], F32)
    tmpb = Ring(nc, es, "tmpb", 3, [128, 512], BF16)
    wring = Ring(nc, es, "wr", 2, [128, KL, 512], BF16)
    act = sb("act", [128, KL, T], BF16)
    actb = Buf("act")

    colsA = sb("colsA", [128, 128], F32)
    colsB = sb("colsB", [128, 2, 96], F32)
    colsC = sb("colsC", [128, 128], F32)
    colsD = sb("colsD", [128, 128], F32)
    colsAb, colsBb, colsCb, colsDb = Buf("cA"), Buf("cB"), Buf("cC"), Buf("cD")
    vdb = Buf("vecd")

    def load_cols(src_ap, nrows, dst_ap, dstb):
        st, stb = big.next()
        P.dma("sync", st[:nrows, :128], src_ap, vdb, stb)
        ps, pb = next_ps()
        P.mm([lambda e: e.transpose(ps[:, :nrows], st[:nrows, :128], ident[:nrows, :nrows])],
             reads=[stb, identb], writes=[pb])
        P.op("vector", lambda e: e.tensor_copy(dst_ap, ps[:, :nrows]), reads=[pb], writes=[dstb])

    load_cols(vecA_d, 128, colsA[:, :], colsAb)
    load_cols(vecB_d[0], 96, colsB[:, 0, :], colsBb)
    load_cols(vecB_d[1], 96, colsB[:, 1, :], colsBb)
    load_cols(vecC_d, 128, colsC[:, :], colsCb)
    load_cols(vecD_d, 128, colsD[:, :], colsDb)
    condT = sb("condT", [128, KD], BF16)
    condb = Buf("cond")
    P.op("scalar", lambda e: e.activation(condT[:], colsA[:, 0:16], AF.Silu), reads=[colsAb], writes=[condb])

    mod = sb("mod", [128, 2, 96], F32)
    modb = Buf("mod")
    gs = sb("gs", [128, 2, 2, KD], F32)
    gsb = Buf("gs")
    ada_steps = []

    def ada_step(L, j):
        psm, pbm = psum[7], psb[7]
        wt, wb = wring.next()
        P.dma("gpsimd", wt[:, :KD, :],
              w_ada_d[L, :, j * 512:(j + 1) * 512].rearrange("(k p) n -> p k n", p=128), wbuf_d, wb)
        for m in range(4):
            col = j * 4 + m
            fns = []
            for k in range(KD):
                fns.append(lambda e, wt=wt, m=m, k=k, col=col, psm=psm: e.matmul(
                    psm[:, col:col + 1], lhsT=wt[:, k, m * 128:(m + 1) * 128], rhs=condT[:, k:k + 1],
                    start=(k == 0), stop=(k == KD - 1)))
            P.mm(fns, reads=[wb, condb], writes=[pbm])
        if j == 23:
            P.op("vector", lambda e, L=L, psm=psm: e.tensor_tensor(
                out=mod[:, L, :], in0=psm[:, 0:96], in1=colsB[:, L, :], op=ALU.add),
                reads=[pbm, colsBb], writes=[modb])
            for sub in range(2):
                sc0 = (sub * 3 + 1) * 16
                P.op("vector", lambda e, L=L, sub=sub, sc0=sc0: e.scalar_tensor_tensor(
                    out=gs[:, L, sub, :], in0=mod[:, L, sc0:sc0 + 16], scalar=1.0,
                    in1=colsA[:, 16 + (L * 2 + sub) * 16: 16 + (L * 2 + sub) * 16 + 16],
                    op0=ALU.add, op1=ALU.mult), reads=[modb, colsAb], writes=[gsb])

    for L in range(2):
        for j in range(24):
            ada_steps.append((L, j))

    def mcol(L, idx, k):
        return mod[:, L, idx * 16 + k: idx * 16 + k + 1]

    xdb = Buf("xd")
    XTv = XT.rearrange("(k p) t -> p k t", p=128)
    HTv = HT.rearrange("(k p) t -> p k t", p=128)
    def phase0_tile(tt):
        xt_, xb_ = big.next()
        P.dma("sync", xt_[:, :D], x_d[tt * 128:(tt + 1) * 128, :], xdb, xb_)
        ot, ob = xto.next()
        for k4 in range(KD // 4):
            ps, pb = next_ps()
            for j in range(4):
                k = k4 * 4 + j
                P.mm([lambda e, ps=ps, j=j, k=k, xt_=xt_: e.transpose(
                    ps[:, j * 128:(j + 1) * 128], xt_[:, k * 128:(k + 1) * 128], ident[:])],
                    reads=[xb_, identb], writes=[pb])
            if k4 % 2 == 0:
                P.op("vector", lambda e, ps=ps, ot=ot, k4=k4: e.tensor_copy(
                    ot[:, k4 * 4:(k4 + 1) * 4, :], ps[:].rearrange("p (j t) -> p j t", j=4)),
                    reads=[pb], writes=[ob])
            else:
                P.op("scalar", lambda e, ps=ps, ot=ot, k4=k4: e.copy(
                    ot[:, k4 * 4:(k4 + 1) * 4, :], ps[:].rearrange("p (j t) -> p j t", j=4)),
                    reads=[pb], writes=[ob])
        P.dma("sync", XTv[:, :, tt * 128:(tt + 1) * 128], ot[:], ob, XTb, sembuf=ob)

    for i, (L_, j_) in enumerate(ada_steps):
        ada_step(L_, j_)
        if i % 3 == 2:
            phase0_tile(i // 3)
    P.fence("sync", xto.b)

    def norm_phase(L, sub):
        P.fence("sync", [XTb])
        for q in range(T // 128):
            xt_, xb_ = big.next()
            xv = xt_[:].rearrange("p (k t) -> p k t", k=KD)
            P.dma("sync", xv, XTv[:, :, q * 128:(q + 1) * 128], XTb, xb_)
            ht, hb = hbo.next()
            P.op("scalar", lambda e, ht=ht, xv=xv: e.activation(ht[:], xv, AF.Square), reads=[xb_], writes=[hb])
            ps, pb = next_ps()
            fns = []
            for k in range(KD):
                fns.append(lambda e, ps=ps, ht=ht, k=k: e.matmul(
                    ps[:, :128], lhsT=ones_bf[:], rhs=ht[:, k, :], start=(k == 0), stop=(k == KD - 1)))
            P.mm(fns, reads=[hb, onesb], writes=[pb])
            rs, rb = tmpf.next()
            P.op("vector", lambda e, rs=rs, ps=ps: e.tensor_scalar(
                out=rs[:, :128], in0=ps[:, :128], scalar1=1.0 / D, scalar2=EPS, op0=ALU.mult, op1=ALU.add),
                reads=[pb], writes=[rb])
            P.op("scalar", lambda e, rs=rs: e.activation(rs[:, :128], rs[:, :128], AF.Sqrt),
                 reads=[rb], writes=[rb])
            P.op("vector", lambda e, rs=rs: e.reciprocal(rs[:, :128], rs[:, :128]),
                 reads=[rb], writes=[rb])
            for k in range(KD):
                P.op("vector", lambda e, xv=xv, k=k, rs=rs: e.scalar_tensor_tensor(
                    out=xv[:, k, :], in0=xv[:, k, :], scalar=gs[:, L, sub, k:k + 1], in1=rs[:, :128],
                    op0=ALU.mult, op1=ALU.mult), reads=[xb_, rb, gsb], writes=[xb_])
                P.op("scalar", lambda e, xv=xv, k=k, ht=ht: e.activation(
                    ht[:, k, :], xv[:, k, :], AF.Identity, bias=mcol(L, sub * 3, k), scale=1.0),
                    reads=[xb_, modb], writes=[hb])
            P.dma("sync", HTv[:, :, q * 128:(q + 1) * 128], ht[:], hb, HTb, sembuf=hb)
        P.fence("sync", hbo.b)

    def load_act(src_v, KT, srcb):
        P.fence("sync", [srcb])
        for k in range(KT):
            P.dma("sync", act[:, k, :], src_v[:, k, :], srcb, actb)

    def gemm(steps, KT, epilogue, krange=None):
        for si, wlist in enumerate(steps):
            wts = []
            for wap in wlist:
                wt, wb = wring.next()
                P.dma("gpsimd", wt[:, :KT, :], wap.rearrange("(k p) n -> p k n", p=128), wbuf_d, wb)
                wts.append((wt, wb))
            for m in range(4):
                for tb in range(NTB):
                    pss = []
                    for (wt, wb) in wts:
                        ps, pb = next_ps()
                        fns = []
                        k0, k1 = (0, KT - 1) if krange is None else krange(si, m)
                        for k in range(k0, k1 + 1):
                            fns.append(lambda e, ps=ps, wt=wt, m=m, k=k, tb=tb, k0=k0, k1=k1: e.matmul(
                                ps[:], lhsT=wt[:, k, m * 128:(m + 1) * 128],
                                rhs=act[:, k, tb * TB:(tb + 1) * TB], start=(k == k0), stop=(k == k1)))
                        P.mm(fns, reads=[wb, actb], writes=[pb])
                        pss.append((ps, pb))
                    epilogue(si, m, tb, pss)

    def ffn(L):
        norm_phase(L, 1)
        load_act(HTv, KD, HTb)
        HIDv = HID.rearrange("(k p) t -> p k t", p=128)

        def ep_swiglu(si, m, tb, pss):
            (pg, pgb), (pu, pub) = pss
            sg, sgb = tmpf.next()
            P.op("scalar", lambda e: e.activation(sg[:], pg[:], AF.Silu), reads=[pgb], writes=[sgb])
            ho, hob = tmpb.next()
            P.op("vector", lambda e: e.tensor_tensor(out=ho[:], in0=pu[:], in1=sg[:], op=ALU.mult),
                 reads=[pub, sgb], writes=[hob])
            P.dma("sync", HIDv[:, si * 4 + m, tb * TB:(tb + 1) * TB], ho[:], hob, HIDb, sembuf=hob)

        steps = [[ffn_gu_d[L, :, j * 512:(j + 1) * 512], ffn_gu_d[L, :, FH + j * 512: FH + (j + 1) * 512]]
                 for j in range(FH // 512)]
        gemm(steps, KD, ep_swiglu)
        P.fence("sync", tmpb.b)

        def ep_res(si, m, tb, pss):
            (ps, pb), = pss
            kk = si * 4 + m
            xt_, xb_ = tmpf.next()
            P.dma("sync", xt_[:], XTv[:, kk, tb * TB:(tb + 1) * TB], XTb, xb_)
            P.op("vector", lambda e: e.scalar_tensor_tensor(
                out=xt_[:], in0=ps[:], scalar=mcol(L, 5, kk), in1=xt_[:], op0=ALU.mult, op1=ALU.add),
                reads=[pb, xb_, modb], writes=[xb_])
            P.dma("sync", XTv[:, kk, tb * TB:(tb + 1) * TB], xt_[:], xb_, XTb, sembuf=xb_)

        for half in range(2):
            load_act(HIDv[:, half * KL:(half + 1) * KL, :], KL, HIDb)
            steps = [[ffn_dn_d[L, half * LW:(half + 1) * LW, j * 512:(j + 1) * 512]] for j in range(D // 512)]
            gemm(steps, KL, ep_res)
            P.fence("sync", tmpf.b)


    lru_in_d = din("lru_w_in", [D, 2 * LW])
    lru_rg_d = din("lru_rg_dense", [LW, LW])
    lru_ig_d = din("lru_ig_dense", [LW, LW])
    lru_out_d = din("lru_w_out", [LW, D])
    GB = dscr("GB", [LW, T], BF16); GBb = Buf("GB")
    XB = dscr("XB", [LW, T], F32); XBb = Buf("XB")
    XC = dscr("XC", [LW, T], F32); XCb_ = Buf("XC")
    XCh = dscr("XCh", [LW, T], BF16); XChb = Buf("XCh")
    AA = dscr("AA", [LW, T], F32); AAb = Buf("AA")
    BT = dscr("BT", [LW, T], F32); BTb = Buf("BT")
    YL = dscr("YL", [LW, T], BF16); YLb = Buf("YL")
    GBv, XBv, XCv, XChv, AAv, BTv, YLv = [a.rearrange("(k p) t -> p k t", p=128) for a in (GB, XB, XC, XCh, AA, BT, YL)]

    def gelu_tile(src_ap, srcb, n, eng="vector"):
        t1, t1b = tmpf.next()
        P.op("scalar", lambda e: e.activation(t1[:, :n], src_ap, AF.Square), reads=[srcb], writes=[t1b])
        P.op(eng, lambda e: e.tensor_scalar(out=t1[:, :n], in0=t1[:, :n], scalar1=0.044715, scalar2=1.0,
                                            op0=ALU.mult, op1=ALU.add), reads=[t1b], writes=[t1b])
        P.op(eng, lambda e: e.tensor_tensor(out=t1[:, :n], in0=t1[:, :n], in1=src_ap, op=ALU.mult),
             reads=[t1b, srcb], writes=[t1b])
        P.op("scalar", lambda e: e.activation(t1[:, :n], t1[:, :n], AF.Sigmoid, scale=1.5957691216057308),
             reads=[t1b], writes=[t1b])
        ho, hob = tmpb.next()
        P.op(eng, lambda e: e.tensor_tensor(out=ho[:, :n], in0=t1[:, :n], in1=src_ap, op=ALU.mult),
             reads=[t1b, srcb], writes=[hob])
        return ho, hob

    def lru_mixer(L):
        norm_phase(L, 0)
        load_act(HTv, KD, HTb)

        def ep_in(si, m, tb, pss):
            (ps, pb), = pss
            kk = si * 4 + m
            if kk < KL:
                ho, hob = gelu_tile(ps[:], pb, TB)
                P.dma("sync", GBv[:, kk, tb * TB:(tb + 1) * TB], ho[:], hob, GBb, sembuf=hob)
            else:
                xo, xob = tmpf.next()
                P.op("scalar", lambda e: e.copy(xo[:], ps[:]), reads=[pb], writes=[xob])
                P.dma("sync", XBv[:, kk - KL, tb * TB:(tb + 1) * TB], xo[:], xob, XBb, sembuf=xob)

        gemm([[lru_in_d[:, j * 512:(j + 1) * 512]] for j in range(2 * LW // 512)], KD, ep_in)
        P.fence("sync", tmpf.b + tmpb.b)

        spc = sb("spc", [128, KL], F32); spb = Buf("spc")
        ex = sb("spx", [128, KL], F32); exb = Buf("spx")
        P.op("scalar", lambda e: e.activation(ex[:], colsD[:, 44:66], AF.Exp, scale=-1.0), reads=[colsDb], writes=[exb])
        P.op("vector", lambda e: e.tensor_scalar(out=spc[:], in0=ex[:], scalar1=-0.25, scalar2=1.0 / 3.0,
                                                 op0=ALU.mult, op1=ALU.add), reads=[exb], writes=[spb])
        P.op("vector", lambda e: e.tensor_tensor(out=spc[:], in0=spc[:], in1=ex[:], op=ALU.mult), reads=[exb, spb], writes=[spb])
        P.op("vector", lambda e: e.tensor_scalar(out=spc[:], in0=spc[:], scalar1=-0.5, scalar2=None, op0=ALU.add),
             reads=[spb], writes=[spb])
        P.op("vector", lambda e: e.tensor_tensor(out=spc[:], in0=spc[:], in1=ex[:], op=ALU.mult), reads=[exb, spb], writes=[spb])
        P.op("vector", lambda e: e.tensor_scalar(out=spc[:], in0=spc[:], scalar1=1.0, scalar2=None, op0=ALU.add),
             reads=[spb], writes=[spb])
        P.op("vector", lambda e: e.tensor_tensor(out=spc[:], in0=spc[:], in1=ex[:], op=ALU.mult), reads=[exb, spb], writes=[spb])
        P.op("vector", lambda e: e.tensor_scalar(out=spc[:], in0=spc[:], scalar1=-8.0, scalar2=None, op0=ALU.mult),
             reads=[spb], writes=[spb])

        for kt in range(KL):
            xt_, xb_ = big.next()
            P.dma("sync", xt_[:, :T], XBv[:, kt, :], XBb, xb_)
            ot, ob = xto.next()
            ov = ot[:].rearrange("p k t -> p (k t)")
            P.op("vector", lambda e, xt_=xt_, ov=ov, kt=kt: e.tensor_scalar(
                out=ov, in0=xt_[:, :T], scalar1=colsC[:, 3 * KL + kt: 3 * KL + kt + 1],
                scalar2=colsC[:, 88 + kt: 88 + kt + 1], op0=ALU.mult, op1=ALU.add),
                reads=[xb_, colsCb], writes=[ob])
            for sh in (1, 2, 3):
                P.op("vector", lambda e, xt_=xt_, ov=ov, kt=kt, sh=sh: e.scalar_tensor_tensor(
                    out=ov[:, sh:], in0=xt_[:, :T - sh], scalar=colsC[:, (3 - sh) * KL + kt: (3 - sh) * KL + kt + 1],
                    in1=ov[:, sh:], op0=ALU.mult, op1=ALU.add), reads=[xb_, colsCb, ob], writes=[ob])
            hb16, hb16b = hbo.next()
            hv = hb16[:].rearrange("p k t -> p (k t)")
            P.op("scalar", lambda e, hv=hv, ov=ov: e.copy(hv, ov), reads=[ob], writes=[hb16b])
            P.dma("sync", XCv[:, kt, :], ov, ob, XCb_, sembuf=ob)
            P.dma("sync", XChv[:, kt, :], hv, hb16b, XChb, sembuf=hb16b)
        P.fence("sync", xto.b + hbo.b)

        load_act(XChv, KL, XChb)

        def kr(si, m):
            kk = si * 4 + m
            h0 = (128 * kk) // 176
            h1 = (128 * kk + 127) // 176
            return (176 * h0) // 128, min(KL - 1, (176 * h1 + 175) // 128)

        def ep_gate(si, m, tb, pss):
            (pr, prb), (pi, pib) = pss
            kk = si * 4 + m
            r_, rb_ = tmpf.next()
            P.op("scalar", lambda e: e.activation(r_[:], pr[:], AF.Sigmoid, bias=colsD[:, kk:kk + 1], scale=1.0),
                 reads=[prb, colsDb], writes=[rb_])
            i_, ib_ = tmpf.next()
            P.op("scalar", lambda e: e.activation(i_[:], pi[:], AF.Sigmoid, bias=colsD[:, 22 + kk:22 + kk + 1], scale=1.0),
                 reads=[pib, colsDb], writes=[ib_])
            xc_, xcb = tmpf.next()
            P.dma("sync", xc_[:], XCv[:, kk, tb * TB:(tb + 1) * TB], XCb_, xcb)
            P.op("scalar", lambda e: e.activation(r_[:], r_[:], AF.Exp, scale=spc[:, kk:kk + 1]),
                 reads=[rb_, spb], writes=[rb_])
            P.op("vector", lambda e: e.tensor_tensor(out=i_[:], in0=i_[:], in1=xc_[:], op=ALU.mult),
                 reads=[ib_, xcb], writes=[ib_])
            P.op("vector", lambda e: e.tensor_tensor(out=xc_[:], in0=r_[:], in1=r_[:], op=ALU.mult),
                 reads=[rb_, xcb], writes=[xcb])
            P.op("vector", lambda e: e.tensor_scalar(out=xc_[:], in0=xc_[:], scalar1=-1.0, scalar2=1.0,
                                                     op0=ALU.mult, op1=ALU.add), reads=[xcb], writes=[xcb])
            P.op("scalar", lambda e: e.activation(xc_[:], xc_[:], AF.Sqrt), reads=[xcb], writes=[xcb])
            P.op("vector", lambda e: e.tensor_tensor(out=i_[:], in0=i_[:], in1=xc_[:], op=ALU.mult),
                 reads=[ib_, xcb], writes=[ib_])
            P.dma("sync", AAv[:, kk, tb * TB:(tb + 1) * TB], r_[:], rb_, AAb, sembuf=rb_)
            P.dma("sync", BTv[:, kk, tb * TB:(tb + 1) * TB], i_[:], ib_, BTb, sembuf=ib_)

        steps = [[lru_rg_d[:, j * 512: min(LW, (j + 1) * 512)], lru_ig_d[:, j * 512: min(LW, (j + 1) * 512)]]
                 for j in range((LW + 511) // 512)]
        gemm_ragged(steps, KL, ep_gate, kr)
        P.fence("sync", tmpf.b)

        P.fence("sync", [AAb, BTb, GBb])
        for kt in range(KL):
            at, atb = big.next()
            P.dma("sync", at[:, :T], AAv[:, kt, :], AAb, atb)
            bt, btb = xto.next()
            bv = bt[:].rearrange("p k t -> p (k t)")
            P.dma("sync", bv, BTv[:, kt, :], BTb, btb)
            gt, gtb = hbo.next()
            gv = gt[:].rearrange("p k t -> p (k t)")
            P.dma("sync", gv, GBv[:, kt, :], GBb, gtb)
            P.op("vector", lambda e, at=at, bv=bv: e.tensor_tensor_scan(
                out=bv, data0=at[:, :T], data1=bv, initial=0.0, op0=ALU.mult, op1=ALU.add),
                reads=[atb, btb], writes=[btb])
            P.op("vector", lambda e, gv=gv, bv=bv: e.tensor_tensor(out=gv, in0=bv, in1=gv, op=ALU.mult),
                 reads=[btb, gtb], writes=[gtb])
            P.dma("sync", YLv[:, kt, :], gv, gtb, YLb, sembuf=gtb)
        P.fence("sync", hbo.b)

        load_act(YLv, KL, YLb)

        def ep_res1(si, m, tb, pss):
            (ps, pb), = pss
            kk = si * 4 + m
            xt_, xb_ = tmpf.next()
            P.dma("sync", xt_[:], XTv[:, kk, tb * TB:(tb + 1) * TB], XTb, xb_)
            P.op("vector", lambda e: e.scalar_tensor_tensor(
                out=xt_[:], in0=ps[:], scalar=mcol(L, 2, kk), in1=xt_[:], op0=ALU.mult, op1=ALU.add),
                reads=[pb, xb_, modb], writes=[xb_])
            P.dma("sync", XTv[:, kk, tb * TB:(tb + 1) * TB], xt_[:], xb_, XTb, sembuf=xb_)

        gemm([[lru_out_d[:, j * 512:(j + 1) * 512]] for j in range(D // 512)], KL, ep_res1)
        P.fence("sync", tmpf.b)

    def gemm_ragged(steps, KT, epilogue, krange):
        for si, wlist in enumerate(steps):
            wts = []
            ncol = wlist[0].shape[1]
            for wap in wlist:
                wt, wb = wring.next()
                P.dma("gpsimd", wt[:, :KT, :ncol], wap.rearrange("(k p) n -> p k n", p=128), wbuf_d, wb)
                wts.append((wt, wb))
            for m in range(ncol // 128):
                for tb in range(NTB):
                    pss = []
                    for (wt, wb) in wts:
                        ps, pb = next_ps()
                        fns = []
                        k0, k1 = krange(si, m)
                        for k in range(k0, k1 + 1):
                            fns.append(lambda e, ps=ps, wt=wt, m=m, k=k, tb=tb, k0=k0, k1=k1: e.matmul(
                                ps[:], lhsT=wt[:, k, m * 128:(m + 1) * 128],
                                rhs=act[:, k, tb * TB:(tb + 1) * TB], start=(k == k0), stop=(k == k1)))
                        P.mm(fns, reads=[wb, actb], writes=[pb])
                        pss.append((ps, pb))
                    epilogue(si, m, tb, pss)

    if dbg == "LRU":
        lru_mixer(1)

    PI = 3.141592653589793
    s5_in_d = din("s5_w_in", [D, D])
    s5_glu_d = din("s5_w_glu", [D, 2 * D])
    lamre_d = din("s5_lam_re", [128, 64]); lamim_d = din("s5_lam_im", [128, 64]); logdt_d = din("s5_log_dt", [128, 1])
    bre_d = din("s5_b_re", [128, 1024]); bim_d = din("s5_b_im", [128, 1024])
    cre_d = din("s5_c_re", [128, 1024]); cim_d = din("s5_c_im", [128, 1024])
    s5d_d = din("s5_d_row", [1, D])
    WST = dscr("WST", [128, 128, 128], BF16); WINT = dscr("WINT", [128, 128, 128], BF16)
    WOR = dscr("WOR", [128, 64, 128], BF16); WOI = dscr("WOI", [128, 64, 128], BF16)
    WSTb, WINTb, WORb, WOIb = Buf("WST"), Buf("WINT"), Buf("WOR"), Buf("WOI")
    YT = dscr("YT", [D, T], BF16); YTb = Buf("YT")
    YTv = YT.rearrange("(k p) t -> p k t", p=128)
    identh = sb("identh", [128, 128], BF16); identhb = Buf("identh")
    P.op("vector", lambda e: e.tensor_copy(identh[:], ident[:]), reads=[identb], writes=[identhb])
    cA4 = sb("cA4", [64, 2, 2, 128], F32); cAb = Buf("cA")
    cA1 = cA4[:, 0, :, :]; cA2 = cA4[:, 1, :, :]
    vpb = [Buf("vst0"), Buf("vst1")]
    dtab = sb("dtab", [128, D], F32); dtabb = Buf("dtab")

    def V(fn, r, w):
        return P.op("vector", fn, reads=r, writes=w)

    def A(fn, r, w):
        return P.op("scalar", fn, reads=r, writes=w)

    def tt(out, a, b, op):
        return lambda e: e.tensor_tensor(out=out, in0=a, in1=b, op=op)

    def s5_prep():
        nat = sb("s5nat", [128, 130], F32); natb = Buf("s5nat")
        P.dma("sync", nat[:, 0:64], lamre_d, vdb, natb)
        P.dma("sync", nat[:, 64:128], lamim_d, vdb, natb)
        P.dma("sync", nat[:, 128:129], logdt_d, vdb, natb)
        P.dma("sync", dtab[:], s5d_d.partition_broadcast(128), vdb, dtabb)
        Lr = sb("s5Lr", [128, 9, 64], F32); Li = sb("s5Li", [128, 9, 64], F32); Lb = Buf("s5L")
        sm = sb("s5sm", [128, 12, 64], F32); smb = Buf("s5sm")
        lr = nat[:, 0:64]; li = nat[:, 64:128]
        dtc = sm[:, 11, 0:1]
        A(lambda e: e.activation(dtc, nat[:, 128:129], AF.Exp), [natb], [smb])
        lrdt, lidt, mag, sa, ca, t1, t2, fr, fi, nr, den = [sm[:, i, :] for i in range(11)]
        V(lambda e: e.tensor_scalar(out=lrdt, in0=lr, scalar1=dtc, scalar2=None, op0=ALU.mult), [natb, smb], [smb])
        V(lambda e: e.tensor_scalar(out=lidt, in0=li, scalar1=dtc, scalar2=None, op0=ALU.mult), [natb, smb], [smb])
        A(lambda e: e.activation(mag, lrdt, AF.Exp), [smb], [smb])
        A(lambda e: e.activation(sa, lidt, AF.Sin, scale=1.0 / 16.0), [smb], [smb])
        V(lambda e: e.tensor_scalar(out=ca, in0=lidt, scalar1=1.0 / 16.0, scalar2=PI / 2, op0=ALU.mult, op1=ALU.add), [smb], [smb])
        A(lambda e: e.activation(ca, ca, AF.Sin), [smb], [smb])
        for _ in range(4):
            V(tt(t1, ca, ca, ALU.mult), [smb], [smb])
            V(tt(t2, sa, sa, ALU.mult), [smb], [smb])
            V(tt(sa, ca, sa, ALU.mult), [smb], [smb])
            V(lambda e: e.tensor_scalar(out=sa, in0=sa, scalar1=2.0, scalar2=None, op0=ALU.mult), [smb], [smb])
            V(tt(ca, t1, t2, ALU.subtract), [smb], [smb])
        V(lambda e: e.memset(Lr[:, 0, :], 1.0), [], [Lb])
        V(lambda e: e.memset(Li[:, 0, :], 0.0), [], [Lb])
        V(tt(Lr[:, 1, :], mag, ca, ALU.mult), [smb], [Lb])
        V(tt(Li[:, 1, :], mag, sa, ALU.mult), [smb], [Lb])
        for k in range(2, 9):
            V(tt(t1, Lr[:, k - 1, :], Lr[:, 1, :], ALU.mult), [Lb], [smb])
            V(tt(t2, Li[:, k - 1, :], Li[:, 1, :], ALU.mult), [Lb], [smb])
            V(tt(Lr[:, k, :], t1, t2, ALU.subtract), [smb], [Lb])
            V(tt(t1, Lr[:, k - 1, :], Li[:, 1, :], ALU.mult), [Lb], [smb])
            V(tt(t2, Li[:, k - 1, :], Lr[:, 1, :], ALU.mult), [Lb], [smb])
            V(tt(Li[:, k, :], t1, t2, ALU.add), [smb], [Lb])
        V(lambda e: e.tensor_scalar(out=nr, in0=Lr[:, 1, :], scalar1=-1.0, scalar2=None, op0=ALU.add), [Lb], [smb])
        V(tt(den, lr, lr, ALU.mult), [natb], [smb])
        V(tt(t1, li, li, ALU.mult), [natb], [smb])
        V(tt(den, den, t1, ALU.add), [smb], [smb])
        V(lambda e: e.reciprocal(den, den), [smb], [smb])
        V(tt(t1, nr, lr, ALU.mult), [smb, natb], [smb])
        V(tt(t2, Li[:, 1, :], li, ALU.mult), [Lb, natb], [smb])
        V(tt(fr, t1, t2, ALU.add), [smb], [smb])
        V(tt(fr, fr, den, ALU.mult), [smb], [smb])
        V(tt(t1, Li[:, 1, :], lr, ALU.mult), [Lb, natb], [smb])
        V(tt(t2, nr, li, ALU.mult), [smb, natb], [smb])
        V(tt(fi, t1, t2, ALU.subtract), [smb], [smb])
        V(tt(fi, fi, den, ALU.mult), [smb], [smb])
        for (src, dst0, dst1, neg) in ((Lr[:, 8, :], cA1[:, 0, :], cA1[:, 1, :], False), (Li[:, 8, :], cA2[:, 1, :], cA2[:, 0, :], True)):
            ps, pb = next_ps()
            P.mm([lambda e, ps=ps, src=src: e.transpose(ps[0:64, 0:128], src, ident[:])], reads=[Lb, identb], writes=[pb])
            V(lambda e, ps=ps, dst0=dst0: e.tensor_copy(dst0, ps[0:64, 0:128]), [pb], [cAb])
            if neg:
                V(lambda e, ps=ps, dst1=dst1: e.tensor_scalar(out=dst1, in0=ps[0:64, 0:128], scalar1=-1.0, scalar2=None, op0=ALU.mult), [pb], [cAb])
            else:
                V(lambda e, ps=ps, dst1=dst1: e.tensor_copy(dst1, ps[0:64, 0:128]), [pb], [cAb])
        s5_prep.sm = sm; s5_prep.smb = smb

        w0, w0b = wring.t[0], wring.b[0]
        w1, w1b = wring.t[1], wring.b[1]
        f32v = w1[:].rearrange("p k n -> p (k n)").bitcast(F32)
        bre, bim, bbre, bbim, tA = [f32v[:, i * 1024:(i + 1) * 1024] for i in range(5)]
        g0, g0b = big.t[0], big.b[0]
        g1_, g1b = big.t[1], big.b[1]
        cre, cim = g0[:, 0:1024], g0[:, 1024:2048]
        tB, tC = g1_[:, 0:1024], g1_[:, 1024:2048]
        w1aux = Buf("w1aux")
        P.dma("sync", bre, bre_d, vdb, w1b, sembuf=w1aux)
        P.dma("sync", bim, bim_d, vdb, w1b, sembuf=w1aux)
        P.dma("sync", cre, cre_d, vdb, g0b)
        P.dma("sync", cim, cim_d, vdb, g0b)
        actf = act[:].rearrange("p k t -> p (k t)")
        stage = actf[:, 0:16384].rearrange("p (g j) -> p g j", j=128)
        Mre = actf[:, 16384:24576].rearrange("p (q s c) -> p q s c", q=64, s=8)
        Mim = actf[:, 24576:32768].rearrange("p (q s c) -> p q s c", q=64, s=8)
        Nre = actf[:, 32768:40960].rearrange("p (t o q) -> p t o q", t=8, o=16)
        Nim = w0[:].rearrange("p k n -> p (k n)")[:, 0:8192].rearrange("p (t o q) -> p t o q", t=8, o=16)
        Krev = xto.t[0][:].rearrange("p k t -> p (k t)").bitcast(BF16)[:, 0:4096].rearrange("p (s o c) -> p s o c", s=16, o=16)
        KT = xto.t[1][:].rearrange("p k t -> p (k t)").bitcast(BF16)[:, 0:2048].rearrange("p (o s c) -> p o s c", o=16, s=8)
        krb, ktb = xto.b[0], xto.b[1]
        b3 = lambda a: a.rearrange("p (q c) -> p q c", c=16)
        bc3 = lambda a: a.unsqueeze(2).broadcast_to([128, 64, 16])
        V(tt(b3(tA), b3(bre), bc3(fr), ALU.mult), [w1b, smb], [w1b])
        V(tt(b3(tB), b3(bim), bc3(fi), ALU.mult), [w1b, smb], [g1b])
        V(tt(bbre, tA, tB, ALU.subtract), [w1b, g1b], [w1b])
        V(tt(b3(tA), b3(bim), bc3(fr), ALU.mult), [w1b, smb], [w1b])
        V(tt(b3(tB), b3(bre), bc3(fi), ALU.mult), [w1b, smb], [g1b])
        V(tt(bbim, tA, tB, ALU.add), [w1b, g1b], [w1b])
        for s_ in range(8):
            k = 7 - s_
            V(tt(b3(tA), b3(bbre), bc3(Lr[:, k, :]), ALU.mult), [w1b, Lb], [w1b])
            V(tt(b3(tB), b3(bbim), bc3(Li[:, k, :]), ALU.mult), [w1b, Lb], [g1b])
            V(tt(Mre[:, :, s_, :], b3(tA), b3(tB), ALU.subtract), [w1b, g1b], [actb])
            V(tt(b3(tA), b3(bbim), bc3(Lr[:, k, :]), ALU.mult), [w1b, Lb], [w1b])
            V(tt(b3(tB), b3(bbre), bc3(Li[:, k, :]), ALU.mult), [w1b, Lb], [g1b])
            V(tt(Mim[:, :, s_, :], b3(tA), b3(tB), ALU.add), [w1b, g1b], [actb])
        V(lambda e: e.memset(Krev, 0.0), [], [krb])
        c3 = lambda a: a.rearrange("p (o q) -> p o q", q=64)
        bo3 = lambda a: a.unsqueeze(1).broadcast_to([128, 16, 64])
        nre_f = tC.rearrange("p (o q) -> p o q", q=64)
        nim_f = f32v[:, 5120:5632]
        nimb = hbo.b[1]
        nim_f = hbo.t[1][:].rearrange("p k t -> p (k t)").bitcast(F32).rearrange("p (o q) -> p o q", q=64)
        kred = sb("kred", [128, 16], F32); kredb = Buf("kred")
        bbreT = bbre.rearrange("p (q c) -> p c q", c=16)
        bbimT = bbim.rearrange("p (q c) -> p c q", c=16)
        for k in range(9):
            V(tt(c3(tA), c3(cre), bo3(Lr[:, k, :]), ALU.mult), [g0b, Lb], [w1b])
            V(tt(c3(tB), c3(cim), bo3(Li[:, k, :]), ALU.mult), [g0b, Lb], [g1b])
            V(tt(nre_f, c3(tA), c3(tB), ALU.subtract), [w1b, g1b], [g1b])
            V(tt(c3(tA), c3(cre), bo3(Li[:, k, :]), ALU.mult), [g0b, Lb], [w1b])
            V(tt(c3(tB), c3(cim), bo3(Lr[:, k, :]), ALU.mult), [g0b, Lb], [g1b])
            V(tt(nim_f, c3(tA), c3(tB), ALU.add), [w1b, g1b], [nimb])
            if k >= 1:
                V(lambda e, k=k: e.tensor_copy(Nre[:, k - 1, :, :], nre_f), [g1b], [actb])
                V(lambda e, k=k: e.tensor_scalar(out=Nim[:, k - 1, :, :], in0=nim_f, scalar1=-1.0, scalar2=None, op0=ALU.mult), [nimb], [w0b])
            if k <= 7:
                for o in range(16):
                    V(tt(tA.rearrange("p (c q) -> p c q", q=64), bbreT, nre_f[:, o, :].unsqueeze(1).broadcast_to([128, 16, 64]), ALU.mult), [w1b, g1b], [w1b])
                    V(tt(tB.rearrange("p (c q) -> p c q", q=64), bbimT, nim_f[:, o, :].unsqueeze(1).broadcast_to([128, 16, 64]), ALU.mult), [w1b, nimb, g1b], [g1b])
                    V(tt(tA, tA, tB, ALU.subtract), [w1b, g1b], [w1b])
                    V(lambda e, k=k, o=o: e.tensor_reduce(out=kred[:], in_=tA.rearrange("p (c q) -> p c q", q=64),
                                                            axis=AX.X, op=ALU.add), [w1b], [kredb])
                    V(lambda e, k=k, o=o: e.tensor_copy(Krev[:, 7 - k, o, :], kred[:]), [kredb], [krb])

        def family(n_in, src_fn, src_bufs, dram, dramb, rows):
            for j4 in range(32):
                ps, pb = next_ps()
                psh = ps[:].bitcast(BF16)
                for q in range(4):
                    j = j4 * 4 + q
                    P.mm([lambda e, psh=psh, q=q, j=j: e.transpose(psh[0:n_in, q * 128:(q + 1) * 128], src_fn(j), identh[:])],
                         reads=src_bufs + [identhb], writes=[pb])
                cp = (lambda e, psh=psh, j4=j4: e.tensor_copy(stage[0:n_in, :, j4 * 4:(j4 + 1) * 4],
                                                              psh[0:n_in, 0:512].rearrange("p (j g) -> p g j", j=4)))
                if j4 % 2 == 0:
                    V(cp, [pb], [actb])
                else:
                    A(lambda e, psh=psh, j4=j4: e.copy(stage[0:n_in, :, j4 * 4:(j4 + 1) * 4],
                                                       psh[0:n_in, 0:512].rearrange("p (j g) -> p g j", j=4)), [pb], [actb])
            P.dma("sync", dram.rearrange("g r j -> r g j"), stage[0:rows, :, :], actb, dramb, sembuf=actb)

        family(128, lambda j: (Mre if j < 64 else Mim)[:, j % 64, :, :].rearrange("p s c -> p (s c)"), [actb], WST, WSTb, 128)
        family(64, lambda j: Nre[:, j // 16, j % 16, :], [actb], WOR, WORb, 64)
        family(64, lambda j: Nim[:, j // 16, j % 16, :], [w0b], WOI, WOIb, 64)
        for t_ in range(8):
            V(lambda e, t_=t_: e.tensor_copy(KT, Krev[:, 7 - t_: 15 - t_, :, :].rearrange("p s o c -> p o s c")), [krb], [ktb])
            for j4 in range(4):
                ps, pb = next_ps()
                psh = ps[:].bitcast(BF16)
                for q in range(4):
                    o = j4 * 4 + q
                    P.mm([lambda e, psh=psh, q=q, o=o: e.transpose(psh[:, q * 128:(q + 1) * 128],
                                                                   KT[:, o, :, :].rearrange("p s c -> p (s c)"), identh[:])],
                         reads=[ktb, identhb], writes=[pb])
                jb = t_ * 16 + j4 * 4
                V(lambda e, psh=psh, jb=jb: e.tensor_copy(stage[:, :, jb:jb + 4],
                                                          psh[:, 0:512].rearrange("p (j g) -> p g j", j=4)), [pb], [actb])
        P.dma("sync", WINT.rearrange("g r j -> r g j"), stage[:, :, :], actb, WINTb, sembuf=actb)
        P.fence("sync", [actb])

    def s5_main():
        norm_phase(0, 0)
        P.fence("sync", [HTb])
        actf = act[:].rearrange("p k t -> p (k t)")
        hsb = actf[:, 0:16384].rearrange("p (k t) -> p k t", k=KD)
        wsec = actf[:, 16384:32768].rearrange("p (w g j) -> p w g j", w=4, g=32)
        wsecb = Buf("wsec")
        sp_ = actf[0:64, 32768:40960].rearrange("p (r g c) -> p r g c", r=2, g=32)
        spb_ = Buf("sprev")
        xb_ = actf[:, 40960:45056].rearrange("p (g c) -> p g c", c=128)
        xbb = Buf("xblk")
        ub = xto.t[0][:].rearrange("p k t -> p (k t)").bitcast(BF16).rearrange("p (g s c) -> p g s c", g=32, s=8)
        ubb = xto.b[0]
        ym = xto.t[1][:].rearrange("p k t -> p (k t)").bitcast(BF16).rearrange("p (t c) -> p t c", c=512)
        ymb = xto.b[1]
        yos = [hbo.t[i][:].rearrange("p k t -> p (k t)").rearrange("p (a t) -> p a t", a=2) for i in range(2)]
        yobs = [hbo.b[0], hbo.b[1]]
        slr = [big.t[ri][:].bitcast(BF16)[0:64, :].rearrange("p (g c) -> p g c", c=128) for ri in range(2)]
        slbs = [big.b[0], big.b[1]]
        tsc = sb("tsc", [64, 2, 2, 32], F32)
        usc = sb("usc", [64, 2, 32], F32)
        tscb = Buf("tsc")
        uscb = Buf("usc")
        vst = s5_prep.sm[0:64, :, :].rearrange("p a b -> p (a b)").rearrange("p (j q s g) -> p j q s g", j=4, q=2, s=3)
        V(lambda e: e.memset(vst, 0.0), [], [s5_prep.smb, vpb[0], vpb[1]])
        for SBi in range(2):
            for k in range(KD):
                P.dma("sync", hsb[:, k, :], HTv[:, k, SBi * 1024:(SBi + 1) * 1024], HTb, actb)
            for j in range(4):
                gb = j * 32
                wt, wb = wring.next()
                P.dma("gpsimd", wt[:, :KD, :], s5_in_d[:, j * 512:(j + 1) * 512].rearrange("(k p) n -> p k n", p=128), wbuf_d, wb)
                P.dma("sync", wsec[:, 0, :, :], WST[gb:gb + 32].rearrange("g r j -> r g j"), WSTb, wsecb)
                P.dma("sync", wsec[:, 1, :, :], WINT[gb:gb + 32].rearrange("g r j -> r g j"), WINTb, wsecb)
                P.dma("sync", wsec[0:64, 2, :, :], WOR[gb:gb + 32].rearrange("g r j -> r g j"), WORb, wsecb)
                P.dma("sync", wsec[0:64, 3, :, :], WOI[gb:gb + 32].rearrange("g r j -> r g j"), WOIb, wsecb)
                for s_ in range(8):
                    ps, pb = next_ps()
                    fns = []
                    for k in range(KD):
                        fns.append(lambda e, ps=ps, k=k, s_=s_, wt=wt: e.matmul(
                            ps[:], lhsT=hsb[:, k, s_:1024:8], rhs=wt[:, k, :],
                            start=(k == 0), stop=(k == KD - 1)))
                    P.mm(fns, reads=[actb, wb], writes=[pb])
                    src = ps[:].rearrange("p (g c) -> p g c", c=16)
                    A(lambda e, s_=s_, src=src: e.copy(ub[:, :, s_, :], src), [pb], [ubb])
                for g4 in range(8):
                    ps, pb = next_ps()
                    psh = ps[:].bitcast(BF16)
                    for q in range(4):
                        gl = g4 * 4 + q
                        P.mm([lambda e, psh=psh, q=q, gl=gl: e.transpose(
                            psh[:, q * 128:(q + 1) * 128],
                            ub[:, gl, :, :].rearrange("p s c -> p (s c)"), identh[:])],
                            reads=[ubb, identhb], writes=[pb])
                    A(lambda e, psh=psh, g4=g4: e.copy(
                        xb_[:, g4 * 4:(g4 + 1) * 4, :], psh[:, 0:512].rearrange("p (g c) -> p g c", g=4)), [pb], [xbb])
                for g2 in range(16):
                    ps, pb = next_ps()
                    for q in range(2):
                        gl = g2 * 2 + q
                        for ri in range(2):
                            P.mm([lambda e, ps=ps, q=q, ri=ri, gl=gl: e.matmul(
                                ps[0:64, (q * 2 + ri) * 128:(q * 2 + ri + 1) * 128],
                                lhsT=wsec[:, 0, gl, ri * 64:(ri + 1) * 64], rhs=xb_[:, gl, :], start=True, stop=True)],
                                reads=[wsecb, xbb], writes=[pb])
                    for ri in range(2):
                        A(lambda e, ps=ps, g2=g2, ri=ri: e.copy(
                            slr[ri][:, g2 * 2:(g2 + 1) * 2, :],
                            ps[0:64, :].rearrange("p (q r c) -> p r q c", q=2, r=2)[:, ri, :, :]), [pb], [slbs[ri]])
                c4 = cA4[:, :, :, gb:gb + 32]
                for c_ in range(128):
                    rd, wr = c_ % 2, 1 - (c_ % 2)
                    v3 = vst[:, j, rd, :, :]
                    w3 = vst[:, j, wr, :, :]
                    win = bass.AP(v3.tensor, v3.offset, [list(v3.ap[0]), [32, 2], [32, 2], [1, 32]])
                    w02 = bass.AP(w3.tensor, w3.offset, [list(w3.ap[0]), [64, 2], [1, 32]])
                    V(tt(tsc[:], c4, win, ALU.mult), [cAb, vpb[rd]], [tscb])
                    V(tt(usc[:], tsc[:, 0, :, :], tsc[:, 1, :, :], ALU.add), [tscb], [uscb])
                    A(lambda e, c_=c_, v3=v3: e.copy(sp_[:, :, :, c_], v3[:, 0:2, :]), [vpb[rd]], [spb_])
                    V(tt(w02, usc[:, 0, :].unsqueeze(1).broadcast_to([64, 2, 32]),
                         slr[0][:, :, c_].unsqueeze(1).broadcast_to([64, 2, 32]), ALU.add), [uscb, slbs[0]], [vpb[wr]])
                    V(tt(w3[:, 1, :], usc[:, 1, :], slr[1][:, :, c_], ALU.add), [uscb, slbs[1]], [vpb[wr]])
                for g4 in range(8):
                    ps, pb = next_ps()
                    for q in range(4):
                        gl = g4 * 4 + q
                        P.mm([lambda e, ps=ps, q=q, gl=gl: e.matmul(
                                ps[:, q * 128:(q + 1) * 128], lhsT=xb_[:, gl, :], rhs=wsec[:, 1, gl, :], start=True, stop=False),
                              lambda e, ps=ps, q=q, gl=gl: e.matmul(
                                ps[:, q * 128:(q + 1) * 128], lhsT=sp_[:, 0, gl, :], rhs=wsec[0:64, 2, gl, :], start=False, stop=False),
                              lambda e, ps=ps, q=q, gl=gl: e.matmul(
                                ps[:, q * 128:(q + 1) * 128], lhsT=sp_[:, 1, gl, :], rhs=wsec[0:64, 3, gl, :], start=False, stop=True)],
                             reads=[xbb, wsecb, spb_], writes=[pb])
                    yf_, yfb = tmpf.next()
                    yf = yf_[:].rearrange("p (g t c) -> p g t c", g=4, t=8)
                    ch0 = (gb + g4 * 4) * 16
                    P.op("gpsimd", tt(yf, ub[:, g4 * 4: g4 * 4 + 4, :, :],
                         dtab[:, ch0:ch0 + 64].rearrange("p (g c) -> p g c", c=16).unsqueeze(2).broadcast_to([128, 4, 8, 16]),
                         ALU.mult), reads=[ubb, dtabb], writes=[yfb])
                    V(tt(yf, yf, ps[:].rearrange("p (g t c) -> p g t c", g=4, t=8), ALU.add), [yfb, pb], [yfb])
                    ho, hob = gelu_tile(yf_[:], yfb, 512, eng="gpsimd")
                    P.op("gpsimd", lambda e, ho=ho, g4=g4: e.tensor_copy(
                        ym[:, :, g4 * 64:(g4 + 1) * 64].rearrange("p t (g c) -> p g t c", c=16),
                        ho[:, :512].rearrange("p (g t c) -> p g t c", g=4, t=8)), reads=[hob], writes=[ymb])
                for t4 in range(8):
                    ps, pb = next_ps()
                    psh = ps[:].bitcast(BF16)
                    for q in range(4):
                        idx = t4 * 4 + q
                        t_, ct = idx // 4, idx % 4
                        P.mm([lambda e, psh=psh, q=q, t_=t_, ct=ct: e.transpose(
                            psh[:, q * 128:(q + 1) * 128], ym[:, t_, ct * 128:(ct + 1) * 128], identh[:])],
                            reads=[ymb, identhb], writes=[pb])
                    for q in range(4):
                        idx = t4 * 4 + q
                        t_, ct = idx // 4, idx % 4
                        A(lambda e, psh=psh, q=q, t_=t_, ct=ct: e.copy(
                            yos[ct // 2][:, ct % 2, t_:1024:8], psh[:, q * 128:(q + 1) * 128]), [pb], [yobs[ct // 2]])
                kt0 = gb // 8
                for hh in range(2):
                    P.dma("sync", YTv[:, kt0 + 2 * hh: kt0 + 2 * hh + 2, SBi * 1024:(SBi + 1) * 1024], yos[hh], yobs[hh], YTb, sembuf=yobs[hh])
        P.fence("sync", hbo.b)
        load_act(YTv, KD, YTb)

        def ep_glu(si, m, tb, pss):
            (pv, pvb), (pg, pgb) = pss
            kk = si * 4 + m
            sg, sgb = tmpf.next()
            A(lambda e: e.activation(sg[:], pg[:], AF.Sigmoid), [pgb], [sgb])
            V(tt(sg[:], sg[:], pv[:], ALU.mult), [sgb, pvb], [sgb])
            xt_, xb2 = tmpf.next()
            P.dma("sync", xt_[:], XTv[:, kk, tb * TB:(tb + 1) * TB], XTb, xb2)
            V(lambda e: e.scalar_tensor_tensor(out=xt_[:], in0=sg[:], scalar=mcol(0, 2, kk), in1=xt_[:],
                                               op0=ALU.mult, op1=ALU.add), [sgb, xb2, modb], [xb2])
            P.dma("sync", XTv[:, kk, tb * TB:(tb + 1) * TB], xt_[:], xb2, XTb, sembuf=xb2)

        gemm([[s5_glu_d[:, jj * 512:(jj + 1) * 512], s5_glu_d[:, D + jj * 512: D + (jj + 1) * 512]] for jj in range(4)], KD, ep_glu)
        P.fence("sync", tmpf.b)

    def final_phase():
        P.fence("sync", [XTb])
        for q in range(T // 128):
            xt_, xb_ = big.next()
            xv = xt_[:].rearrange("p (k t) -> p k t", k=KD)
            P.dma("sync", xv, XTv[:, :, q * 128:(q + 1) * 128], XTb, xb_)
            ht, hb = hbo.next()
            A(lambda e, ht=ht, xv=xv: e.activation(ht[:], xv, AF.Square), [xb_], [hb])
            ps, pb = next_ps()
            fns = []
            for k in range(KD):
                fns.append(lambda e, ps=ps, ht=ht, k=k: e.matmul(
                    ps[:, :128], lhsT=ones_bf[:], rhs=ht[:, k, :], start=(k == 0), stop=(k == KD - 1)))
            P.mm(fns, reads=[hb, onesb], writes=[pb])
            rs, rb = tmpf.next()
            V(lambda e, rs=rs, ps=ps: e.tensor_scalar(out=rs[:, :128], in0=ps[:, :128], scalar1=1.0 / D, scalar2=EPS,
                                                      op0=ALU.mult, op1=ALU.add), [pb], [rb])
            A(lambda e, rs=rs: e.activation(rs[:, :128], rs[:, :128], AF.Sqrt), [rb], [rb])
            V(lambda e, rs=rs: e.reciprocal(rs[:, :128], rs[:, :128]), [rb], [rb])
            for k in range(KD):
                V(lambda e, xv=xv, k=k, rs=rs: e.scalar_tensor_tensor(
                    out=xv[:, k, :], in0=xv[:, k, :], scalar=colsA[:, 80 + k:81 + k], in1=rs[:, :128],
                    op0=ALU.mult, op1=ALU.mult), [xb_, rb, colsAb], [xb_])
            ot, ob = xto.next()
            otv = ot[:].rearrange("p k t -> p (k t)")
            for k4 in range(4):
                ps2, pb2 = next_ps()
                for jq in range(4):
                    k = k4 * 4 + jq
                    P.mm([lambda e, ps2=ps2, jq=jq, k=k, xv=xv: e.transpose(
                        ps2[:, jq * 128:(jq + 1) * 128], xv[:, k, :], ident[:])], reads=[xb_, identb], writes=[pb2])
                if k4 % 2 == 0:
                    V(lambda e, ps2=ps2, k4=k4, otv=otv: e.tensor_copy(otv[:, k4 * 512:(k4 + 1) * 512], ps2[:]), [pb2], [ob])
                else:
                    A(lambda e, ps2=ps2, k4=k4, otv=otv: e.copy(otv[:, k4 * 512:(k4 + 1) * 512], ps2[:]), [pb2], [ob])
            P.dma("sync", out_d[q * 128:(q + 1) * 128, :], otv, ob, Buf("outd"), sembuf=ob)
        P.fence("sync", xto.b)

    if dbg == "S5":
        s5_prep()
        s5_main()
    snaps = []

    def snap(name):
        d_ = nc.dram_tensor(name, [D, T], F32, kind="ExternalOutput").ap()
        P.fence("sync", [XTb] + tmpf.b)
        b_ = Buf(name)
        P.dma("sync", d_, XT, XTb, b_)
        snaps.append(b_)

    if dbg is None or dbg == "ALL":
        s5_prep()
        s5_main()
        if dbg: snap("dbg0")
        ffn(0)
        if dbg: snap("dbg1")
        lru_mixer(1)
        if dbg: snap("dbg2")
        ffn(1)
        if dbg: snap("dbg3")
        final_phase()
        P.fence("sync", snaps)

    if dbg == "FFN0":
        ffn(0)

    if dbg in ("XT", "FFN0", "LRU", "S5"):
        dbg_d = nc.dram_tensor("dbg", [D, T], F32, kind="ExternalOutput").ap()
        P.fence("sync", [XTb] + tmpf.b)
        dbgb = Buf("dbgb")
        P.dma("sync", dbg_d, XT, XTb, dbgb)
        P.fence("sync", [dbgb])
    P.emit()
    return nc, es


def kernel(**inputs):
    dbg = os.environ.get("KDBG")
    nc, es = build(dbg=dbg)
    f = lambda a: np.ascontiguousarray(a, dtype=np.float32)
    x = f(inputs["x"]); c = f(inputs["c"])
    ng = f(inputs["norm_g"]).reshape(64, 128)
    fg = f(inputs["final_g"]).reshape(16, 128)
    sd = f(inputs["s5_d"]).reshape(16, 128)
    vecB = f(inputs["b_ada"]).reshape(2, 96, 128)
    vecC = np.zeros((128, 128), np.float32)
    vecC[0:88] = f(inputs["lru_conv_w"]).reshape(88, 128)
    vecC[88:110] = f(inputs["lru_conv_b"]).reshape(22, 128)
    vecD = np.zeros((128, 128), np.float32)
    vecD[0:22] = f(inputs["lru_b_rg"]).reshape(22, 128)
    vecD[22:44] = f(inputs["lru_b_ig"]).reshape(22, 128)
    vecD[44:66] = f(inputs["lru_lam"]).reshape(22, 128)
    def dense_bd(w):
        w = f(w)[0]
        o = np.zeros((LW, LW), np.float32)
        for h in range(16):
            o[h * 176:(h + 1) * 176, h * 176:(h + 1) * 176] = w[h]
        return o
    rgd = dense_bd(inputs["lru_w_rg"]); igd = dense_bd(inputs["lru_w_ig"])
    in_maps = []
    for b in range(NCORES):
        vecA = np.zeros((128, 128), np.float32)
        vecA[0:16] = c[b].reshape(16, 128)
        vecA[16:80] = ng
        vecA[80:96] = fg
        vecA[96:112] = sd
        in_maps.append({"x": x[b], "c": c[b].reshape(KD, 128), "ident": np.eye(128, dtype=np.float32),
                        "w_ada": f(inputs["w_ada"]), "vecA": vecA, "vecB": vecB, "vecC": vecC, "vecD": vecD,
                        "ffn_w_gu": f(inputs["ffn_w_gu"]), "ffn_w_down": f(inputs["ffn_w_down"]),
                        "lru_w_in": f(inputs["lru_w_in"])[0], "lru_rg_dense": rgd, "lru_ig_dense": igd,
                        "lru_w_out": f(inputs["lru_w_out"])[0],
                        "s5_w_in": f(inputs["s5_w_in"])[0], "s5_w_glu": f(inputs["s5_w_glu"])[0],
                        "s5_lam_re": f(inputs["s5_lam_re"])[0], "s5_lam_im": f(inputs["s5_lam_im"])[0],
                        "s5_log_dt": f(inputs["s5_log_dt"]).reshape(128, 1),
                        "s5_b_re": f(inputs["s5_b_re"]).reshape(128, 1024), "s5_b_im": f(inputs["s5_b_im"]).reshape(128, 1024),
                        "s5_c_re": f(inputs["s5_c_re"]).reshape(128, 1024), "s5_c_im": f(inputs["s5_c_im"]).reshape(128, 1024),
                        "s5_d_row": f(inputs["s5_d"]).reshape(1, D)})
    res = run_bass_kernel_spmd(nc, in_maps, core_ids=list(range(NCORES)))
    es.close()
    if dbg == "ALL":
        return [{k: r[k] for k in ("dbg0", "dbg1", "dbg2", "dbg3", "out")} for r in res.results]
    if dbg:
        return [r["dbg"] for r in res.results]
    return np.stack([r["out"] for r in res.results], axis=0)
```

```python
import os
from contextlib import ExitStack
import numpy as np
import concourse.bass as bass
import concourse.mybir as mybir
from concourse.bass_utils import run_bass_kernel_spmd

F32 = mybir.dt.float32
BF16 = mybir.dt.bfloat16
AF = mybir.ActivationFunctionType
ALU = mybir.AluOpType
AX = mybir.AxisListType

D = 2048
T = 2048
TB = 512
NTB = T // TB
KD = D // 128
FH = 5632
KF = FH // 128
LW = 2816
KL = LW // 128
EPS = 1e-6
NCORES = 4


class Buf:
    def __init__(self, name):
        self.name = name
        self.w = None
        self.r = []
        self.sem = None
        self.cnt = 0


class Prog:
    ENGS = ["sync", "scalar", "vector", "gpsimd", "tensor"]

    def __init__(self, nc, es):
        self.nc = nc
        self.es = es
        self.q = {e: [] for e in self.ENGS}
        self.esem = {}
        self.ecnt = {}
        for e in ["scalar", "vector", "gpsimd", "tensor"]:
            self.esem[e] = es.enter_context(nc.semaphore("es_" + e))
            self.ecnt[e] = 0
        self.known = {e: {} for e in self.ENGS}
        self.nsem = 4
        self.semobj = {}

    def _waits(self, eng, toks):
        need = {}
        for t in toks:
            if t is None:
                continue
            s, v = t
            if self.known[eng].get(id(s), 0) >= v:
                continue
            if need.get(id(s), (s, 0))[1] < v:
                need[id(s)] = (s, v)
        out = []
        for k, (s, v) in need.items():
            self.known[eng][k] = v
            out.append((s, v))
        return out

    def _deps(self, reads, writes):
        toks = []
        for b in reads:
            toks.append(b.w)
        for b in writes:
            toks.append(b.w)
            toks.extend(b.r)
        return toks

    def op(self, eng, fn, reads=(), writes=(), pe_same_ok=False):
        toks = self._deps(reads, writes)
        if pe_same_ok:
            toks = [t for t in toks if t is None or t[0] is not self.esem["tensor"]]
        w = self._waits(eng, toks)
        self.ecnt[eng] += 1
        tok = (self.esem[eng], self.ecnt[eng])
        self.q[eng].append((w, fn, (self.esem[eng], 1)))
        for b in writes:
            b.w = tok
            b.r = []
        for b in reads:
            if b not in writes:
                b.r.append(tok)
        return tok

    def mm(self, fns, reads=(), writes=()):
        toks = self._deps(reads, writes)
        toks = [t for t in toks if t is None or t[0] is not self.esem["tensor"]]
        w = self._waits("tensor", toks)
        n = len(fns)
        for i, fn in enumerate(fns):
            if i == n - 1:
                self.ecnt["tensor"] += 1
                tok = (self.esem["tensor"], self.ecnt["tensor"])
                self.q["tensor"].append((w if i == 0 else [], fn, (self.esem["tensor"], 1)))
            else:
                self.q["tensor"].append((w if i == 0 else [], fn, None))
        for b in writes:
            b.w = tok
            b.r = []
        for b in reads:
            if b not in writes:
                b.r.append(tok)
        return tok

    def dma(self, eng, out_ap, in_ap, src, dst, sembuf=None):
        sb = sembuf if sembuf is not None else dst
        if sb.sem is None:
            sb.sem = self.es.enter_context(self.nc.semaphore("d_" + sb.name))
            self.nsem += 1
        toks = self._deps([src], [dst])
        w = self._waits(eng, toks)
        sb.cnt += 16
        tok = (sb.sem, sb.cnt)
        self.q[eng].append((w, lambda e, o=out_ap, i=in_ap: e.dma_start(out=o, in_=i), (sb.sem, 16)))
        dst.w = tok
        dst.r = []
        src.r.append(tok)
        return tok

    def fence(self, eng, bufs):
        toks = []
        for b in bufs:
            toks.append(b.w)
            toks.extend(b.r)
        w = self._waits(eng, toks)
        if w:
            self.q[eng].append((w, None, None))

    def emit(self):
        nc = self.nc
        with nc.Block() as block:
            def run(name):
                def body(eng):
                    for (w, fn, inc) in self.q[name]:
                        for (s, v) in w:
                            eng.wait_ge(s, v)
                        if fn is None:
                            continue
                        ins = fn(eng)
                        if inc is not None:
                            ins.then_inc(inc[0], inc[1])
                return body
            block.sync(run("sync"))
            block.scalar(run("scalar"))
            block.vector(run("vector"))
            block.gpsimd(run("gpsimd"))
            block.tensor(run("tensor"))


class Ring:
    def __init__(self, nc, es, name, n, shape, dt):
        self.t = [es.enter_context(nc.sbuf_tensor(f"sb_{name}{i}", shape, dt)) for i in range(n)]
        self.b = [Buf(f"{name}{i}") for i in range(n)]
        self.i = 0
        self.n = n

    def next(self):
        k = self.i % self.n
        self.i += 1
        return self.t[k], self.b[k]


def build(stop_after=99, dbg=None):
    nc = bass.Bass("TRN2", target_bir_lowering=False)
    es = ExitStack()
    P = Prog(nc, es)

    def din(name, shape, dt=F32):
        return nc.dram_tensor(name, list(shape), dt, kind="ExternalInput").ap()

    def dscr(name, shape, dt):
        return nc.dram_tensor(name, list(shape), dt).ap()

    x_d = din("x", [T, D])
    c_d = din("c", [KD, 128])
    ident_d = din("ident", [128, 128])
    out_d = nc.dram_tensor("out", [T, D], F32, kind="ExternalOutput").ap()

    XT = dscr("XT", [D, T], F32)
    XTb = Buf("XT")
    HT = dscr("HT", [D, T], BF16)
    HTb = Buf("HT")

    def sb(name, shape, dt):
        return es.enter_context(nc.sbuf_tensor("sb_" + name, list(shape), dt))

    w_ada_d = din("w_ada", [2, D, 6 * D])
    vecA_d = din("vecA", [128, 128])
    vecB_d = din("vecB", [2, 96, 128])
    vecC_d = din("vecC", [128, 128])
    vecD_d = din("vecD", [128, 128])
    ffn_gu_d = din("ffn_w_gu", [2, D, 2 * FH])
    ffn_dn_d = din("ffn_w_down", [2, FH, D])
    wbuf_d = Buf("wdram")

    HID = dscr("HID", [FH, T], BF16)
    HIDb = Buf("HID")

    ident = sb("ident", [128, 128], F32)
    identb = Buf("ident")
    P.dma("sync", ident[:], ident_d, Buf("identd"), identb)
    ones_bf = sb("ones_bf", [128, 128], BF16)
    onesb = Buf("ones")
    P.op("vector", lambda e: e.memset(ones_bf[:], 1.0), writes=[onesb])

    psum = [es.enter_context(nc.psum_tensor(f"ps{i}", [128, 512], F32)) for i in range(8)]
    psb = [Buf(f"ps{i}") for i in range(8)]
    pctr = [0]

    def next_ps():
        k = pctr[0] % 7
        pctr[0] += 1
        return psum[k], psb[k]

    big = Ring(nc, es, "big", 2, [128, 2048], F32)
    xto = Ring(nc, es, "xto", 2, [128, KD, 128], F32)
    hbo = Ring(nc, es, "hbo", 2, [128, KD, 128], BF16)
    tmpf = Ring(nc, es, "tmpf", 4, [128, 512], F32)
    tmpb = Ring(nc, es, "tmpb", 3, [128, 512], BF16)
    wring = Ring(nc, es, "wr", 2, [128, KL, 512], BF16)
    act = sb("act", [128, KL, T], BF16)
    actb = Buf("act")

    colsA = sb("colsA", [128, 128], F32)
    colsB = sb("colsB", [128, 2, 96], F32)
    colsC = sb("colsC", [128, 128], F32)
    colsD = sb("colsD", [128, 128], F32)
    colsAb, colsBb, colsCb, colsDb = Buf("cA"), Buf("cB"), Buf("cC"), Buf("cD")
    vdb = Buf("vecd")

    def load_cols(src_ap, nrows, dst_ap, dstb):
        st, stb = big.next()
        P.dma("sync", st[:nrows, :128], src_ap, vdb, stb)
        ps, pb = next_ps()
        P.mm([lambda e: e.transpose(ps[:, :nrows], st[:nrows, :128], ident[:nrows, :nrows])],
             reads=[stb, identb], writes=[pb])
        P.op("vector", lambda e: e.tensor_copy(dst_ap, ps[:, :nrows]), reads=[pb], writes=[dstb])

    load_cols(vecA_d, 128, colsA[:, :], colsAb)
    load_cols(vecB_d[0], 96, colsB[:, 0, :], colsBb)
    load_cols(vecB_d[1], 96, colsB[:, 1, :], colsBb)
    load_cols(vecC_d, 128, colsC[:, :], colsCb)
    load_cols(vecD_d, 128, colsD[:, :], colsDb)
    condT = sb("condT", [128, KD], BF16)
    condb = Buf("cond")
    P.op("scalar", lambda e: e.activation(condT[:], colsA[:, 0:16], AF.Silu), reads=[colsAb], writes=[condb])

    mod = sb("mod", [128, 2, 96], F32)
    modb = Buf("mod")
    gs = sb("gs", [128, 2, 2, KD], F32)
    gsb = Buf("gs")
    ada_steps = []

    def ada_step(L, j):
        psm, pbm = psum[7], psb[7]
        wt, wb = wring.next()
        P.dma("gpsimd", wt[:, :KD, :],
              w_ada_d[L, :, j * 512:(j + 1) * 512].rearrange("(k p) n -> p k n", p=128), wbuf_d, wb)
        for m in range(4):
            col = j * 4 + m
            fns = []
            for k in range(KD):
                fns.append(lambda e, wt=wt, m=m, k=k, col=col, psm=psm: e.matmul(
                    psm[:, col:col + 1], lhsT=wt[:, k, m * 128:(m + 1) * 128], rhs=condT[:, k:k + 1],
                    start=(k == 0), stop=(k == KD - 1)))
            P.mm(fns, reads=[wb, condb], writes=[pbm])
        if j == 23:
            P.op("vector", lambda e, L=L, psm=psm: e.tensor_tensor(
                out=mod[:, L, :], in0=psm[:, 0:96], in1=colsB[:, L, :], op=ALU.add),
                reads=[pbm, colsBb], writes=[modb])
            for sub in range(2):
                sc0 = (sub * 3 + 1) * 16
                P.op("vector", lambda e, L=L, sub=sub, sc0=sc0: e.scalar_tensor_tensor(
                    out=gs[:, L, sub, :], in0=mod[:, L, sc0:sc0 + 16], scalar=1.0,
                    in1=colsA[:, 16 + (L * 2 + sub) * 16: 16 + (L * 2 + sub) * 16 + 16],
                    op0=ALU.add, op1=ALU.mult), reads=[modb, colsAb], writes=[gsb])

    for L in range(2):
        for j in range(24):
            ada_steps.append((L, j))

    def mcol(L, idx, k):
        return mod[:, L, idx * 16 + k: idx * 16 + k + 1]

    xdb = Buf("xd")
    XTv = XT.rearrange("(k p) t -> p k t", p=128)
    HTv = HT.rearrange("(k p) t -> p k t", p=128)
    def phase0_tile(tt):
        xt_, xb_ = big.next()
        P.dma("sync", xt_[:, :D], x_d[tt * 128:(tt + 1) * 128, :], xdb, xb_)
        ot, ob = xto.next()
        for k4 in range(KD // 4):
            ps, pb = next_ps()
            for j in range(4):
                k = k4 * 4 + j
                P.mm([lambda e, ps=ps, j=j, k=k, xt_=xt_: e.transpose(
                    ps[:, j * 128:(j + 1) * 128], xt_[:, k * 128:(k + 1) * 128], ident[:])],
                    reads=[xb_, identb], writes=[pb])
            if k4 % 2 == 0:
                P.op("vector", lambda e, ps=ps, ot=ot, k4=k4: e.tensor_copy(
                    ot[:, k4 * 4:(k4 + 1) * 4, :], ps[:].rearrange("p (j t) -> p j t", j=4)),
                    reads=[pb], writes=[ob])
            else:
                P.op("scalar", lambda e, ps=ps, ot=ot, k4=k4: e.copy(
                    ot[:, k4 * 4:(k4 + 1) * 4, :], ps[:].rearrange("p (j t) -> p j t", j=4)),
                    reads=[pb], writes=[ob])
        P.dma("sync", XTv[:, :, tt * 128:(tt + 1) * 128], ot[:], ob, XTb, sembuf=ob)

    for i, (L_, j_) in enumerate(ada_steps):
        ada_step(L_, j_)
        if i % 3 == 2:
            phase0_tile(i // 3)
    P.fence("sync", xto.b)

    def norm_phase(L, sub, to_act=False):
        P.fence("sync", [XTb])
        for q in range(T // 128):
            xt_, xb_ = big.next()
            xv = xt_[:].rearrange("p (k t) -> p k t", k=KD)
            P.dma("sync", xv, XTv[:, :, q * 128:(q + 1) * 128], XTb, xb_)
            ht, hb = hbo.next()
            P.op("scalar", lambda e, ht=ht, xv=xv: e.activation(ht[:], xv, AF.Square), reads=[xb_], writes=[hb])
            ps, pb = next_ps()
            fns = []
            for k in range(KD):
                fns.append(lambda e, ps=ps, ht=ht, k=k: e.matmul(
                    ps[:, :128], lhsT=ones_bf[:], rhs=ht[:, k, :], start=(k == 0), stop=(k == KD - 1)))
            P.mm(fns, reads=[hb, onesb], writes=[pb])
            rs, rb = tmpf.next()
            P.op("vector", lambda e, rs=rs, ps=ps: e.tensor_scalar(
                out=rs[:, :128], in0=ps[:, :128], scalar1=1.0 / D, scalar2=EPS, op0=ALU.mult, op1=ALU.add),
                reads=[pb], writes=[rb])
            P.op("scalar", lambda e, rs=rs: e.activation(rs[:, :128], rs[:, :128], AF.Sqrt),
                 reads=[rb], writes=[rb])
            P.op("vector", lambda e, rs=rs: e.reciprocal(rs[:, :128], rs[:, :128]),
                 reads=[rb], writes=[rb])
            P.op("vector", lambda e, xv=xv, rs=rs: e.tensor_tensor(
                out=xv, in0=xv, in1=rs[:, :128].unsqueeze(1).broadcast_to([128, KD, 128]), op=ALU.mult),
                reads=[xb_, rb], writes=[xb_])
            for k in range(KD):
                dst = act[:, k, q * 128:(q + 1) * 128] if to_act else ht[:, k, :]
                P.op("scalar", lambda e, xv=xv, k=k, dst=dst: e.activation(
                    dst, xv[:, k, :], AF.Identity, bias=mcol(L, sub * 3, k), scale=gs[:, L, sub, k:k + 1]),
                    reads=[xb_, modb, gsb], writes=[actb if to_act else hb])
            if not to_act:
                P.dma("sync", HTv[:, :, q * 128:(q + 1) * 128], ht[:], hb, HTb, sembuf=hb)
        P.fence("sync", hbo.b)

    def load_act(src_v, KT, srcb):
        P.fence("sync", [srcb])
        for k in range(KT):
            P.dma("sync", act[:, k, :], src_v[:, k, :], srcb, actb)

    def gemm(steps, KT, epilogue, krange=None):
        for si, wlist in enumerate(steps):
            wts = []
            for wap in wlist:
                wt, wb = wring.next()
                P.dma("gpsimd", wt[:, :KT, :], wap.rearrange("(k p) n -> p k n", p=128), wbuf_d, wb)
                wts.append((wt, wb))
            for m in range(4):
                for tb in range(NTB):
                    pss = []
                    for (wt, wb) in wts:
                        ps, pb = next_ps()
                        fns = []
                        k0, k1 = (0, KT - 1) if krange is None else krange(si, m)
                        for k in range(k0, k1 + 1):
                            fns.append(lambda e, ps=ps, wt=wt, m=m, k=k, tb=tb, k0=k0, k1=k1: e.matmul(
                                ps[:], lhsT=wt[:, k, m * 128:(m + 1) * 128],
                                rhs=act[:, k, tb * TB:(tb + 1) * TB], start=(k == k0), stop=(k == k1)))
                        P.mm(fns, reads=[wb, actb], writes=[pb])
                        pss.append((ps, pb))
                    epilogue(si, m, tb, pss)

    def ffn(L):
        norm_phase(L, 1, to_act=True)
        HIDv = HID.rearrange("(k p) t -> p k t", p=128)

        def ep_swiglu(si, m, tb, pss):
            (pg, pgb), (pu, pub) = pss
            sg, sgb = tmpf.next()
            P.op("scalar", lambda e: e.activation(sg[:], pg[:], AF.Silu), reads=[pgb], writes=[sgb])
            ho, hob = tmpb.next()
            P.op("vector", lambda e: e.tensor_tensor(out=ho[:], in0=pu[:], in1=sg[:], op=ALU.mult),
                 reads=[pub, sgb], writes=[hob])
            P.dma("sync", HIDv[:, si * 4 + m, tb * TB:(tb + 1) * TB], ho[:], hob, HIDb, sembuf=hob)

        steps = [[ffn_gu_d[L, :, j * 512:(j + 1) * 512], ffn_gu_d[L, :, FH + j * 512: FH + (j + 1) * 512]]
                 for j in range(FH // 512)]
        gemm(steps, KD, ep_swiglu)
        P.fence("sync", tmpb.b)

        def ep_res(si, m, tb, pss):
            (ps, pb), = pss
            kk = si * 4 + m
            xt_, xb_ = tmpf.next()
            P.dma("sync", xt_[:], XTv[:, kk, tb * TB:(tb + 1) * TB], XTb, xb_)
            P.op("vector", lambda e: e.scalar_tensor_tensor(
                out=xt_[:], in0=ps[:], scalar=mcol(L, 5, kk), in1=xt_[:], op0=ALU.mult, op1=ALU.add),
                reads=[pb, xb_, modb], writes=[xb_])
            P.dma("sync", XTv[:, kk, tb * TB:(tb + 1) * TB], xt_[:], xb_, XTb, sembuf=xb_)

        for half in range(2):
            load_act(HIDv[:, half * KL:(half + 1) * KL, :], KL, HIDb)
            steps = [[ffn_dn_d[L, half * LW:(half + 1) * LW, j * 512:(j + 1) * 512]] for j in range(D // 512)]
            gemm(steps, KL, ep_res)
            P.fence("sync", tmpf.b)


    lru_in_d = din("lru_w_in", [D, 2 * LW])
    lru_rg_d = din("lru_rg_dense", [LW, LW])
    lru_ig_d = din("lru_ig_dense", [LW, LW])
    lru_out_d = din("lru_w_out", [LW, D])
    GB = dscr("GB", [LW, T], BF16); GBb = Buf("GB")
    XB = dscr("XB", [LW, T], F32); XBb = Buf("XB")
    XC = dscr("XC", [LW, T], F32); XCb_ = Buf("XC")
    XCh = dscr("XCh", [LW, T], BF16); XChb = Buf("XCh")
    AA = dscr("AA", [LW, T], F32); AAb = Buf("AA")
    BT = dscr("BT", [LW, T], F32); BTb = Buf("BT")
    YL = dscr("YL", [LW, T], BF16); YLb = Buf("YL")
    GBv, XBv, XCv, XChv, AAv, BTv, YLv = [a.rearrange("(k p) t -> p k t", p=128) for a in (GB, XB, XC, XCh, AA, BT, YL)]

    def gelu_tile(src_ap, srcb, n, eng="vector"):
        t1, t1b = tmpf.next()
        P.op("scalar", lambda e: e.activation(t1[:, :n], src_ap, AF.Square), reads=[srcb], writes=[t1b])
        P.op(eng, lambda e: e.tensor_scalar(out=t1[:, :n], in0=t1[:, :n], scalar1=0.044715, scalar2=1.0,
                                            op0=ALU.mult, op1=ALU.add), reads=[t1b], writes=[t1b])
        P.op(eng, lambda e: e.tensor_tensor(out=t1[:, :n], in0=t1[:, :n], in1=src_ap, op=ALU.mult),
             reads=[t1b, srcb], writes=[t1b])
        P.op("scalar", lambda e: e.activation(t1[:, :n], t1[:, :n], AF.Sigmoid, scale=1.5957691216057308),
             reads=[t1b], writes=[t1b])
        ho, hob = tmpb.next()
        P.op(eng, lambda e: e.tensor_tensor(out=ho[:, :n], in0=t1[:, :n], in1=src_ap, op=ALU.mult),
             reads=[t1b, srcb], writes=[hob])
        return ho, hob

    def lru_mixer(L):
        norm_phase(L, 0, to_act=True)

        def ep_in(si, m, tb, pss):
            (ps, pb), = pss
            kk = si * 4 + m
            if kk < KL:
                ho, hob = gelu_tile(ps[:], pb, TB)
                P.dma("sync", GBv[:, kk, tb * TB:(tb + 1) * TB], ho[:], hob, GBb, sembuf=hob)
            else:
                xo, xob = tmpf.next()
                P.op("scalar", lambda e: e.copy(xo[:], ps[:]), reads=[pb], writes=[xob])
                P.dma("sync", XBv[:, kk - KL, tb * TB:(tb + 1) * TB], xo[:], xob, XBb, sembuf=xob)

        gemm([[lru_in_d[:, j * 512:(j + 1) * 512]] for j in range(2 * LW // 512)], KD, ep_in)
        P.fence("sync", tmpf.b + tmpb.b)

        spc = sb("spc", [128, KL], F32); spb = Buf("spc")
        ex = sb("spx", [128, KL], F32); exb = Buf("spx")
        P.op("scalar", lambda e: e.activation(ex[:], colsD[:, 44:66], AF.Exp, scale=-1.0), reads=[colsDb], writes=[exb])
        P.op("vector", lambda e: e.tensor_scalar(out=spc[:], in0=ex[:], scalar1=-0.25, scalar2=1.0 / 3.0,
                                                 op0=ALU.mult, op1=ALU.add), reads=[exb], writes=[spb])
        P.op("vector", lambda e: e.tensor_tensor(out=spc[:], in0=spc[:], in1=ex[:], op=ALU.mult), reads=[exb, spb], writes=[spb])
        P.op("vector", lambda e: e.tensor_scalar(out=spc[:], in0=spc[:], scalar1=-0.5, scalar2=None, op0=ALU.add),
             reads=[spb], writes=[spb])
        P.op("vector", lambda e: e.tensor_tensor(out=spc[:], in0=spc[:], in1=ex[:], op=ALU.mult), reads=[exb, spb], writes=[spb])
        P.op("vector", lambda e: e.tensor_scalar(out=spc[:], in0=spc[:], scalar1=1.0, scalar2=None, op0=ALU.add),
             reads=[spb], writes=[spb])
        P.op("vector", lambda e: e.tensor_tensor(out=spc[:], in0=spc[:], in1=ex[:], op=ALU.mult), reads=[exb, spb], writes=[spb])
        P.op("vector", lambda e: e.tensor_scalar(out=spc[:], in0=spc[:], scalar1=-8.0, scalar2=None, op0=ALU.mult),
             reads=[spb], writes=[spb])

        for kt in range(KL):
            xt_, xb_ = big.next()
            P.dma("sync", xt_[:, :T], XBv[:, kt, :], XBb, xb_)
            ot, ob = xto.next()
            ov = ot[:].rearrange("p k t -> p (k t)")
            P.op("vector", lambda e, xt_=xt_, ov=ov, kt=kt: e.tensor_scalar(
                out=ov, in0=xt_[:, :T], scalar1=colsC[:, 3 * KL + kt: 3 * KL + kt + 1],
                scalar2=colsC[:, 88 + kt: 88 + kt + 1], op0=ALU.mult, op1=ALU.add),
                reads=[xb_, colsCb], writes=[ob])
            for sh in (1, 2, 3):
                P.op("vector", lambda e, xt_=xt_, ov=ov, kt=kt, sh=sh: e.scalar_tensor_tensor(
                    out=ov[:, sh:], in0=xt_[:, :T - sh], scalar=colsC[:, (3 - sh) * KL + kt: (3 - sh) * KL + kt + 1],
                    in1=ov[:, sh:], op0=ALU.mult, op1=ALU.add), reads=[xb_, colsCb, ob], writes=[ob])
            hb16, hb16b = hbo.next()
            hv = hb16[:].rearrange("p k t -> p (k t)")
            P.op("scalar", lambda e, hv=hv, ov=ov: e.copy(hv, ov), reads=[ob], writes=[hb16b])
            P.dma("sync", XCv[:, kt, :], ov, ob, XCb_, sembuf=ob)
            P.dma("sync", XChv[:, kt, :], hv, hb16b, XChb, sembuf=hb16b)
        P.fence("sync", xto.b + hbo.b)

        load_act(XChv, KL, XChb)

        def kr(si, m):
            kk = si * 4 + m
            h0 = (128 * kk) // 176
            h1 = (128 * kk + 127) // 176
            return (176 * h0) // 128, min(KL - 1, (176 * h1 + 175) // 128)

        def ep_gate(si, m, tb, pss):
            (pr, prb), (pi, pib) = pss
            kk = si * 4 + m
            r_, rb_ = tmpf.next()
            P.op("scalar", lambda e: e.activation(r_[:], pr[:], AF.Sigmoid, bias=colsD[:, kk:kk + 1], scale=1.0),
                 reads=[prb, colsDb], writes=[rb_])
            i_, ib_ = tmpf.next()
            P.op("scalar", lambda e: e.activation(i_[:], pi[:], AF.Sigmoid, bias=colsD[:, 22 + kk:22 + kk + 1], scale=1.0),
                 reads=[pib, colsDb], writes=[ib_])
            xc_, xcb = tmpf.next()
            P.dma("sync", xc_[:], XCv[:, kk, tb * TB:(tb + 1) * TB], XCb_, xcb)
            P.op("scalar", lambda e: e.activation(r_[:], r_[:], AF.Exp, scale=spc[:, kk:kk + 1]),
                 reads=[rb_, spb], writes=[rb_])
            P.op("vector", lambda e: e.tensor_tensor(out=i_[:], in0=i_[:], in1=xc_[:], op=ALU.mult),
                 reads=[ib_, xcb], writes=[ib_])
            P.op("vector", lambda e: e.tensor_tensor(out=xc_[:], in0=r_[:], in1=r_[:], op=ALU.mult),
                 reads=[rb_, xcb], writes=[xcb])
            P.op("vector", lambda e: e.tensor_scalar(out=xc_[:], in0=xc_[:], scalar1=-1.0, scalar2=1.0,
                                                     op0=ALU.mult, op1=ALU.add), reads=[xcb], writes=[xcb])
            P.op("scalar", lambda e: e.activation(xc_[:], xc_[:], AF.Sqrt), reads=[xcb], writes=[xcb])
            P.op("vector", lambda e: e.tensor_tensor(out=i_[:], in0=i_[:], in1=xc_[:], op=ALU.mult),
                 reads=[ib_, xcb], writes=[ib_])
            P.dma("sync", AAv[:, kk, tb * TB:(tb + 1) * TB], r_[:], rb_, AAb, sembuf=rb_)
            P.dma("sync", BTv[:, kk, tb * TB:(tb + 1) * TB], i_[:], ib_, BTb, sembuf=ib_)

        steps = [[lru_rg_d[:, j * 512: min(LW, (j + 1) * 512)], lru_ig_d[:, j * 512: min(LW, (j + 1) * 512)]]
                 for j in range((LW + 511) // 512)]
        gemm_ragged(steps, KL, ep_gate, kr)
        P.fence("sync", tmpf.b)

        P.fence("sync", [AAb, BTb, GBb])
        for kt in range(KL):
            at, atb = big.next()
            P.dma("sync", at[:, :T], AAv[:, kt, :], AAb, atb)
            bt, btb = xto.next()
            bv = bt[:].rearrange("p k t -> p (k t)")
            P.dma("sync", bv, BTv[:, kt, :], BTb, btb)
            gt, gtb = hbo.next()
            gv = gt[:].rearrange("p k t -> p (k t)")
            P.dma("sync", gv, GBv[:, kt, :], GBb, gtb)
            P.op("vector", lambda e, at=at, bv=bv: e.tensor_tensor_scan(
                out=bv, data0=at[:, :T], data1=bv, initial=0.0, op0=ALU.mult, op1=ALU.add),
                reads=[atb, btb], writes=[btb])
            P.op("vector", lambda e, gv=gv, bv=bv: e.tensor_tensor(out=gv, in0=bv, in1=gv, op=ALU.mult),
                 reads=[btb, gtb], writes=[gtb])
            P.dma("sync", YLv[:, kt, :], gv, gtb, YLb, sembuf=gtb)
        P.fence("sync", hbo.b)

        load_act(YLv, KL, YLb)

        def ep_res1(si, m, tb, pss):
            (ps, pb), = pss
            kk = si * 4 + m
            xt_, xb_ = tmpf.next()
            P.dma("sync", xt_[:], XTv[:, kk, tb * TB:(tb + 1) * TB], XTb, xb_)
            P.op("vector", lambda e: e.scalar_tensor_tensor(
                out=xt_[:], in0=ps[:], scalar=mcol(L, 2, kk), in1=xt_[:], op0=ALU.mult, op1=ALU.add),
                reads=[pb, xb_, modb], writes=[xb_])
            P.dma("sync", XTv[:, kk, tb * TB:(tb + 1) * TB], xt_[:], xb_, XTb, sembuf=xb_)

        gemm([[lru_out_d[:, j * 512:(j + 1) * 512]] for j in range(D // 512)], KL, ep_res1)
        P.fence("sync", tmpf.b)

    def gemm_ragged(steps, KT, epilogue, krange):
        for si, wlist in enumerate(steps):
            wts = []
            ncol = wlist[0].shape[1]
            for wap in wlist:
                wt, wb = wring.next()
                P.dma("gpsimd", wt[:, :KT, :ncol], wap.rearrange("(k p) n -> p k n", p=128), wbuf_d, wb)
                wts.append((wt, wb))
            for m in range(ncol // 128):
                for tb in range(NTB):
                    pss = []
                    for (wt, wb) in wts:
                        ps, pb = next_ps()
                        fns = []
                        k0, k1 = krange(si, m)
                        for k in range(k0, k1 + 1):
                            fns.append(lambda e, ps=ps, wt=wt, m=m, k=k, tb=tb, k0=k0, k1=k1: e.matmul(
                                ps[:], lhsT=wt[:, k, m * 128:(m + 1) * 128],
                                rhs=act[:, k, tb * TB:(tb + 1) * TB], start=(k == k0), stop=(k == k1)))
                        P.mm(fns, reads=[wb, actb], writes=[pb])
                        pss.append((ps, pb))
                    epilogue(si, m, tb, pss)

    if dbg == "LRU":
        lru_mixer(1)

    PI = 3.141592653589793
    s5_in_d = din("s5_w_in", [D, D])
    s5_glu_d = din("s5_w_glu", [D, 2 * D])
    lamre_d = din("s5_lam_re", [128, 64]); lamim_d = din("s5_lam_im", [128, 64]); logdt_d = din("s5_log_dt", [128, 1])
    bre_d = din("s5_b_re", [128, 1024]); bim_d = din("s5_b_im", [128, 1024])
    cre_d = din("s5_c_re", [128, 1024]); cim_d = din("s5_c_im", [128, 1024])
    s5d_d = din("s5_d_row", [1, D])
    WST = dscr("WST", [128, 128, 128], BF16); WINT = dscr("WINT", [128, 128, 128], BF16)
    WOR = dscr("WOR", [128, 64, 128], BF16); WOI = dscr("WOI", [128, 64, 128], BF16)
    WSTb, WINTb, WORb, WOIb = Buf("WST"), Buf("WINT"), Buf("WOR"), Buf("WOI")
    YT = dscr("YT", [D, T], BF16); YTb = Buf("YT")
    YTv = YT.rearrange("(k p) t -> p k t", p=128)
    identh = sb("identh", [128, 128], BF16); identhb = Buf("identh")
    P.op("vector", lambda e: e.tensor_copy(identh[:], ident[:]), reads=[identb], writes=[identhb])
    cA4 = sb("cA4", [64, 2, 2, 128], F32); cAb = Buf("cA")
    cA1 = cA4[:, 0, :, :]; cA2 = cA4[:, 1, :, :]
    vpb = [Buf("vst0"), Buf("vst1")]
    dtab = sb("dtab", [128, D], F32); dtabb = Buf("dtab")

    def V(fn, r, w):
        return P.op("vector", fn, reads=r, writes=w)

    def A(fn, r, w):
        return P.op("scalar", fn, reads=r, writes=w)

    def tt(out, a, b, op):
        return lambda e: e.tensor_tensor(out=out, in0=a, in1=b, op=op)

    def s5_prep():
        nat = sb("s5nat", [128, 130], F32); natb = Buf("s5nat")
        P.dma("sync", nat[:, 0:64], lamre_d, vdb, natb)
        P.dma("sync", nat[:, 64:128], lamim_d, vdb, natb)
        P.dma("sync", nat[:, 128:129], logdt_d, vdb, natb)
        P.dma("sync", dtab[:], s5d_d.partition_broadcast(128), vdb, dtabb)
        Lr = sb("s5Lr", [128, 9, 64], F32); Li = sb("s5Li", [128, 9, 64], F32); Lb = Buf("s5L")
        sm = sb("s5sm", [128, 12, 64], F32); smb = Buf("s5sm")
        lr = nat[:, 0:64]; li = nat[:, 64:128]
        dtc = sm[:, 11, 0:1]
        A(lambda e: e.activation(dtc, nat[:, 128:129], AF.Exp), [natb], [smb])
        lrdt, lidt, mag, sa, ca, t1, t2, fr, fi, nr, den = [sm[:, i, :] for i in range(11)]
        V(lambda e: e.tensor_scalar(out=lrdt, in0=lr, scalar1=dtc, scalar2=None, op0=ALU.mult), [natb, smb], [smb])
        V(lambda e: e.tensor_scalar(out=lidt, in0=li, scalar1=dtc, scalar2=None, op0=ALU.mult), [natb, smb], [smb])
        A(lambda e: e.activation(mag, lrdt, AF.Exp), [smb], [smb])
        A(lambda e: e.activation(sa, lidt, AF.Sin, scale=1.0 / 16.0), [smb], [smb])
        V(lambda e: e.tensor_scalar(out=ca, in0=lidt, scalar1=1.0 / 16.0, scalar2=PI / 2, op0=ALU.mult, op1=ALU.add), [smb], [smb])
        A(lambda e: e.activation(ca, ca, AF.Sin), [smb], [smb])
        for _ in range(4):
            V(tt(t1, ca, ca, ALU.mult), [smb], [smb])
            V(tt(t2, sa, sa, ALU.mult), [smb], [smb])
            V(tt(sa, ca, sa, ALU.mult), [smb], [smb])
            V(lambda e: e.tensor_scalar(out=sa, in0=sa, scalar1=2.0, scalar2=None, op0=ALU.mult), [smb], [smb])
            V(tt(ca, t1, t2, ALU.subtract), [smb], [smb])
        V(lambda e: e.memset(Lr[:, 0, :], 1.0), [], [Lb])
        V(lambda e: e.memset(Li[:, 0, :], 0.0), [], [Lb])
        V(tt(Lr[:, 1, :], mag, ca, ALU.mult), [smb], [Lb])
        V(tt(Li[:, 1, :], mag, sa, ALU.mult), [smb], [Lb])
        for k in range(2, 9):
            V(tt(t1, Lr[:, k - 1, :], Lr[:, 1, :], ALU.mult), [Lb], [smb])
            V(tt(t2, Li[:, k - 1, :], Li[:, 1, :], ALU.mult), [Lb], [smb])
            V(tt(Lr[:, k, :], t1, t2, ALU.subtract), [smb], [Lb])
            V(tt(t1, Lr[:, k - 1, :], Li[:, 1, :], ALU.mult), [Lb], [smb])
            V(tt(t2, Li[:, k - 1, :], Lr[:, 1, :], ALU.mult), [Lb], [smb])
            V(tt(Li[:, k, :], t1, t2, ALU.add), [smb], [Lb])
        V(lambda e: e.tensor_scalar(out=nr, in0=Lr[:, 1, :], scalar1=-1.0, scalar2=None, op0=ALU.add), [Lb], [smb])
        V(tt(den, lr, lr, ALU.mult), [natb], [smb])
        V(tt(t1, li, li, ALU.mult), [natb], [smb])
        V(tt(den, den, t1, ALU.add), [smb], [smb])
        V(lambda e: e.reciprocal(den, den), [smb], [smb])
        V(tt(t1, nr, lr, ALU.mult), [smb, natb], [smb])
        V(tt(t2, Li[:, 1, :], li, ALU.mult), [Lb, natb], [smb])
        V(tt(fr, t1, t2, ALU.add), [smb], [smb])
        V(tt(fr, fr, den, ALU.mult), [smb], [smb])
        V(tt(t1, Li[:, 1, :], lr, ALU.mult), [Lb, natb], [smb])
        V(tt(t2, nr, li, ALU.mult), [smb, natb], [smb])
        V(tt(fi, t1, t2, ALU.subtract), [smb], [smb])
        V(tt(fi, fi, den, ALU.mult), [smb], [smb])
        for (src, dst0, dst1, neg) in ((Lr[:, 8, :], cA1[:, 0, :], cA1[:, 1, :], False), (Li[:, 8, :], cA2[:, 1, :], cA2[:, 0, :], True)):
            ps, pb = next_ps()
            P.mm([lambda e, ps=ps, src=src: e.transpose(ps[0:64, 0:128], src, ident[:])], reads=[Lb, identb], writes=[pb])
            V(lambda e, ps=ps, dst0=dst0: e.tensor_copy(dst0, ps[0:64, 0:128]), [pb], [cAb])
            if neg:
                V(lambda e, ps=ps, dst1=dst1: e.tensor_scalar(out=dst1, in0=ps[0:64, 0:128], scalar1=-1.0, scalar2=None, op0=ALU.mult), [pb], [cAb])
            else:
                V(lambda e, ps=ps, dst1=dst1: e.tensor_copy(dst1, ps[0:64, 0:128]), [pb], [cAb])
        s5_prep.sm = sm; s5_prep.smb = smb

        w0, w0b = wring.t[0], wring.b[0]
        w1, w1b = wring.t[1], wring.b[1]
        f32v = w1[:].rearrange("p k n -> p (k n)").bitcast(F32)
        bre, bim, bbre, bbim, tA = [f32v[:, i * 1024:(i + 1) * 1024] for i in range(5)]
        g0, g0b = big.t[0], big.b[0]
        g1_, g1b = big.t[1], big.b[1]
        cre, cim = g0[:, 0:1024], g0[:, 1024:2048]
        tB, tC = g1_[:, 0:1024], g1_[:, 1024:2048]
        w1aux = Buf("w1aux")
        P.dma("sync", bre, bre_d, vdb, w1b, sembuf=w1aux)
        P.dma("sync", bim, bim_d, vdb, w1b, sembuf=w1aux)
        P.dma("sync", cre, cre_d, vdb, g0b)
        P.dma("sync", cim, cim_d, vdb, g0b)
        actf = act[:].rearrange("p k t -> p (k t)")
        stage = actf[:, 0:16384].rearrange("p (g j) -> p g j", j=128)
        Mre = actf[:, 16384:24576].rearrange("p (q s c) -> p q s c", q=64, s=8)
        Mim = actf[:, 24576:32768].rearrange("p (q s c) -> p q s c", q=64, s=8)
        Nre = actf[:, 32768:40960].rearrange("p (t o q) -> p t o q", t=8, o=16)
        Nim = w0[:].rearrange("p k n -> p (k n)")[:, 0:8192].rearrange("p (t o q) -> p t o q", t=8, o=16)
        Krev = xto.t[0][:].rearrange("p k t -> p (k t)").bitcast(BF16)[:, 0:4096].rearrange("p (s o c) -> p s o c", s=16, o=16)
        KT = xto.t[1][:].rearrange("p k t -> p (k t)").bitcast(BF16)[:, 0:2048].rearrange("p (o s c) -> p o s c", o=16, s=8)
        krb, ktb = xto.b[0], xto.b[1]
        b3 = lambda a: a.rearrange("p (q c) -> p q c", c=16)
        bc3 = lambda a: a.unsqueeze(2).broadcast_to([128, 64, 16])
        V(tt(b3(tA), b3(bre), bc3(fr), ALU.mult), [w1b, smb], [w1b])
        V(tt(b3(tB), b3(bim), bc3(fi), ALU.mult), [w1b, smb], [g1b])
        V(tt(bbre, tA, tB, ALU.subtract), [w1b, g1b], [w1b])
        V(tt(b3(tA), b3(bim), bc3(fr), ALU.mult), [w1b, smb], [w1b])
        V(tt(b3(tB), b3(bre), bc3(fi), ALU.mult), [w1b, smb], [g1b])
        V(tt(bbim, tA, tB, ALU.add), [w1b, g1b], [w1b])
        for s_ in range(8):
            k = 7 - s_
            V(tt(b3(tA), b3(bbre), bc3(Lr[:, k, :]), ALU.mult), [w1b, Lb], [w1b])
            V(tt(b3(tB), b3(bbim), bc3(Li[:, k, :]), ALU.mult), [w1b, Lb], [g1b])
            V(tt(Mre[:, :, s_, :], b3(tA), b3(tB), ALU.subtract), [w1b, g1b], [actb])
            V(tt(b3(tA), b3(bbim), bc3(Lr[:, k, :]), ALU.mult), [w1b, Lb], [w1b])
            V(tt(b3(tB), b3(bbre), bc3(Li[:, k, :]), ALU.mult), [w1b, Lb], [g1b])
            V(tt(Mim[:, :, s_, :], b3(tA), b3(tB), ALU.add), [w1b, g1b], [actb])
        V(lambda e: e.memset(Krev, 0.0), [], [krb])
        c3 = lambda a: a.rearrange("p (o q) -> p o q", q=64)
        bo3 = lambda a: a.unsqueeze(1).broadcast_to([128, 16, 64])
        nre_f = tC.rearrange("p (o q) -> p o q", q=64)
        nim_f = f32v[:, 5120:5632]
        nimb = hbo.b[1]
        nim_f = hbo.t[1][:].rearrange("p k t -> p (k t)").bitcast(F32).rearrange("p (o q) -> p o q", q=64)
        kred = sb("kred", [128, 16], F32); kredb = Buf("kred")
        bbreT = bbre.rearrange("p (q c) -> p c q", c=16)
        bbimT = bbim.rearrange("p (q c) -> p c q", c=16)
        for k in range(9):
            V(tt(c3(tA), c3(cre), bo3(Lr[:, k, :]), ALU.mult), [g0b, Lb], [w1b])
            V(tt(c3(tB), c3(cim), bo3(Li[:, k, :]), ALU.mult), [g0b, Lb], [g1b])
            V(tt(nre_f, c3(tA), c3(tB), ALU.subtract), [w1b, g1b], [g1b])
            V(tt(c3(tA), c3(cre), bo3(Li[:, k, :]), ALU.mult), [g0b, Lb], [w1b])
            V(tt(c3(tB), c3(cim), bo3(Lr[:, k, :]), ALU.mult), [g0b, Lb], [g1b])
            V(tt(nim_f, c3(tA), c3(tB), ALU.add), [w1b, g1b], [nimb])
            if k >= 1:
                V(lambda e, k=k: e.tensor_copy(Nre[:, k - 1, :, :], nre_f), [g1b], [actb])
                V(lambda e, k=k: e.tensor_scalar(out=Nim[:, k - 1, :, :], in0=nim_f, scalar1=-1.0, scalar2=None, op0=ALU.mult), [nimb], [w0b])
            if k <= 7:
                for o in range(16):
                    V(tt(tA.rearrange("p (c q) -> p c q", q=64), bbreT, nre_f[:, o, :].unsqueeze(1).broadcast_to([128, 16, 64]), ALU.mult), [w1b, g1b], [w1b])
                    V(tt(tB.rearrange("p (c q) -> p c q", q=64), bbimT, nim_f[:, o, :].unsqueeze(1).broadcast_to([128, 16, 64]), ALU.mult), [w1b, nimb, g1b], [g1b])
                    V(tt(tA, tA, tB, ALU.subtract), [w1b, g1b], [w1b])
                    V(lambda e, k=k, o=o: e.tensor_reduce(out=kred[:], in_=tA.rearrange("p (c q) -> p c q", q=64),
                                                            axis=AX.X, op=ALU.add), [w1b], [kredb])
                    V(lambda e, k=k, o=o: e.tensor_copy(Krev[:, 7 - k, o, :], kred[:]), [kredb], [krb])

        def family(n_in, src_fn, src_bufs, dram, dramb, rows):
            for j4 in range(32):
                ps, pb = next_ps()
                psh = ps[:].bitcast(BF16)
                for q in range(4):
                    j = j4 * 4 + q
                    P.mm([lambda e, psh=psh, q=q, j=j: e.transpose(psh[0:n_in, q * 128:(q + 1) * 128], src_fn(j), identh[:])],
                         reads=src_bufs + [identhb], writes=[pb])
                cp = (lambda e, psh=psh, j4=j4: e.tensor_copy(stage[0:n_in, :, j4 * 4:(j4 + 1) * 4],
                                                              psh[0:n_in, 0:512].rearrange("p (j g) -> p g j", j=4)))
                if j4 % 2 == 0:
                    V(cp, [pb], [actb])
                else:
                    A(lambda e, psh=psh, j4=j4: e.copy(stage[0:n_in, :, j4 * 4:(j4 + 1) * 4],
                                                       psh[0:n_in, 0:512].rearrange("p (j g) -> p g j", j=4)), [pb], [actb])
            P.dma("sync", dram.rearrange("g r j -> r g j"), stage[0:rows, :, :], actb, dramb, sembuf=actb)

        family(128, lambda j: (Mre if j < 64 else Mim)[:, j % 64, :, :].rearrange("p s c -> p (s c)"), [actb], WST, WSTb, 128)
        family(64, lambda j: Nre[:, j // 16, j % 16, :], [actb], WOR, WORb, 64)
        family(64, lambda j: Nim[:, j // 16, j % 16, :], [w0b], WOI, WOIb, 64)
        for t_ in range(8):
            V(lambda e, t_=t_: e.tensor_copy(KT, Krev[:, 7 - t_: 15 - t_, :, :].rearrange("p s o c -> p o s c")), [krb], [ktb])
            for j4 in range(4):
                ps, pb = next_ps()
                psh = ps[:].bitcast(BF16)
                for q in range(4):
                    o = j4 * 4 + q
                    P.mm([lambda e, psh=psh, q=q, o=o: e.transpose(psh[:, q * 128:(q + 1) * 128],
                                                                   KT[:, o, :, :].rearrange("p s c -> p (s c)"), identh[:])],
                         reads=[ktb, identhb], writes=[pb])
                jb = t_ * 16 + j4 * 4
                V(lambda e, psh=psh, jb=jb: e.tensor_copy(stage[:, :, jb:jb + 4],
                                                          psh[:, 0:512].rearrange("p (j g) -> p g j", j=4)), [pb], [actb])
        P.dma("sync", WINT.rearrange("g r j -> r g j"), stage[:, :, :], actb, WINTb, sembuf=actb)
        P.fence("sync", [actb])

    def s5_main():
        norm_phase(0, 0)
        P.fence("sync", [HTb])
        actf = act[:].rearrange("p k t -> p (k t)")
        hsb = actf[:, 0:16384].rearrange("p (k t) -> p k t", k=KD)
        wsec = actf[:, 16384:32768].rearrange("p (w g j) -> p w g j", w=4, g=32)
        wsecb = Buf("wsec")
        sp_ = actf[0:64, 32768:40960].rearrange("p (r g c) -> p r g c", r=2, g=32)
        spb_ = Buf("sprev")
        xb_ = actf[:, 40960:45056].rearrange("p (g c) -> p g c", c=128)
        xbb = Buf("xblk")
        ub = xto.t[0][:].rearrange("p k t -> p (k t)").bitcast(BF16).rearrange("p (g s c) -> p g s c", g=32, s=8)
        ubb = xto.b[0]
        ym = xto.t[1][:].rearrange("p k t -> p (k t)").bitcast(BF16).rearrange("p (t c) -> p t c", c=512)
        ymb = xto.b[1]
        yos = [hbo.t[i][:].rearrange("p k t -> p (k t)").rearrange("p (a t) -> p a t", a=2) for i in range(2)]
        yobs = [hbo.b[0], hbo.b[1]]
        slr = [big.t[ri][:].bitcast(BF16)[0:64, :].rearrange("p (g c) -> p g c", c=128) for ri in range(2)]
        slbs = [big.b[0], big.b[1]]
        tsc = sb("tsc", [64, 2, 2, 32], F32)
        usc = sb("usc", [64, 2, 32], F32)
        tscb = Buf("tsc")
        uscb = Buf("usc")
        vst = s5_prep.sm[0:64, :, :].rearrange("p a b -> p (a b)").rearrange("p (j q s g) -> p j q s g", j=4, q=2, s=3)
        V(lambda e: e.memset(vst, 0.0), [], [s5_prep.smb, vpb[0], vpb[1]])
        for SBi in range(2):
            for k in range(KD):
                P.dma("sync", hsb[:, k, :], HTv[:, k, SBi * 1024:(SBi + 1) * 1024], HTb, actb)
            for j in range(4):
                gb = j * 32
                wt, wb = wring.next()
                P.dma("gpsimd", wt[:, :KD, :], s5_in_d[:, j * 512:(j + 1) * 512].rearrange("(k p) n -> p k n", p=128), wbuf_d, wb)
                P.dma("sync", wsec[:, 0, :, :], WST[gb:gb + 32].rearrange("g r j -> r g j"), WSTb, wsecb)
                P.dma("sync", wsec[:, 1, :, :], WINT[gb:gb + 32].rearrange("g r j -> r g j"), WINTb, wsecb)
                P.dma("sync", wsec[0:64, 2, :, :], WOR[gb:gb + 32].rearrange("g r j -> r g j"), WORb, wsecb)
                P.dma("sync", wsec[0:64, 3, :, :], WOI[gb:gb + 32].rearrange("g r j -> r g j"), WOIb, wsecb)
                for s_ in range(8):
                    ps, pb = next_ps()
                    fns = []
                    for k in range(KD):
                        fns.append(lambda e, ps=ps, k=k, s_=s_, wt=wt: e.matmul(
                            ps[:], lhsT=hsb[:, k, s_:1024:8], rhs=wt[:, k, :],
                            start=(k == 0), stop=(k == KD - 1)))
                    P.mm(fns, reads=[actb, wb], writes=[pb])
                    src = ps[:].rearrange("p (g c) -> p g c", c=16)
                    A(lambda e, s_=s_, src=src: e.copy(ub[:, :, s_, :], src), [pb], [ubb])
                for g4 in range(8):
                    ps, pb = next_ps()
                    psh = ps[:].bitcast(BF16)
                    for q in range(4):
                        gl = g4 * 4 + q
                        P.mm([lambda e, psh=psh, q=q, gl=gl: e.transpose(
                            psh[:, q * 128:(q + 1) * 128],
                            ub[:, gl, :, :].rearrange("p s c -> p (s c)"), identh[:])],
                            reads=[ubb, identhb], writes=[pb])
                    A(lambda e, psh=psh, g4=g4: e.copy(
                        xb_[:, g4 * 4:(g4 + 1) * 4, :], psh[:, 0:512].rearrange("p (g c) -> p g c", g=4)), [pb], [xbb])
                for g2 in range(16):
                    ps, pb = next_ps()
                    for q in range(2):
                        gl = g2 * 2 + q
                        for ri in range(2):
                            P.mm([lambda e, ps=ps, q=q, ri=ri, gl=gl: e.matmul(
                                ps[0:64, (q * 2 + ri) * 128:(q * 2 + ri + 1) * 128],
                                lhsT=wsec[:, 0, gl, ri * 64:(ri + 1) * 64], rhs=xb_[:, gl, :], start=True, stop=True)],
                                reads=[wsecb, xbb], writes=[pb])
                    for ri in range(2):
                        A(lambda e, ps=ps, g2=g2, ri=ri: e.copy(
                            slr[ri][:, g2 * 2:(g2 + 1) * 2, :],
                            ps[0:64, :].rearrange("p (q r c) -> p r q c", q=2, r=2)[:, ri, :, :]), [pb], [slbs[ri]])
                c4 = cA4[:, :, :, gb:gb + 32]
                for c_ in range(128):
                    rd, wr = c_ % 2, 1 - (c_ % 2)
                    v3 = vst[:, j, rd, :, :]
                    w3 = vst[:, j, wr, :, :]
                    win = bass.AP(v3.tensor, v3.offset, [list(v3.ap[0]), [32, 2], [32, 2], [1, 32]])
                    w02 = bass.AP(w3.tensor, w3.offset, [list(w3.ap[0]), [64, 2], [1, 32]])
                    V(tt(tsc[:], c4, win, ALU.mult), [cAb, vpb[rd]], [tscb])
                    V(tt(usc[:], tsc[:, 0, :, :], tsc[:, 1, :, :], ALU.add), [tscb], [uscb])
                    A(lambda e, c_=c_, v3=v3: e.copy(sp_[:, :, :, c_], v3[:, 0:2, :]), [vpb[rd]], [spb_])
                    V(tt(w02, usc[:, 0, :].unsqueeze(1).broadcast_to([64, 2, 32]),
                         slr[0][:, :, c_].unsqueeze(1).broadcast_to([64, 2, 32]), ALU.add), [uscb, slbs[0]], [vpb[wr]])
                    V(tt(w3[:, 1, :], usc[:, 1, :], slr[1][:, :, c_], ALU.add), [uscb, slbs[1]], [vpb[wr]])
                for g4 in range(8):
                    ps, pb = next_ps()
                    for q in range(4):
                        gl = g4 * 4 + q
                        P.mm([lambda e, ps=ps, q=q, gl=gl: e.matmul(
                                ps[:, q * 128:(q + 1) * 128], lhsT=xb_[:, gl, :], rhs=wsec[:, 1, gl, :], start=True, stop=False),
                              lambda e, ps=ps, q=q, gl=gl: e.matmul(
                                ps[:, q * 128:(q + 1) * 128], lhsT=sp_[:, 0, gl, :], rhs=wsec[0:64, 2, gl, :], start=False, stop=False),
                              lambda e, ps=ps, q=q, gl=gl: e.matmul(
                                ps[:, q * 128:(q + 1) * 128], lhsT=sp_[:, 1, gl, :], rhs=wsec[0:64, 3, gl, :], start=False, stop=True)],
                             reads=[xbb, wsecb, spb_], writes=[pb])
                    yf_, yfb = tmpf.next()
                    yf = yf_[:].rearrange("p (g t c) -> p g t c", g=4, t=8)
                    ch0 = (gb + g4 * 4) * 16
                    P.op("gpsimd", tt(yf, ub[:, g4 * 4: g4 * 4 + 4, :, :],
                         dtab[:, ch0:ch0 + 64].rearrange("p (g c) -> p g c", c=16).unsqueeze(2).broadcast_to([128, 4, 8, 16]),
                         ALU.mult), reads=[ubb, dtabb], writes=[yfb])
                    V(tt(yf, yf, ps[:].rearrange("p (g t c) -> p g t c", g=4, t=8), ALU.add), [yfb, pb], [yfb])
                    ho, hob = gelu_tile(yf_[:], yfb, 512, eng="gpsimd")
                    P.op("gpsimd", lambda e, ho=ho, g4=g4: e.tensor_copy(
                        ym[:, :, g4 * 64:(g4 + 1) * 64].rearrange("p t (g c) -> p g t c", c=16),
                        ho[:, :512].rearrange("p (g t c) -> p g t c", g=4, t=8)), reads=[hob], writes=[ymb])
                for t4 in range(8):
                    ps, pb = next_ps()
                    psh = ps[:].bitcast(BF16)
                    for q in range(4):
                        idx = t4 * 4 + q
                        t_, ct = idx // 4, idx % 4
                        P.mm([lambda e, psh=psh, q=q, t_=t_, ct=ct: e.transpose(
                            psh[:, q * 128:(q + 1) * 128], ym[:, t_, ct * 128:(ct + 1) * 128], identh[:])],
                            reads=[ymb, identhb], writes=[pb])
                    for q in range(4):
                        idx = t4 * 4 + q
                        t_, ct = idx // 4, idx % 4
                        A(lambda e, psh=psh, q=q, t_=t_, ct=ct: e.copy(
                            yos[ct // 2][:, ct % 2, t_:1024:8], psh[:, q * 128:(q + 1) * 128]), [pb], [yobs[ct // 2]])
                kt0 = gb // 8
                for hh in range(2):
                    P.dma("sync", YTv[:, kt0 + 2 * hh: kt0 + 2 * hh + 2, SBi * 1024:(SBi + 1) * 1024], yos[hh], yobs[hh], YTb, sembuf=yobs[hh])
        P.fence("sync", hbo.b)
        load_act(YTv, KD, YTb)

        def ep_glu(si, m, tb, pss):
            (pv, pvb), (pg, pgb) = pss
            kk = si * 4 + m
            sg, sgb = tmpf.next()
            A(lambda e: e.activation(sg[:], pg[:], AF.Sigmoid), [pgb], [sgb])
            V(tt(sg[:], sg[:], pv[:], ALU.mult), [sgb, pvb], [sgb])
            xt_, xb2 = tmpf.next()
            P.dma("sync", xt_[:], XTv[:, kk, tb * TB:(tb + 1) * TB], XTb, xb2)
            V(lambda e: e.scalar_tensor_tensor(out=xt_[:], in0=sg[:], scalar=mcol(0, 2, kk), in1=xt_[:],
                                               op0=ALU.mult, op1=ALU.add), [sgb, xb2, modb], [xb2])
            P.dma("sync", XTv[:, kk, tb * TB:(tb + 1) * TB], xt_[:], xb2, XTb, sembuf=xb2)

        gemm([[s5_glu_d[:, jj * 512:(jj + 1) * 512], s5_glu_d[:, D + jj * 512: D + (jj + 1) * 512]] for jj in range(4)], KD, ep_glu)
        P.fence("sync", tmpf.b)

    def final_phase():
        P.fence("sync", [XTb])
        for q in range(T // 128):
            xt_, xb_ = big.next()
            xv = xt_[:].rearrange("p (k t) -> p k t", k=KD)
            P.dma("sync", xv, XTv[:, :, q * 128:(q + 1) * 128], XTb, xb_)
            ht, hb = hbo.next()
            A(lambda e, ht=ht, xv=xv: e.activation(ht[:], xv, AF.Square), [xb_], [hb])
            ps, pb = next_ps()
            fns = []
            for k in range(KD):
                fns.append(lambda e, ps=ps, ht=ht, k=k: e.matmul(
                    ps[:, :128], lhsT=ones_bf[:], rhs=ht[:, k, :], start=(k == 0), stop=(k == KD - 1)))
            P.mm(fns, reads=[hb, onesb], writes=[pb])
            rs, rb = tmpf.next()
            V(lambda e, rs=rs, ps=ps: e.tensor_scalar(out=rs[:, :128], in0=ps[:, :128], scalar1=1.0 / D, scalar2=EPS,
                                                      op0=ALU.mult, op1=ALU.add), [pb], [rb])
            A(lambda e, rs=rs: e.activation(rs[:, :128], rs[:, :128], AF.Sqrt), [rb], [rb])
            V(lambda e, rs=rs: e.reciprocal(rs[:, :128], rs[:, :128]), [rb], [rb])
            for k in range(KD):
                V(lambda e, xv=xv, k=k, rs=rs: e.scalar_tensor_tensor(
                    out=xv[:, k, :], in0=xv[:, k, :], scalar=colsA[:, 80 + k:81 + k], in1=rs[:, :128],
                    op0=ALU.mult, op1=ALU.mult), [xb_, rb, colsAb], [xb_])
            ot, ob = xto.next()
            otv = ot[:].rearrange("p k t -> p (k t)")
            for k4 in range(4):
                ps2, pb2 = next_ps()
                for jq in range(4):
                    k = k4 * 4 + jq
                    P.mm([lambda e, ps2=ps2, jq=jq, k=k, xv=xv: e.transpose(
                        ps2[:, jq * 128:(jq + 1) * 128], xv[:, k, :], ident[:])], reads=[xb_, identb], writes=[pb2])
                if k4 % 2 == 0:
                    V(lambda e, ps2=ps2, k4=k4, otv=otv: e.tensor_copy(otv[:, k4 * 512:(k4 + 1) * 512], ps2[:]), [pb2], [ob])
                else:
                    A(lambda e, ps2=ps2, k4=k4, otv=otv: e.copy(otv[:, k4 * 512:(k4 + 1) * 512], ps2[:]), [pb2], [ob])
            P.dma("sync", out_d[q * 128:(q + 1) * 128, :], otv, ob, Buf("outd"), sembuf=ob)
        P.fence("sync", xto.b)

    if dbg == "S5":
        s5_prep()
        s5_main()
    snaps = []

    def snap(name):
        d_ = nc.dram_tensor(name, [D, T], F32, kind="ExternalOutput").ap()
        P.fence("sync", [XTb] + tmpf.b)
        b_ = Buf(name)
        P.dma("sync", d_, XT, XTb, b_)
        snaps.append(b_)

    if dbg is None or dbg == "ALL":
        s5_prep()
        s5_main()
        if dbg: snap("dbg0")
        ffn(0)
        if dbg: snap("dbg1")
        lru_mixer(1)
        if dbg: snap("dbg2")
        ffn(1)
        if dbg: snap("dbg3")
        final_phase()
        P.fence("sync", snaps)

    if dbg == "FFN0":
        ffn(0)

    if dbg in ("XT", "FFN0", "LRU", "S5"):
        dbg_d = nc.dram_tensor("dbg", [D, T], F32, kind="ExternalOutput").ap()
        P.fence("sync", [XTb] + tmpf.b)
        dbgb = Buf("dbgb")
        P.dma("sync", dbg_d, XT, XTb, dbgb)
        P.fence("sync", [dbgb])
    P.emit()
    return nc, es


def kernel(**inputs):
    dbg = os.environ.get("KDBG")
    nc, es = build(dbg=dbg)
    f = lambda a: np.ascontiguousarray(a, dtype=np.float32)
    x = f(inputs["x"]); c = f(inputs["c"])
    ng = f(inputs["norm_g"]).reshape(64, 128)
    fg = f(inputs["final_g"]).reshape(16, 128)
    sd = f(inputs["s5_d"]).reshape(16, 128)
    vecB = f(inputs["b_ada"]).reshape(2, 96, 128)
    vecC = np.zeros((128, 128), np.float32)
    vecC[0:88] = f(inputs["lru_conv_w"]).reshape(88, 128)
    vecC[88:110] = f(inputs["lru_conv_b"]).reshape(22, 128)
    vecD = np.zeros((128, 128), np.float32)
    vecD[0:22] = f(inputs["lru_b_rg"]).reshape(22, 128)
    vecD[22:44] = f(inputs["lru_b_ig"]).reshape(22, 128)
    vecD[44:66] = f(inputs["lru_lam"]).reshape(22, 128)
    def dense_bd(w):
        w = f(w)[0]
        o = np.zeros((LW, LW), np.float32)
        for h in range(16):
            o[h * 176:(h + 1) * 176, h * 176:(h + 1) * 176] = w[h]
        return o
    rgd = dense_bd(inputs["lru_w_rg"]); igd = dense_bd(inputs["lru_w_ig"])
    in_maps = []
    for b in range(NCORES):
        vecA = np.zeros((128, 128), np.float32)
        vecA[0:16] = c[b].reshape(16, 128)
        vecA[16:80] = ng
        vecA[80:96] = fg
        vecA[96:112] = sd
        in_maps.append({"x": x[b], "c": c[b].reshape(KD, 128), "ident": np.eye(128, dtype=np.float32),
                        "w_ada": f(inputs["w_ada"]), "vecA": vecA, "vecB": vecB, "vecC": vecC, "vecD": vecD,
                        "ffn_w_gu": f(inputs["ffn_w_gu"]), "ffn_w_down": f(inputs["ffn_w_down"]),
                        "lru_w_in": f(inputs["lru_w_in"])[0], "lru_rg_dense": rgd, "lru_ig_dense": igd,
                        "lru_w_out": f(inputs["lru_w_out"])[0],
                        "s5_w_in": f(inputs["s5_w_in"])[0], "s5_w_glu": f(inputs["s5_w_glu"])[0],
                        "s5_lam_re": f(inputs["s5_lam_re"])[0], "s5_lam_im": f(inputs["s5_lam_im"])[0],
                        "s5_log_dt": f(inputs["s5_log_dt"]).reshape(128, 1),
                        "s5_b_re": f(inputs["s5_b_re"]).reshape(128, 1024), "s5_b_im": f(inputs["s5_b_im"]).reshape(128, 1024),
                        "s5_c_re": f(inputs["s5_c_re"]).reshape(128, 1024), "s5_c_im": f(inputs["s5_c_im"]).reshape(128, 1024),
                        "s5_d_row": f(inputs["s5_d"]).reshape(1, D)})
    res = run_bass_kernel_spmd(nc, in_maps, core_ids=list(range(NCORES)))
    es.close()
    if dbg == "ALL":
        return [{k: r[k] for k in ("dbg0", "dbg1", "dbg2", "dbg3", "out")} for r in res.results]
    if dbg:
        return [r["dbg"] for r in res.results]
    return np.stack([r["out"] for r in res.results], axis=0)
```

```python
import os
from contextlib import ExitStack
import numpy as np
import concourse.bass as bass
import concourse.mybir as mybir
from concourse.bass_utils import run_bass_kernel_spmd

F32 = mybir.dt.float32
BF16 = mybir.dt.bfloat16
AF = mybir.ActivationFunctionType
ALU = mybir.AluOpType
AX = mybir.AxisListType

D = 2048
T = 2048
TB = 512
NTB = T // TB
KD = D // 128
FH = 5632
KF = FH // 128
LW = 2816
KL = LW // 128
EPS = 1e-6
NCORES = 4


class Buf:
    def __init__(self, name):
        self.name = name
        self.w = None
        self.r = []
        self.sem = None
        self.cnt = 0


class Prog:
    ENGS = ["sync", "scalar", "vector", "gpsimd", "tensor"]

    def __init__(self, nc, es):
        self.nc = nc
        self.es = es
        self.q = {e: [] for e in self.ENGS}
        self.esem = {}
        self.ecnt = {}
        for e in ["scalar", "vector", "gpsimd", "tensor"]:
            self.esem[e] = es.enter_context(nc.semaphore("es_" + e))
            self.ecnt[e] = 0
        self.known = {e: {} for e in self.ENGS}
        self.nsem = 4
        self.semobj = {}

    def _waits(self, eng, toks):
        need = {}
        for t in toks:
            if t is None:
                continue
            s, v = t
            if self.known[eng].get(id(s), 0) >= v:
                continue
            if need.get(id(s), (s, 0))[1] < v:
                need[id(s)] = (s, v)
        out = []
        for k, (s, v) in need.items():
            self.known[eng][k] = v
            out.append((s, v))
        return out

    def _deps(self, reads, writes):
        toks = []
        for b in reads:
            toks.append(b.w)
        for b in writes:
            toks.append(b.w)
            toks.extend(b.r)
        return toks

    def op(self, eng, fn, reads=(), writes=(), pe_same_ok=False):
        toks = self._deps(reads, writes)
        if pe_same_ok:
            toks = [t for t in toks if t is None or t[0] is not self.esem["tensor"]]
        w = self._waits(eng, toks)
        self.ecnt[eng] += 1
        tok = (self.esem[eng], self.ecnt[eng])
        self.q[eng].append((w, fn, (self.esem[eng], 1)))
        for b in writes:
            b.w = tok
            b.r = []
        for b in reads:
            if b not in writes:
                b.r.append(tok)
        return tok

    def mm(self, fns, reads=(), writes=()):
        toks = self._deps(reads, writes)
        toks = [t for t in toks if t is None or t[0] is not self.esem["tensor"]]
        w = self._waits("tensor", toks)
        n = len(fns)
        for i, fn in enumerate(fns):
            if i == n - 1:
                self.ecnt["tensor"] += 1
                tok = (self.esem["tensor"], self.ecnt["tensor"])
                self.q["tensor"].append((w if i == 0 else [], fn, (self.esem["tensor"], 1)))
            else:
                self.q["tensor"].append((w if i == 0 else [], fn, None))
        for b in writes:
            b.w = tok
            b.r = []
        for b in reads:
            if b not in writes:
                b.r.append(tok)
        return tok

    def dma(self, eng, out_ap, in_ap, src, dst, sembuf=None):
        sb = sembuf if sembuf is not None else dst
        if sb.sem is None:
            sb.sem = self.es.enter_context(self.nc.semaphore("d_" + sb.name))
            self.nsem += 1
        toks = self._deps([src], [dst])
        w = self._waits(eng, toks)
        sb.cnt += 16
        tok = (sb.sem, sb.cnt)
        self.q[eng].append((w, lambda e, o=out_ap, i=in_ap: e.dma_start(out=o, in_=i), (sb.sem, 16)))
        dst.w = tok
        dst.r = []
        src.r.append(tok)
        return tok

    def fence(self, eng, bufs):
        toks = []
        for b in bufs:
            toks.append(b.w)
            toks.extend(b.r)
        w = self._waits(eng, toks)
        if w:
            self.q[eng].append((w, None, None))

    def emit(self):
        nc = self.nc
        with nc.Block() as block:
            def run(name):
                def body(eng):
                    for (w, fn, inc) in self.q[name]:
                        for (s, v) in w:
                            eng.wait_ge(s, v)
                        if fn is None:
                            continue
                        ins = fn(eng)
                        if inc is not None:
                            ins.then_inc(inc[0], inc[1])
                return body
            block.sync(run("sync"))
            block.scalar(run("scalar"))
            block.vector(run("vector"))
            block.gpsimd(run("gpsimd"))
            block.tensor(run("tensor"))


class Ring:
    def __init__(self, nc, es, name, n, shape, dt):
        self.t = [es.enter_context(nc.sbuf_tensor(f"sb_{name}{i}", shape, dt)) for i in range(n)]
        self.b = [Buf(f"{name}{i}") for i in range(n)]
        self.i = 0
        self.n = n

    def next(self):
        k = self.i % self.n
        self.i += 1
        return self.t[k], self.b[k]


def build(stop_after=99, dbg=None):
    nc = bass.Bass("TRN2", target_bir_lowering=False)
    es = ExitStack()
    P = Prog(nc, es)

    def din(name, shape, dt=F32):
        return nc.dram_tensor(name, list(shape), dt, kind="ExternalInput").ap()

    def dscr(name, shape, dt):
        return nc.dram_tensor(name, list(shape), dt).ap()

    x_d = din("x", [T, D])
    c_d = din("c", [KD, 128])
    ident_d = din("ident", [128, 128])
    out_d = nc.dram_tensor("out", [T, D], F32, kind="ExternalOutput").ap()

    XT = dscr("XT", [D, T], F32)
    XTb = Buf("XT")
    HT = dscr("HT", [D, T], BF16)
    HTb = Buf("HT")

    def sb(name, shape, dt):
        return es.enter_context(nc.sbuf_tensor("sb_" + name, list(shape), dt))

    w_ada_d = din("w_ada", [2, D, 6 * D])
    vecA_d = din("vecA", [128, 128])
    vecB_d = din("vecB", [2, 96, 128])
    vecC_d = din("vecC", [128, 128])
    vecD_d = din("vecD", [128, 128])
    ffn_gu_d = din("ffn_w_gu", [2, D, 2 * FH])
    ffn_dn_d = din("ffn_w_down", [2, FH, D])
    wbuf_d = Buf("wdram")

    HID = dscr("HID", [FH, T], BF16)
    HIDb = Buf("HID")

    ident = sb("ident", [128, 128], F32)
    identb = Buf("ident")
    P.dma("sync", ident[:], ident_d, Buf("identd"), identb)
    ones_bf = sb("ones_bf", [128, 128], BF16)
    onesb = Buf("ones")
    P.op("vector", lambda e: e.memset(ones_bf[:], 1.0), writes=[onesb])

    psum = [es.enter_context(nc.psum_tensor(f"ps{i}", [128, 512], F32)) for i in range(8)]
    psb = [Buf(f"ps{i}") for i in range(8)]
    pctr = [0]

    def next_ps():
        k = pctr[0] % 7
        pctr[0] += 1
        return psum[k], psb[k]

    big = Ring(nc, es, "big", 2, [128, 2048], F32)
    xto = Ring(nc, es, "xto", 2, [128, KD, 128], F32)
    hbo = Ring(nc, es, "hbo", 2, [128, KD, 128], BF16)
    tmpf = Ring(nc, es, "tmpf", 4, [128, 512], F32)
    tmpb = Ring(nc, es, "tmpb", 3, [128, 512], BF16)
    wring = Ring(nc, es, "wr", 2, [128, KL, 512], BF16)
    act = sb("act", [128, KL, T], BF16)
    actb = Buf("act")

    colsA = sb("colsA", [128, 128], F32)
    colsB = sb("colsB", [128, 2, 96], F32)
    colsC = sb("colsC", [128, 128], F32)
    colsD = sb("colsD", [128, 128], F32)
    colsAb, colsBb, colsCb, colsDb = Buf("cA"), Buf("cB"), Buf("cC"), Buf("cD")
    vdb = Buf("vecd")

    def load_cols(src_ap, nrows, dst_ap, dstb):
        st, stb = big.next()
        P.dma("sync", st[:nrows, :128], src_ap, vdb, stb)
        ps, pb = next_ps()
        P.mm([lambda e: e.transpose(ps[:, :nrows], st[:nrows, :128], ident[:nrows, :nrows])],
             reads=[stb, identb], writes=[pb])
        P.op("vector", lambda e: e.tensor_copy(dst_ap, ps[:, :nrows]), reads=[pb], writes=[dstb])

    load_cols(vecA_d, 128, colsA[:, :], colsAb)
    load_cols(vecB_d[0], 96, colsB[:, 0, :], colsBb)
    load_cols(vecB_d[1], 96, colsB[:, 1, :], colsBb)
    load_cols(vecC_d, 128, colsC[:, :], colsCb)
    load_cols(vecD_d, 128, colsD[:, :], colsDb)
    condT = sb("condT", [128, KD], BF16)
    condb = Buf("cond")
    P.op("scalar", lambda e: e.activation(condT[:], colsA[:, 0:16], AF.Silu), reads=[colsAb], writes=[condb])

    mod = sb("mod", [128, 2, 96], F32)
    modb = Buf("mod")
    gs = sb("gs", [128, 2, 2, KD], F32)
    gsb = Buf("gs")
    ada_steps = []

    def ada_step(L, j):
        psm, pbm = psum[7], psb[7]
        wt, wb = wring.next()
        P.dma("gpsimd", wt[:, :KD, :],
              w_ada_d[L, :, j * 512:(j + 1) * 512].rearrange("(k p) n -> p k n", p=128), wbuf_d, wb)
        for m in range(4):
            col = j * 4 + m
            fns = []
            for k in range(KD):
                fns.append(lambda e, wt=wt, m=m, k=k, col=col, psm=psm: e.matmul(
                    psm[:, col:col + 1], lhsT=wt[:, k, m * 128:(m + 1) * 128], rhs=condT[:, k:k + 1],
                    start=(k == 0), stop=(k == KD - 1)))
            P.mm(fns, reads=[wb, condb], writes=[pbm])
        if j == 23:
            P.op("vector", lambda e, L=L, psm=psm: e.tensor_tensor(
                out=mod[:, L, :], in0=psm[:, 0:96], in1=colsB[:, L, :], op=ALU.add),
                reads=[pbm, colsBb], writes=[modb])
            for sub in range(2):
                sc0 = (sub * 3 + 1) * 16
                P.op("vector", lambda e, L=L, sub=sub, sc0=sc0: e.scalar_tensor_tensor(
                    out=gs[:, L, sub, :], in0=mod[:, L, sc0:sc0 + 16], scalar=1.0,
                    in1=colsA[:, 16 + (L * 2 + sub) * 16: 16 + (L * 2 + sub) * 16 + 16],
                    op0=ALU.add, op1=ALU.mult), reads=[modb, colsAb], writes=[gsb])

    for L in range(2):
        for j in range(24):
            ada_steps.append((L, j))

    def mcol(L, idx, k):
        return mod[:, L, idx * 16 + k: idx * 16 + k + 1]

    xdb = Buf("xd")
    XTv = XT.rearrange("(k p) t -> p k t", p=128)
    HTv = HT.rearrange("(k p) t -> p k t", p=128)
    def phase0_tile(tt):
        xt_, xb_ = big.next()
        P.dma("sync", xt_[:, :D], x_d[tt * 128:(tt + 1) * 128, :], xdb, xb_)
        ot, ob = xto.next()
        for k4 in range(KD // 4):
            ps, pb = next_ps()
            for j in range(4):
                k = k4 * 4 + j
                P.mm([lambda e, ps=ps, j=j, k=k, xt_=xt_: e.transpose(
                    ps[:, j * 128:(j + 1) * 128], xt_[:, k * 128:(k + 1) * 128], ident[:])],
                    reads=[xb_, identb], writes=[pb])
            if k4 % 2 == 0:
                P.op("vector", lambda e, ps=ps, ot=ot, k4=k4: e.tensor_copy(
                    ot[:, k4 * 4:(k4 + 1) * 4, :], ps[:].rearrange("p (j t) -> p j t", j=4)),
                    reads=[pb], writes=[ob])
            else:
                P.op("scalar", lambda e, ps=ps, ot=ot, k4=k4: e.copy(
                    ot[:, k4 * 4:(k4 + 1) * 4, :], ps[:].rearrange("p (j t) -> p j t", j=4)),
                    reads=[pb], writes=[ob])
        P.dma("sync", XTv[:, :, tt * 128:(tt + 1) * 128], ot[:], ob, XTb, sembuf=ob)

    for i, (L_, j_) in enumerate(ada_steps):
        ada_step(L_, j_)
        if i % 3 == 2:
            phase0_tile(i // 3)
    P.fence("sync", xto.b)

    def norm_phase(L, sub, to_act=False):
        P.fence("sync", [XTb])
        for q in range(T // 128):
            xt_, xb_ = big.next()
            xv = xt_[:].rearrange("p (k t) -> p k t", k=KD)
            P.dma("sync", xv, XTv[:, :, q * 128:(q + 1) * 128], XTb, xb_)
            ht, hb = hbo.next()
            P.op("scalar", lambda e, ht=ht, xv=xv: e.activation(ht[:], xv, AF.Square), reads=[xb_], writes=[hb])
            ps, pb = next_ps()
            fns = []
            for k in range(KD):
                fns.append(lambda e, ps=ps, ht=ht, k=k: e.matmul(
                    ps[:, :128], lhsT=ones_bf[:], rhs=ht[:, k, :], start=(k == 0), stop=(k == KD - 1)))
            P.mm(fns, reads=[hb, onesb], writes=[pb])
            rs, rb = tmpf.next()
            P.op("vector", lambda e, rs=rs, ps=ps: e.tensor_scalar(
                out=rs[:, :128], in0=ps[:, :128], scalar1=1.0 / D, scalar2=EPS, op0=ALU.mult, op1=ALU.add),
                reads=[pb], writes=[rb])
            P.op("scalar", lambda e, rs=rs: e.activation(rs[:, :128], rs[:, :128], AF.Sqrt),
                 reads=[rb], writes=[rb])
            P.op("vector", lambda e, rs=rs: e.reciprocal(rs[:, :128], rs[:, :128]),
                 reads=[rb], writes=[rb])
            P.op("vector", lambda e, xv=xv, rs=rs: e.tensor_tensor(
                out=xv, in0=xv, in1=rs[:, :128].unsqueeze(1).broadcast_to([128, KD, 128]), op=ALU.mult),
                reads=[xb_, rb], writes=[xb_])
            for k in range(KD):
                dst = act[:, k, q * 128:(q + 1) * 128] if to_act else ht[:, k, :]
                P.op("scalar", lambda e, xv=xv, k=k, dst=dst: e.activation(
                    dst, xv[:, k, :], AF.Identity, bias=mcol(L, sub * 3, k), scale=gs[:, L, sub, k:k + 1]),
                    reads=[xb_, modb, gsb], writes=[actb if to_act else hb])
            if not to_act:
                P.dma("sync", HTv[:, :, q * 128:(q + 1) * 128], ht[:], hb, HTb, sembuf=hb)
        P.fence("sync", hbo.b)

    def load_act(src_v, KT, srcb):
        P.fence("sync", [srcb])
        for k in range(KT):
            P.dma("sync", act[:, k, :], src_v[:, k, :], srcb, actb)

    def gemm(steps, KT, epilogue, krange=None):
        for si, wlist in enumerate(steps):
            wts = []
            for wap in wlist:
                wt, wb = wring.next()
                P.dma("gpsimd", wt[:, :KT, :], wap.rearrange("(k p) n -> p k n", p=128), wbuf_d, wb)
                wts.append((wt, wb))
            for m in range(4):
                for tb in range(NTB):
                    pss = []
                    for (wt, wb) in wts:
                        ps, pb = next_ps()
                        fns = []
                        k0, k1 = (0, KT - 1) if krange is None else krange(si, m)
                        for k in range(k0, k1 + 1):
                            fns.append(lambda e, ps=ps, wt=wt, m=m, k=k, tb=tb, k0=k0, k1=k1: e.matmul(
                                ps[:], lhsT=wt[:, k, m * 128:(m + 1) * 128],
                                rhs=act[:, k, tb * TB:(tb + 1) * TB], start=(k == k0), stop=(k == k1)))
                        P.mm(fns, reads=[wb, actb], writes=[pb])
                        pss.append((ps, pb))
                    epilogue(si, m, tb, pss)

    def ffn(L):
        norm_phase(L, 1, to_act=True)
        HIDv = HID.rearrange("(k p) t -> p k t", p=128)

        def ep_swiglu(si, m, tb, pss):
            (pg, pgb), (pu, pub) = pss
            sg, sgb = tmpf.next()
            P.op("scalar", lambda e: e.activation(sg[:], pg[:], AF.Silu), reads=[pgb], writes=[sgb])
            ho, hob = tmpb.next()
            P.op("vector", lambda e: e.tensor_tensor(out=ho[:], in0=pu[:], in1=sg[:], op=ALU.mult),
                 reads=[pub, sgb], writes=[hob])
            P.dma("sync", HIDv[:, si * 4 + m, tb * TB:(tb + 1) * TB], ho[:], hob, HIDb, sembuf=hob)

        steps = [[ffn_gu_d[L, :, j * 512:(j + 1) * 512], ffn_gu_d[L, :, FH + j * 512: FH + (j + 1) * 512]]
                 for j in range(FH // 512)]
        gemm(steps, KD, ep_swiglu)
        P.fence("sync", tmpb.b)

        def ep_res(si, m, tb, pss):
            (ps, pb), = pss
            kk = si * 4 + m
            xt_, xb_ = tmpf.next()
            P.dma("sync", xt_[:], XTv[:, kk, tb * TB:(tb + 1) * TB], XTb, xb_)
            P.op("vector", lambda e: e.scalar_tensor_tensor(
                out=xt_[:], in0=ps[:], scalar=mcol(L, 5, kk), in1=xt_[:], op0=ALU.mult, op1=ALU.add),
                reads=[pb, xb_, modb], writes=[xb_])
            P.dma("sync", XTv[:, kk, tb * TB:(tb + 1) * TB], xt_[:], xb_, XTb, sembuf=xb_)

        for half in range(2):
            load_act(HIDv[:, half * KL:(half + 1) * KL, :], KL, HIDb)
            steps = [[ffn_dn_d[L, half * LW:(half + 1) * LW, j * 512:(j + 1) * 512]] for j in range(D // 512)]
            gemm(steps, KL, ep_res)
            P.fence("sync", tmpf.b)


    lru_in_d = din("lru_w_in", [D, 2 * LW])
    lru_rg_d = din("lru_rg_dense", [LW, LW])
    lru_ig_d = din("lru_ig_dense", [LW, LW])
    lru_out_d = din("lru_w_out", [LW, D])
    GB = dscr("GB", [LW, T], BF16); GBb = Buf("GB")
    XB = dscr("XB", [LW, T], F32); XBb = Buf("XB")
    XC = dscr("XC", [LW, T], F32); XCb_ = Buf("XC")
    XCh = dscr("XCh", [LW, T], BF16); XChb = Buf("XCh")
    AA = dscr("AA", [LW, T], F32); AAb = Buf("AA")
    BT = dscr("BT", [LW, T], F32); BTb = Buf("BT")
    YL = dscr("YL", [LW, T], BF16); YLb = Buf("YL")
    GBv, XBv, XCv, XChv, AAv, BTv, YLv = [a.rearrange("(k p) t -> p k t", p=128) for a in (GB, XB, XC, XCh, AA, BT, YL)]

    def gelu_tile(src_ap, srcb, n, eng="vector"):
        t1, t1b = tmpf.next()
        P.op("scalar", lambda e: e.activation(t1[:, :n], src_ap, AF.Square), reads=[srcb], writes=[t1b])
        P.op(eng, lambda e: e.tensor_scalar(out=t1[:, :n], in0=t1[:, :n], scalar1=0.044715, scalar2=1.0,
                                            op0=ALU.mult, op1=ALU.add), reads=[t1b], writes=[t1b])
        P.op(eng, lambda e: e.tensor_tensor(out=t1[:, :n], in0=t1[:, :n], in1=src_ap, op=ALU.mult),
             reads=[t1b, srcb], writes=[t1b])
        P.op("scalar", lambda e: e.activation(t1[:, :n], t1[:, :n], AF.Sigmoid, scale=1.5957691216057308),
             reads=[t1b], writes=[t1b])
        ho, hob = tmpb.next()
        P.op(eng, lambda e: e.tensor_tensor(out=ho[:, :n], in0=t1[:, :n], in1=src_ap, op=ALU.mult),
             reads=[t1b, srcb], writes=[hob])
        return ho, hob

    def lru_mixer(L):
        norm_phase(L, 0, to_act=True)

        def ep_in(si, m, tb, pss):
            (ps, pb), = pss
            kk = si * 4 + m
            if kk < KL:
                ho, hob = gelu_tile(ps[:], pb, TB)
                P.dma("sync", GBv[:, kk, tb * TB:(tb + 1) * TB], ho[:], hob, GBb, sembuf=hob)
            else:
                xo, xob = tmpf.next()
                P.op("scalar", lambda e: e.copy(xo[:], ps[:]), reads=[pb], writes=[xob])
                P.dma("sync", XBv[:, kk - KL, tb * TB:(tb + 1) * TB], xo[:], xob, XBb, sembuf=xob)

        gemm([[lru_in_d[:, j * 512:(j + 1) * 512]] for j in range(2 * LW // 512)], KD, ep_in)
        P.fence("sync", tmpf.b + tmpb.b)

        spc = sb("spc", [128, KL], F32); spb = Buf("spc")
        ex = sb("spx", [128, KL], F32); exb = Buf("spx")
        P.op("scalar", lambda e: e.activation(ex[:], colsD[:, 44:66], AF.Exp, scale=-1.0), reads=[colsDb], writes=[exb])
        P.op("vector", lambda e: e.tensor_scalar(out=spc[:], in0=ex[:], scalar1=-0.25, scalar2=1.0 / 3.0,
                                                 op0=ALU.mult, op1=ALU.add), reads=[exb], writes=[spb])
        P.op("vector", lambda e: e.tensor_tensor(out=spc[:], in0=spc[:], in1=ex[:], op=ALU.mult), reads=[exb, spb], writes=[spb])
        P.op("vector", lambda e: e.tensor_scalar(out=spc[:], in0=spc[:], scalar1=-0.5, scalar2=None, op0=ALU.add),
             reads=[spb], writes=[spb])
        P.op("vector", lambda e: e.tensor_tensor(out=spc[:], in0=spc[:], in1=ex[:], op=ALU.mult), reads=[exb, spb], writes=[spb])
        P.op("vector", lambda e: e.tensor_scalar(out=spc[:], in0=spc[:], scalar1=1.0, scalar2=None, op0=ALU.add),
             reads=[spb], writes=[spb])
        P.op("vector", lambda e: e.tensor_tensor(out=spc[:], in0=spc[:], in1=ex[:], op=ALU.mult), reads=[exb, spb], writes=[spb])
        P.op("vector", lambda e: e.tensor_scalar(out=spc[:], in0=spc[:], scalar1=-8.0, scalar2=None, op0=ALU.mult),
             reads=[spb], writes=[spb])

        for kt in range(KL):
            xt_, xb_ = big.next()
            P.dma("sync", xt_[:, :T], XBv[:, kt, :], XBb, xb_)
            ot, ob = xto.next()
            ov = ot[:].rearrange("p k t -> p (k t)")
            P.op("vector", lambda e, xt_=xt_, ov=ov, kt=kt: e.tensor_scalar(
                out=ov, in0=xt_[:, :T], scalar1=colsC[:, 3 * KL + kt: 3 * KL + kt + 1],
                scalar2=colsC[:, 88 + kt: 88 + kt + 1], op0=ALU.mult, op1=ALU.add),
                reads=[xb_, colsCb], writes=[ob])
            for sh in (1, 2, 3):
                P.op("vector", lambda e, xt_=xt_, ov=ov, kt=kt, sh=sh: e.scalar_tensor_tensor(
                    out=ov[:, sh:], in0=xt_[:, :T - sh], scalar=colsC[:, (3 - sh) * KL + kt: (3 - sh) * KL + kt + 1],
                    in1=ov[:, sh:], op0=ALU.mult, op1=ALU.add), reads=[xb_, colsCb, ob], writes=[ob])
            hb16, hb16b = hbo.next()
            hv = hb16[:].rearrange("p k t -> p (k t)")
            P.op("scalar", lambda e, hv=hv, ov=ov: e.copy(hv, ov), reads=[ob], writes=[hb16b])
            P.dma("sync", XCv[:, kt, :], ov, ob, XCb_, sembuf=ob)
            P.dma("sync", XChv[:, kt, :], hv, hb16b, XChb, sembuf=hb16b)
        P.fence("sync", xto.b + hbo.b)

        load_act(XChv, KL, XChb)

        def kr(si, m):
            kk = si * 4 + m
            h0 = (128 * kk) // 176
            h1 = (128 * kk + 127) // 176
            return (176 * h0) // 128, min(KL - 1, (176 * h1 + 175) // 128)

        gst = {}

        def ep_gate(si, m, tb, pss):
            (pr, prb), (pi, pib) = pss
            kk = si * 4 + m
            if tb == 0:
                gst["a"] = big.next()
                gst["b"] = xto.next()
            at, atb = gst["a"]
            bt, btb = gst["b"]
            bv = bt[:].rearrange("p k t -> p (k t)")
            sl = slice(tb * TB, (tb + 1) * TB)
            r_, rb_ = tmpf.next()
            P.op("scalar", lambda e: e.activation(r_[:], pr[:], AF.Sigmoid, bias=colsD[:, kk:kk + 1], scale=1.0),
                 reads=[prb, colsDb], writes=[rb_])
            i_, ib_ = tmpf.next()
            P.op("scalar", lambda e: e.activation(i_[:], pi[:], AF.Sigmoid, bias=colsD[:, 22 + kk:22 + kk + 1], scale=1.0),
                 reads=[pib, colsDb], writes=[ib_])
            xc_, xcb = tmpf.next()
            P.dma("sync", xc_[:], XCv[:, kk, tb * TB:(tb + 1) * TB], XCb_, xcb)
            P.op("scalar", lambda e: e.activation(at[:, sl], r_[:], AF.Exp, scale=spc[:, kk:kk + 1]),
                 reads=[rb_, spb], writes=[atb])
            P.op("vector", lambda e: e.tensor_tensor(out=i_[:], in0=i_[:], in1=xc_[:], op=ALU.mult),
                 reads=[ib_, xcb], writes=[ib_])
            P.op("vector", lambda e: e.tensor_tensor(out=xc_[:], in0=at[:, sl], in1=at[:, sl], op=ALU.mult),
                 reads=[atb, xcb], writes=[xcb])
            P.op("vector", lambda e: e.tensor_scalar(out=xc_[:], in0=xc_[:], scalar1=-1.0, scalar2=1.0,
                                                     op0=ALU.mult, op1=ALU.add), reads=[xcb], writes=[xcb])
            P.op("scalar", lambda e: e.activation(xc_[:], xc_[:], AF.Sqrt), reads=[xcb], writes=[xcb])
            P.op("vector", lambda e: e.tensor_tensor(out=bv[:, sl], in0=i_[:], in1=xc_[:], op=ALU.mult),
                 reads=[ib_, xcb], writes=[btb])
            if tb == NTB - 1:
                gt, gtb = hbo.next()
                gv = gt[:].rearrange("p k t -> p (k t)")
                P.dma("sync", gv, GBv[:, kk, :], GBb, gtb)
                P.op("vector", lambda e: e.tensor_tensor_scan(
                    out=bv, data0=at[:, :T], data1=bv, initial=0.0, op0=ALU.mult, op1=ALU.add),
                    reads=[atb, btb], writes=[btb])
                P.op("vector", lambda e: e.tensor_tensor(out=gv, in0=bv, in1=gv, op=ALU.mult),
                     reads=[btb, gtb], writes=[gtb])
                P.dma("sync", YLv[:, kk, :], gv, gtb, YLb, sembuf=gtb)

        steps = [[lru_rg_d[:, j * 512: min(LW, (j + 1) * 512)], lru_ig_d[:, j * 512: min(LW, (j + 1) * 512)]]
                 for j in range((LW + 511) // 512)]
        gemm_ragged(steps, KL, ep_gate, kr)
        P.fence("sync", tmpf.b + hbo.b)

        load_act(YLv, KL, YLb)

        def ep_res1(si, m, tb, pss):
            (ps, pb), = pss
            kk = si * 4 + m
            xt_, xb_ = tmpf.next()
            P.dma("sync", xt_[:], XTv[:, kk, tb * TB:(tb + 1) * TB], XTb, xb_)
            P.op("vector", lambda e: e.scalar_tensor_tensor(
                out=xt_[:], in0=ps[:], scalar=mcol(L, 2, kk), in1=xt_[:], op0=ALU.mult, op1=ALU.add),
                reads=[pb, xb_, modb], writes=[xb_])
            P.dma("sync", XTv[:, kk, tb * TB:(tb + 1) * TB], xt_[:], xb_, XTb, sembuf=xb_)

        gemm([[lru_out_d[:, j * 512:(j + 1) * 512]] for j in range(D // 512)], KL, ep_res1)
        P.fence("sync", tmpf.b)

    def gemm_ragged(steps, KT, epilogue, krange):
        for si, wlist in enumerate(steps):
            wts = []
            ncol = wlist[0].shape[1]
            for wap in wlist:
                wt, wb = wring.next()
                P.dma("gpsimd", wt[:, :KT, :ncol], wap.rearrange("(k p) n -> p k n", p=128), wbuf_d, wb)
                wts.append((wt, wb))
            for m in range(ncol // 128):
                for tb in range(NTB):
                    pss = []
                    for (wt, wb) in wts:
                        ps, pb = next_ps()
                        fns = []
                        k0, k1 = krange(si, m)
                        for k in range(k0, k1 + 1):
                            fns.append(lambda e, ps=ps, wt=wt, m=m, k=k, tb=tb, k0=k0, k1=k1: e.matmul(
                                ps[:], lhsT=wt[:, k, m * 128:(m + 1) * 128],
                                rhs=act[:, k, tb * TB:(tb + 1) * TB], start=(k == k0), stop=(k == k1)))
                        P.mm(fns, reads=[wb, actb], writes=[pb])
                        pss.append((ps, pb))
                    epilogue(si, m, tb, pss)

    if dbg == "LRU":
        lru_mixer(1)

    PI = 3.141592653589793
    s5_in_d = din("s5_w_in", [D, D])
    s5_glu_d = din("s5_w_glu", [D, 2 * D])
    lamre_d = din("s5_lam_re", [128, 64]); lamim_d = din("s5_lam_im", [128, 64]); logdt_d = din("s5_log_dt", [128, 1])
    bre_d = din("s5_b_re", [128, 1024]); bim_d = din("s5_b_im", [128, 1024])
    cre_d = din("s5_c_re", [128, 1024]); cim_d = din("s5_c_im", [128, 1024])
    s5d_d = din("s5_d_row", [1, D])
    WST = dscr("WST", [128, 128, 128], BF16); WINT = dscr("WINT", [128, 128, 128], BF16)
    WOR = dscr("WOR", [128, 64, 128], BF16); WOI = dscr("WOI", [128, 64, 128], BF16)
    WSTb, WINTb, WORb, WOIb = Buf("WST"), Buf("WINT"), Buf("WOR"), Buf("WOI")
    YT = dscr("YT", [D, T], BF16); YTb = Buf("YT")
    YTv = YT.rearrange("(k p) t -> p k t", p=128)
    identh = sb("identh", [128, 128], BF16); identhb = Buf("identh")
    P.op("vector", lambda e: e.tensor_copy(identh[:], ident[:]), reads=[identb], writes=[identhb])
    cA4 = sb("cA4", [64, 2, 2, 128], F32); cAb = Buf("cA")
    cA1 = cA4[:, 0, :, :]; cA2 = cA4[:, 1, :, :]
    vpb = [Buf("vst0"), Buf("vst1")]
    dtab = sb("dtab", [128, D], F32); dtabb = Buf("dtab")

    def V(fn, r, w):
        return P.op("vector", fn, reads=r, writes=w)

    def A(fn, r, w):
        return P.op("scalar", fn, reads=r, writes=w)

    def tt(out, a, b, op):
        return lambda e: e.tensor_tensor(out=out, in0=a, in1=b, op=op)

    def s5_prep():
        nat = sb("s5nat", [128, 130], F32); natb = Buf("s5nat")
        P.dma("sync", nat[:, 0:64], lamre_d, vdb, natb)
        P.dma("sync", nat[:, 64:128], lamim_d, vdb, natb)
        P.dma("sync", nat[:, 128:129], logdt_d, vdb, natb)
        P.dma("sync", dtab[:], s5d_d.partition_broadcast(128), vdb, dtabb)
        Lr = sb("s5Lr", [128, 9, 64], F32); Li = sb("s5Li", [128, 9, 64], F32); Lb = Buf("s5L")
        sm = sb("s5sm", [128, 12, 64], F32); smb = Buf("s5sm")
        lr = nat[:, 0:64]; li = nat[:, 64:128]
        dtc = sm[:, 11, 0:1]
        A(lambda e: e.activation(dtc, nat[:, 128:129], AF.Exp), [natb], [smb])
        lrdt, lidt, mag, sa, ca, t1, t2, fr, fi, nr, den = [sm[:, i, :] for i in range(11)]
        V(lambda e: e.tensor_scalar(out=lrdt, in0=lr, scalar1=dtc, scalar2=None, op0=ALU.mult), [natb, smb], [smb])
        V(lambda e: e.tensor_scalar(out=lidt, in0=li, scalar1=dtc, scalar2=None, op0=ALU.mult), [natb, smb], [smb])
        A(lambda e: e.activation(mag, lrdt, AF.Exp), [smb], [smb])
        A(lambda e: e.activation(sa, lidt, AF.Sin, scale=1.0 / 16.0), [smb], [smb])
        V(lambda e: e.tensor_scalar(out=ca, in0=lidt, scalar1=1.0 / 16.0, scalar2=PI / 2, op0=ALU.mult, op1=ALU.add), [smb], [smb])
        A(lambda e: e.activation(ca, ca, AF.Sin), [smb], [smb])
        for _ in range(4):
            V(tt(t1, ca, ca, ALU.mult), [smb], [smb])
            V(tt(t2, sa, sa, ALU.mult), [smb], [smb])
            V(tt(sa, ca, sa, ALU.mult), [smb], [smb])
            V(lambda e: e.tensor_scalar(out=sa, in0=sa, scalar1=2.0, scalar2=None, op0=ALU.mult), [smb], [smb])
            V(tt(ca, t1, t2, ALU.subtract), [smb], [smb])
        V(lambda e: e.memset(Lr[:, 0, :], 1.0), [], [Lb])
        V(lambda e: e.memset(Li[:, 0, :], 0.0), [], [Lb])
        V(tt(Lr[:, 1, :], mag, ca, ALU.mult), [smb], [Lb])
        V(tt(Li[:, 1, :], mag, sa, ALU.mult), [smb], [Lb])
        for k in range(2, 9):
            V(tt(t1, Lr[:, k - 1, :], Lr[:, 1, :], ALU.mult), [Lb], [smb])
            V(tt(t2, Li[:, k - 1, :], Li[:, 1, :], ALU.mult), [Lb], [smb])
            V(tt(Lr[:, k, :], t1, t2, ALU.subtract), [smb], [Lb])
            V(tt(t1, Lr[:, k - 1, :], Li[:, 1, :], ALU.mult), [Lb], [smb])
            V(tt(t2, Li[:, k - 1, :], Lr[:, 1, :], ALU.mult), [Lb], [smb])
            V(tt(Li[:, k, :], t1, t2, ALU.add), [smb], [Lb])
        V(lambda e: e.tensor_scalar(out=nr, in0=Lr[:, 1, :], scalar1=-1.0, scalar2=None, op0=ALU.add), [Lb], [smb])
        V(tt(den, lr, lr, ALU.mult), [natb], [smb])
        V(tt(t1, li, li, ALU.mult), [natb], [smb])
        V(tt(den, den, t1, ALU.add), [smb], [smb])
        V(lambda e: e.reciprocal(den, den), [smb], [smb])
        V(tt(t1, nr, lr, ALU.mult), [smb, natb], [smb])
        V(tt(t2, Li[:, 1, :], li, ALU.mult), [Lb, natb], [smb])
        V(tt(fr, t1, t2, ALU.add), [smb], [smb])
        V(tt(fr, fr, den, ALU.mult), [smb], [smb])
        V(tt(t1, Li[:, 1, :], lr, ALU.mult), [Lb, natb], [smb])
        V(tt(t2, nr, li, ALU.mult), [smb, natb], [smb])
        V(tt(fi, t1, t2, ALU.subtract), [smb], [smb])
        V(tt(fi, fi, den, ALU.mult), [smb], [smb])
        for (src, dst0, dst1, neg) in ((Lr[:, 8, :], cA1[:, 0, :], cA1[:, 1, :], False), (Li[:, 8, :], cA2[:, 1, :], cA2[:, 0, :], True)):
            ps, pb = next_ps()
            P.mm([lambda e, ps=ps, src=src: e.transpose(ps[0:64, 0:128], src, ident[:])], reads=[Lb, identb], writes=[pb])
            V(lambda e, ps=ps, dst0=dst0: e.tensor_copy(dst0, ps[0:64, 0:128]), [pb], [cAb])
            if neg:
                V(lambda e, ps=ps, dst1=dst1: e.tensor_scalar(out=dst1, in0=ps[0:64, 0:128], scalar1=-1.0, scalar2=None, op0=ALU.mult), [pb], [cAb])
            else:
                V(lambda e, ps=ps, dst1=dst1: e.tensor_copy(dst1, ps[0:64, 0:128]), [pb], [cAb])
        s5_prep.sm = sm; s5_prep.smb = smb

        w0, w0b = wring.t[0], wring.b[0]
        w1, w1b = wring.t[1], wring.b[1]
        f32v = w1[:].rearrange("p k n -> p (k n)").bitcast(F32)
        bre, bim, bbre, bbim, tA = [f32v[:, i * 1024:(i + 1) * 1024] for i in range(5)]
        g0, g0b = big.t[0], big.b[0]
        g1_, g1b = big.t[1], big.b[1]
        cre, cim = g0[:, 0:1024], g0[:, 1024:2048]
        tB, tC = g1_[:, 0:1024], g1_[:, 1024:2048]
        w1aux = Buf("w1aux")
        P.dma("sync", bre, bre_d, vdb, w1b, sembuf=w1aux)
        P.dma("sync", bim, bim_d, vdb, w1b, sembuf=w1aux)
        P.dma("sync", cre, cre_d, vdb, g0b)
        P.dma("sync", cim, cim_d, vdb, g0b)
        actf = act[:].rearrange("p k t -> p (k t)")
        stage = actf[:, 0:16384].rearrange("p (g j) -> p g j", j=128)
        Mre = actf[:, 16384:24576].rearrange("p (q s c) -> p q s c", q=64, s=8)
        Mim = actf[:, 24576:32768].rearrange("p (q s c) -> p q s c", q=64, s=8)
        Nre = actf[:, 32768:40960].rearrange("p (t o q) -> p t o q", t=8, o=16)
        Nim = w0[:].rearrange("p k n -> p (k n)")[:, 0:8192].rearrange("p (t o q) -> p t o q", t=8, o=16)
        Krev = xto.t[0][:].rearrange("p k t -> p (k t)").bitcast(BF16)[:, 0:4096].rearrange("p (s o c) -> p s o c", s=16, o=16)
        KT = xto.t[1][:].rearrange("p k t -> p (k t)").bitcast(BF16)[:, 0:2048].rearrange("p (o s c) -> p o s c", o=16, s=8)
        krb, ktb = xto.b[0], xto.b[1]
        b3 = lambda a: a.rearrange("p (q c) -> p q c", c=16)
        bc3 = lambda a: a.unsqueeze(2).broadcast_to([128, 64, 16])
        V(tt(b3(tA), b3(bre), bc3(fr), ALU.mult), [w1b, smb], [w1b])
        V(tt(b3(tB), b3(bim), bc3(fi), ALU.mult), [w1b, smb], [g1b])
        V(tt(bbre, tA, tB, ALU.subtract), [w1b, g1b], [w1b])
        V(tt(b3(tA), b3(bim), bc3(fr), ALU.mult), [w1b, smb], [w1b])
        V(tt(b3(tB), b3(bre), bc3(fi), ALU.mult), [w1b, smb], [g1b])
        V(tt(bbim, tA, tB, ALU.add), [w1b, g1b], [w1b])
        for s_ in range(8):
            k = 7 - s_
            V(tt(b3(tA), b3(bbre), bc3(Lr[:, k, :]), ALU.mult), [w1b, Lb], [w1b])
            V(tt(b3(tB), b3(bbim), bc3(Li[:, k, :]), ALU.mult), [w1b, Lb], [g1b])
            V(tt(Mre[:, :, s_, :], b3(tA), b3(tB), ALU.subtract), [w1b, g1b], [actb])
            V(tt(b3(tA), b3(bbim), bc3(Lr[:, k, :]), ALU.mult), [w1b, Lb], [w1b])
            V(tt(b3(tB), b3(bbre), bc3(Li[:, k, :]), ALU.mult), [w1b, Lb], [g1b])
            V(tt(Mim[:, :, s_, :], b3(tA), b3(tB), ALU.add), [w1b, g1b], [actb])
        V(lambda e: e.memset(Krev, 0.0), [], [krb])
        c3 = lambda a: a.rearrange("p (o q) -> p o q", q=64)
        bo3 = lambda a: a.unsqueeze(1).broadcast_to([128, 16, 64])
        nre_f = tC.rearrange("p (o q) -> p o q", q=64)
        nim_f = f32v[:, 5120:5632]
        nimb = hbo.b[1]
        nim_f = hbo.t[1][:].rearrange("p k t -> p (k t)").bitcast(F32).rearrange("p (o q) -> p o q", q=64)
        kred = sb("kred", [128, 16], F32); kredb = Buf("kred")
        bbreT = bbre.rearrange("p (q c) -> p c q", c=16)
        bbimT = bbim.rearrange("p (q c) -> p c q", c=16)
        for k in range(9):
            V(tt(c3(tA), c3(cre), bo3(Lr[:, k, :]), ALU.mult), [g0b, Lb], [w1b])
            V(tt(c3(tB), c3(cim), bo3(Li[:, k, :]), ALU.mult), [g0b, Lb], [g1b])
            V(tt(nre_f, c3(tA), c3(tB), ALU.subtract), [w1b, g1b], [g1b])
            V(tt(c3(tA), c3(cre), bo3(Li[:, k, :]), ALU.mult), [g0b, Lb], [w1b])
            V(tt(c3(tB), c3(cim), bo3(Lr[:, k, :]), ALU.mult), [g0b, Lb], [g1b])
            V(tt(nim_f, c3(tA), c3(tB), ALU.add), [w1b, g1b], [nimb])
            if k >= 1:
                V(lambda e, k=k: e.tensor_copy(Nre[:, k - 1, :, :], nre_f), [g1b], [actb])
                V(lambda e, k=k: e.tensor_scalar(out=Nim[:, k - 1, :, :], in0=nim_f, scalar1=-1.0, scalar2=None, op0=ALU.mult), [nimb], [w0b])
            if k <= 7:
                for o in range(16):
                    V(tt(tA.rearrange("p (c q) -> p c q", q=64), bbreT, nre_f[:, o, :].unsqueeze(1).broadcast_to([128, 16, 64]), ALU.mult), [w1b, g1b], [w1b])
                    V(tt(tB.rearrange("p (c q) -> p c q", q=64), bbimT, nim_f[:, o, :].unsqueeze(1).broadcast_to([128, 16, 64]), ALU.mult), [w1b, nimb, g1b], [g1b])
                    V(tt(tA, tA, tB, ALU.subtract), [w1b, g1b], [w1b])
                    V(lambda e, k=k, o=o: e.tensor_reduce(out=kred[:], in_=tA.rearrange("p (c q) -> p c q", q=64),
                                                            axis=AX.X, op=ALU.add), [w1b], [kredb])
                    V(lambda e, k=k, o=o: e.tensor_copy(Krev[:, 7 - k, o, :], kred[:]), [kredb], [krb])

        def family(n_in, src_fn, src_bufs, dram, dramb, rows):
            for j4 in range(32):
                ps, pb = next_ps()
                psh = ps[:].bitcast(BF16)
                for q in range(4):
                    j = j4 * 4 + q
                    P.mm([lambda e, psh=psh, q=q, j=j: e.transpose(psh[0:n_in, q * 128:(q + 1) * 128], src_fn(j), identh[:])],
                         reads=src_bufs + [identhb], writes=[pb])
                cp = (lambda e, psh=psh, j4=j4: e.tensor_copy(stage[0:n_in, :, j4 * 4:(j4 + 1) * 4],
                                                              psh[0:n_in, 0:512].rearrange("p (j g) -> p g j", j=4)))
                if j4 % 2 == 0:
                    V(cp, [pb], [actb])
                else:
                    A(lambda e, psh=psh, j4=j4: e.copy(stage[0:n_in, :, j4 * 4:(j4 + 1) * 4],
                                                       psh[0:n_in, 0:512].rearrange("p (j g) -> p g j", j=4)), [pb], [actb])
            P.dma("sync", dram.rearrange("g r j -> r g j"), stage[0:rows, :, :], actb, dramb, sembuf=actb)

        family(128, lambda j: (Mre if j < 64 else Mim)[:, j % 64, :, :].rearrange("p s c -> p (s c)"), [actb], WST, WSTb, 128)
        family(64, lambda j: Nre[:, j // 16, j % 16, :], [actb], WOR, WORb, 64)
        family(64, lambda j: Nim[:, j // 16, j % 16, :], [w0b], WOI, WOIb, 64)
        for t_ in range(8):
            V(lambda e, t_=t_: e.tensor_copy(KT, Krev[:, 7 - t_: 15 - t_, :, :].rearrange("p s o c -> p o s c")), [krb], [ktb])
            for j4 in range(4):
                ps, pb = next_ps()
                psh = ps[:].bitcast(BF16)
                for q in range(4):
                    o = j4 * 4 + q
                    P.mm([lambda e, psh=psh, q=q, o=o: e.transpose(psh[:, q * 128:(q + 1) * 128],
                                                                   KT[:, o, :, :].rearrange("p s c -> p (s c)"), identh[:])],
                         reads=[ktb, identhb], writes=[pb])
                jb = t_ * 16 + j4 * 4
                V(lambda e, psh=psh, jb=jb: e.tensor_copy(stage[:, :, jb:jb + 4],
                                                          psh[:, 0:512].rearrange("p (j g) -> p g j", j=4)), [pb], [actb])
        P.dma("sync", WINT.rearrange("g r j -> r g j"), stage[:, :, :], actb, WINTb, sembuf=actb)
        P.fence("sync", [actb])

    def s5_main():
        norm_phase(0, 0)
        P.fence("sync", [HTb])
        actf = act[:].rearrange("p k t -> p (k t)")
        hsb = actf[:, 0:16384].rearrange("p (k t) -> p k t", k=KD)
        wsec = actf[:, 16384:32768].rearrange("p (w g j) -> p w g j", w=4, g=32)
        wsecb = Buf("wsec")
        sp_ = actf[0:64, 32768:40960].rearrange("p (r g c) -> p r g c", r=2, g=32)
        spb_ = Buf("sprev")
        xb_ = actf[:, 40960:45056].rearrange("p (g c) -> p g c", c=128)
        xbb = Buf("xblk")
        ub = xto.t[0][:].rearrange("p k t -> p (k t)").bitcast(BF16).rearrange("p (g s c) -> p g s c", g=32, s=8)
        ubb = xto.b[0]
        ym = xto.t[1][:].rearrange("p k t -> p (k t)").bitcast(BF16).rearrange("p (t c) -> p t c", c=512)
        ymb = xto.b[1]
        yos = [hbo.t[i][:].rearrange("p k t -> p (k t)").rearrange("p (a t) -> p a t", a=2) for i in range(2)]
        yobs = [hbo.b[0], hbo.b[1]]
        slr = [big.t[ri][:].bitcast(BF16)[0:64, :].rearrange("p (g c) -> p g c", c=128) for ri in range(2)]
        slbs = [big.b[0], big.b[1]]
        tsc = sb("tsc", [64, 2, 2, 32], F32)
        usc = sb("usc", [64, 2, 32], F32)
        tscb = Buf("tsc")
        uscb = Buf("usc")
        vst = s5_prep.sm[0:64, :, :].rearrange("p a b -> p (a b)").rearrange("p (j q s g) -> p j q s g", j=4, q=2, s=3)
        V(lambda e: e.memset(vst, 0.0), [], [s5_prep.smb, vpb[0], vpb[1]])
        for SBi in range(2):
            for k in range(KD):
                P.dma("sync", hsb[:, k, :], HTv[:, k, SBi * 1024:(SBi + 1) * 1024], HTb, actb)
            for j in range(4):
                gb = j * 32
                wt, wb = wring.next()
                P.dma("gpsimd", wt[:, :KD, :], s5_in_d[:, j * 512:(j + 1) * 512].rearrange("(k p) n -> p k n", p=128), wbuf_d, wb)
                P.dma("sync", wsec[:, 0, :, :], WST[gb:gb + 32].rearrange("g r j -> r g j"), WSTb, wsecb)
                P.dma("sync", wsec[:, 1, :, :], WINT[gb:gb + 32].rearrange("g r j -> r g j"), WINTb, wsecb)
                P.dma("sync", wsec[0:64, 2, :, :], WOR[gb:gb + 32].rearrange("g r j -> r g j"), WORb, wsecb)
                P.dma("sync", wsec[0:64, 3, :, :], WOI[gb:gb + 32].rearrange("g r j -> r g j"), WOIb, wsecb)
                for s_ in range(8):
                    ps, pb = next_ps()
                    fns = []
                    for k in range(KD):
                        fns.append(lambda e, ps=ps, k=k, s_=s_, wt=wt: e.matmul(
                            ps[:], lhsT=hsb[:, k, s_:1024:8], rhs=wt[:, k, :],
                            start=(k == 0), stop=(k == KD - 1)))
                    P.mm(fns, reads=[actb, wb], writes=[pb])
                    src = ps[:].rearrange("p (g c) -> p g c", c=16)
                    A(lambda e, s_=s_, src=src: e.copy(ub[:, :, s_, :], src), [pb], [ubb])
                for g4 in range(8):
                    ps, pb = next_ps()
                    psh = ps[:].bitcast(BF16)
                    for q in range(4):
                        gl = g4 * 4 + q
                        P.mm([lambda e, psh=psh, q=q, gl=gl: e.transpose(
                            psh[:, q * 128:(q + 1) * 128],
                            ub[:, gl, :, :].rearrange("p s c -> p (s c)"), identh[:])],
                            reads=[ubb, identhb], writes=[pb])
                    A(lambda e, psh=psh, g4=g4: e.copy(
                        xb_[:, g4 * 4:(g4 + 1) * 4, :], psh[:, 0:512].rearrange("p (g c) -> p g c", g=4)), [pb], [xbb])
                for g2 in range(16):
                    ps, pb = next_ps()
                    for q in range(2):
                        gl = g2 * 2 + q
                        for ri in range(2):
                            P.mm([lambda e, ps=ps, q=q, ri=ri, gl=gl: e.matmul(
                                ps[0:64, (q * 2 + ri) * 128:(q * 2 + ri + 1) * 128],
                                lhsT=wsec[:, 0, gl, ri * 64:(ri + 1) * 64], rhs=xb_[:, gl, :], start=True, stop=True)],
                                reads=[wsecb, xbb], writes=[pb])
                    for ri in range(2):
                        A(lambda e, ps=ps, g2=g2, ri=ri: e.copy(
                            slr[ri][:, g2 * 2:(g2 + 1) * 2, :],
                            ps[0:64, :].rearrange("p (q r c) -> p r q c", q=2, r=2)[:, ri, :, :]), [pb], [slbs[ri]])
                c4 = cA4[:, :, :, gb:gb + 32]
                for c_ in range(128):
                    rd, wr = c_ % 2, 1 - (c_ % 2)
                    v3 = vst[:, j, rd, :, :]
                    w3 = vst[:, j, wr, :, :]
                    win = bass.AP(v3.tensor, v3.offset, [list(v3.ap[0]), [32, 2], [32, 2], [1, 32]])
                    w02 = bass.AP(w3.tensor, w3.offset, [list(w3.ap[0]), [64, 2], [1, 32]])
                    V(tt(tsc[:], c4, win, ALU.mult), [cAb, vpb[rd]], [tscb])
                    V(tt(usc[:], tsc[:, 0, :, :], tsc[:, 1, :, :], ALU.add), [tscb], [uscb])
                    A(lambda e, c_=c_, v3=v3: e.copy(sp_[:, :, :, c_], v3[:, 0:2, :]), [vpb[rd]], [spb_])
                    V(tt(w02, usc[:, 0, :].unsqueeze(1).broadcast_to([64, 2, 32]),
                         slr[0][:, :, c_].unsqueeze(1).broadcast_to([64, 2, 32]), ALU.add), [uscb, slbs[0]], [vpb[wr]])
                    V(tt(w3[:, 1, :], usc[:, 1, :], slr[1][:, :, c_], ALU.add), [uscb, slbs[1]], [vpb[wr]])
                for g4 in range(8):
                    ps, pb = next_ps()
                    for q in range(4):
                        gl = g4 * 4 + q
                        P.mm([lambda e, ps=ps, q=q, gl=gl: e.matmul(
                                ps[:, q * 128:(q + 1) * 128], lhsT=xb_[:, gl, :], rhs=wsec[:, 1, gl, :], start=True, stop=False),
                              lambda e, ps=ps, q=q, gl=gl: e.matmul(
                                ps[:, q * 128:(q + 1) * 128], lhsT=sp_[:, 0, gl, :], rhs=wsec[0:64, 2, gl, :], start=False, stop=False),
                              lambda e, ps=ps, q=q, gl=gl: e.matmul(
                                ps[:, q * 128:(q + 1) * 128], lhsT=sp_[:, 1, gl, :], rhs=wsec[0:64, 3, gl, :], start=False, stop=True)],
                             reads=[xbb, wsecb, spb_], writes=[pb])
                    yf_, yfb = tmpf.next()
                    yf = yf_[:].rearrange("p (g t c) -> p g t c", g=4, t=8)
                    ch0 = (gb + g4 * 4) * 16
                    P.op("gpsimd", tt(yf, ub[:, g4 * 4: g4 * 4 + 4, :, :],
                         dtab[:, ch0:ch0 + 64].rearrange("p (g c) -> p g c", c=16).unsqueeze(2).broadcast_to([128, 4, 8, 16]),
                         ALU.mult), reads=[ubb, dtabb], writes=[yfb])
                    V(tt(yf, yf, ps[:].rearrange("p (g t c) -> p g t c", g=4, t=8), ALU.add), [yfb, pb], [yfb])
                    ho, hob = gelu_tile(yf_[:], yfb, 512, eng="gpsimd")
                    P.op("gpsimd", lambda e, ho=ho, g4=g4: e.tensor_copy(
                        ym[:, :, g4 * 64:(g4 + 1) * 64].rearrange("p t (g c) -> p g t c", c=16),
                        ho[:, :512].rearrange("p (g t c) -> p g t c", g=4, t=8)), reads=[hob], writes=[ymb])
                for t4 in range(8):
                    ps, pb = next_ps()
                    psh = ps[:].bitcast(BF16)
                    for q in range(4):
                        idx = t4 * 4 + q
                        t_, ct = idx // 4, idx % 4
                        P.mm([lambda e, psh=psh, q=q, t_=t_, ct=ct: e.transpose(
                            psh[:, q * 128:(q + 1) * 128], ym[:, t_, ct * 128:(ct + 1) * 128], identh[:])],
                            reads=[ymb, identhb], writes=[pb])
                    for q in range(4):
                        idx = t4 * 4 + q
                        t_, ct = idx // 4, idx % 4
                        A(lambda e, psh=psh, q=q, t_=t_, ct=ct: e.copy(
                            yos[ct // 2][:, ct % 2, t_:1024:8], psh[:, q * 128:(q + 1) * 128]), [pb], [yobs[ct // 2]])
                kt0 = gb // 8
                for hh in range(2):
                    P.dma("sync", YTv[:, kt0 + 2 * hh: kt0 + 2 * hh + 2, SBi * 1024:(SBi + 1) * 1024], yos[hh], yobs[hh], YTb, sembuf=yobs[hh])
        P.fence("sync", hbo.b)
        load_act(YTv, KD, YTb)

        def ep_glu(si, m, tb, pss):
            (pv, pvb), (pg, pgb) = pss
            kk = si * 4 + m
            sg, sgb = tmpf.next()
            A(lambda e: e.activation(sg[:], pg[:], AF.Sigmoid), [pgb], [sgb])
            V(tt(sg[:], sg[:], pv[:], ALU.mult), [sgb, pvb], [sgb])
            xt_, xb2 = tmpf.next()
            P.dma("sync", xt_[:], XTv[:, kk, tb * TB:(tb + 1) * TB], XTb, xb2)
            V(lambda e: e.scalar_tensor_tensor(out=xt_[:], in0=sg[:], scalar=mcol(0, 2, kk), in1=xt_[:],
                                               op0=ALU.mult, op1=ALU.add), [sgb, xb2, modb], [xb2])
            P.dma("sync", XTv[:, kk, tb * TB:(tb + 1) * TB], xt_[:], xb2, XTb, sembuf=xb2)

        gemm([[s5_glu_d[:, jj * 512:(jj + 1) * 512], s5_glu_d[:, D + jj * 512: D + (jj + 1) * 512]] for jj in range(4)], KD, ep_glu)
        P.fence("sync", tmpf.b)

    def final_phase():
        P.fence("sync", [XTb])
        for q in range(T // 128):
            xt_, xb_ = big.next()
            xv = xt_[:].rearrange("p (k t) -> p k t", k=KD)
            P.dma("sync", xv, XTv[:, :, q * 128:(q + 1) * 128], XTb, xb_)
            ht, hb = hbo.next()
            A(lambda e, ht=ht, xv=xv: e.activation(ht[:], xv, AF.Square), [xb_], [hb])
            ps, pb = next_ps()
            fns = []
            for k in range(KD):
                fns.append(lambda e, ps=ps, ht=ht, k=k: e.matmul(
                    ps[:, :128], lhsT=ones_bf[:], rhs=ht[:, k, :], start=(k == 0), stop=(k == KD - 1)))
            P.mm(fns, reads=[hb, onesb], writes=[pb])
            rs, rb = tmpf.next()
            V(lambda e, rs=rs, ps=ps: e.tensor_scalar(out=rs[:, :128], in0=ps[:, :128], scalar1=1.0 / D, scalar2=EPS,
                                                      op0=ALU.mult, op1=ALU.add), [pb], [rb])
            A(lambda e, rs=rs: e.activation(rs[:, :128], rs[:, :128], AF.Sqrt), [rb], [rb])
            V(lambda e, rs=rs: e.reciprocal(rs[:, :128], rs[:, :128]), [rb], [rb])
            for k in range(KD):
                V(lambda e, xv=xv, k=k, rs=rs: e.scalar_tensor_tensor(
                    out=xv[:, k, :], in0=xv[:, k, :], scalar=colsA[:, 80 + k:81 + k], in1=rs[:, :128],
                    op0=ALU.mult, op1=ALU.mult), [xb_, rb, colsAb], [xb_])
            ot, ob = xto.next()
            otv = ot[:].rearrange("p k t -> p (k t)")
            for k4 in range(4):
                ps2, pb2 = next_ps()
                for jq in range(4):
                    k = k4 * 4 + jq
                    P.mm([lambda e, ps2=ps2, jq=jq, k=k, xv=xv: e.transpose(
                        ps2[:, jq * 128:(jq + 1) * 128], xv[:, k, :], ident[:])], reads=[xb_, identb], writes=[pb2])
                if k4 % 2 == 0:
                    V(lambda e, ps2=ps2, k4=k4, otv=otv: e.tensor_copy(otv[:, k4 * 512:(k4 + 1) * 512], ps2[:]), [pb2], [ob])
                else:
                    A(lambda e, ps2=ps2, k4=k4, otv=otv: e.copy(otv[:, k4 * 512:(k4 + 1) * 512], ps2[:]), [pb2], [ob])
            P.dma("sync", out_d[q * 128:(q + 1) * 128, :], otv, ob, Buf("outd"), sembuf=ob)
        P.fence("sync", xto.b)

    if dbg == "S5":
        s5_prep()
        s5_main()
    snaps = []

    def snap(name):
        d_ = nc.dram_tensor(name, [D, T], F32, kind="ExternalOutput").ap()
        P.fence("sync", [XTb] + tmpf.b)
        b_ = Buf(name)
        P.dma("sync", d_, XT, XTb, b_)
        snaps.append(b_)

    if dbg is None or dbg == "ALL":
        s5_prep()
        s5_main()
        if dbg: snap("dbg0")
        ffn(0)
        if dbg: snap("dbg1")
        lru_mixer(1)
        if dbg: snap("dbg2")
        ffn(1)
        if dbg: snap("dbg3")
        final_phase()
        P.fence("sync", snaps)

    if dbg == "FFN0":
        ffn(0)

    if dbg in ("XT", "FFN0", "LRU", "S5"):
        dbg_d = nc.dram_tensor("dbg", [D, T], F32, kind="ExternalOutput").ap()
        P.fence("sync", [XTb] + tmpf.b)
        dbgb = Buf("dbgb")
        P.dma("sync", dbg_d, XT, XTb, dbgb)
        P.fence("sync", [dbgb])
    P.emit()
    return nc, es


def kernel(**inputs):
    dbg = os.environ.get("KDBG")
    nc, es = build(dbg=dbg)
    f = lambda a: np.ascontiguousarray(a, dtype=np.float32)
    x = f(inputs["x"]); c = f(inputs["c"])
    ng = f(inputs["norm_g"]).reshape(64, 128)
    fg = f(inputs["final_g"]).reshape(16, 128)
    sd = f(inputs["s5_d"]).reshape(16, 128)
    vecB = f(inputs["b_ada"]).reshape(2, 96, 128)
    vecC = np.zeros((128, 128), np.float32)
    vecC[0:88] = f(inputs["lru_conv_w"]).reshape(88, 128)
    vecC[88:110] = f(inputs["lru_conv_b"]).reshape(22, 128)
    vecD = np.zeros((128, 128), np.float32)
    vecD[0:22] = f(inputs["lru_b_rg"]).reshape(22, 128)
    vecD[22:44] = f(inputs["lru_b_ig"]).reshape(22, 128)
    vecD[44:66] = f(inputs["lru_lam"]).reshape(22, 128)
    def dense_bd(w):
        w = f(w)[0]
        o = np.zeros((LW, LW), np.float32)
        for h in range(16):
            o[h * 176:(h + 1) * 176, h * 176:(h + 1) * 176] = w[h]
        return o
    rgd = dense_bd(inputs["lru_w_rg"]); igd = dense_bd(inputs["lru_w_ig"])
    in_maps = []
    for b in range(NCORES):
        vecA = np.zeros((128, 128), np.float32)
        vecA[0:16] = c[b].reshape(16, 128)
        vecA[16:80] = ng
        vecA[80:96] = fg
        vecA[96:112] = sd
        in_maps.append({"x": x[b], "c": c[b].reshape(KD, 128), "ident": np.eye(128, dtype=np.float32),
                        "w_ada": f(inputs["w_ada"]), "vecA": vecA, "vecB": vecB, "vecC": vecC, "vecD": vecD,
                        "ffn_w_gu": f(inputs["ffn_w_gu"]), "ffn_w_down": f(inputs["ffn_w_down"]),
                        "lru_w_in": f(inputs["lru_w_in"])[0], "lru_rg_dense": rgd, "lru_ig_dense": igd,
                        "lru_w_out": f(inputs["lru_w_out"])[0],
                        "s5_w_in": f(inputs["s5_w_in"])[0], "s5_w_glu": f(inputs["s5_w_glu"])[0],
                        "s5_lam_re": f(inputs["s5_lam_re"])[0], "s5_lam_im": f(inputs["s5_lam_im"])[0],
                        "s5_log_dt": f(inputs["s5_log_dt"]).reshape(128, 1),
                        "s5_b_re": f(inputs["s5_b_re"]).reshape(128, 1024), "s5_b_im": f(inputs["s5_b_im"]).reshape(128, 1024),
                        "s5_c_re": f(inputs["s5_c_re"]).reshape(128, 1024), "s5_c_im": f(inputs["s5_c_im"]).reshape(128, 1024),
                        "s5_d_row": f(inputs["s5_d"]).reshape(1, D)})
    res = run_bass_kernel_spmd(nc, in_maps, core_ids=list(range(NCORES)))
    es.close()
    if dbg == "ALL":
        return [{k: r[k] for k in ("dbg0", "dbg1", "dbg2", "dbg3", "out")} for r in res.results]
    if dbg:
        return [r["dbg"] for r in res.results]
    return np.stack([r["out"] for r in res.results], axis=0)
```

```python
import os
from contextlib import ExitStack
import numpy as np
import concourse.bass as bass
import concourse.mybir as mybir
from concourse.bass_utils import run_bass_kernel_spmd

F32 = mybir.dt.float32
BF16 = mybir.dt.bfloat16
AF = mybir.ActivationFunctionType
ALU = mybir.AluOpType
AX = mybir.AxisListType

D = 2048
T = 2048
TB = 512
NTB = T // TB
KD = D // 128
FH = 5632
KF = FH // 128
LW = 2816
KL = LW // 128
EPS = 1e-6
NCORES = 4


class Buf:
    def __init__(self, name):
        self.name = name
        self.w = None
        self.r = []
        self.sem = None
        self.cnt = 0


class Prog:
    ENGS = ["sync", "scalar", "vector", "gpsimd", "tensor"]

    def __init__(self, nc, es):
        self.nc = nc
        self.es = es
        self.q = {e: [] for e in self.ENGS}
        self.esem = {}
        self.ecnt = {}
        for e in ["scalar", "vector", "gpsimd", "tensor"]:
            self.esem[e] = es.enter_context(nc.semaphore("es_" + e))
            self.ecnt[e] = 0
        self.known = {e: {} for e in self.ENGS}
        self.nsem = 4
        self.semobj = {}

    def _waits(self, eng, toks):
        need = {}
        for t in toks:
            if t is None:
                continue
            s, v = t
            if self.known[eng].get(id(s), 0) >= v:
                continue
            if need.get(id(s), (s, 0))[1] < v:
                need[id(s)] = (s, v)
        out = []
        for k, (s, v) in need.items():
            self.known[eng][k] = v
            out.append((s, v))
        return out

    def _deps(self, reads, writes):
        toks = []
        for b in reads:
            toks.append(b.w)
        for b in writes:
            toks.append(b.w)
            toks.extend(b.r)
        return toks

    def op(self, eng, fn, reads=(), writes=(), pe_same_ok=False):
        toks = self._deps(reads, writes)
        if pe_same_ok:
            toks = [t for t in toks if t is None or t[0] is not self.esem["tensor"]]
        w = self._waits(eng, toks)
        self.ecnt[eng] += 1
        tok = (self.esem[eng], self.ecnt[eng])
        self.q[eng].append((w, fn, (self.esem[eng], 1)))
        for b in writes:
            b.w = tok
            b.r = []
        for b in reads:
            if b not in writes:
                b.r.append(tok)
        return tok

    def mm(self, fns, reads=(), writes=()):
        toks = self._deps(reads, writes)
        toks = [t for t in toks if t is None or t[0] is not self.esem["tensor"]]
        w = self._waits("tensor", toks)
        n = len(fns)
        for i, fn in enumerate(fns):
            if i == n - 1:
                self.ecnt["tensor"] += 1
                tok = (self.esem["tensor"], self.ecnt["tensor"])
                self.q["tensor"].append((w if i == 0 else [], fn, (self.esem["tensor"], 1)))
            else:
                self.q["tensor"].append((w if i == 0 else [], fn, None))
        for b in writes:
            b.w = tok
            b.r = []
        for b in reads:
            if b not in writes:
                b.r.append(tok)
        return tok

    def dma(self, eng, out_ap, in_ap, src, dst, sembuf=None):
        sb = sembuf if sembuf is not None else dst
        if sb.sem is None:
            sb.sem = self.es.enter_context(self.nc.semaphore("d_" + sb.name))
            self.nsem += 1
        toks = self._deps([src], [dst])
        w = self._waits(eng, toks)
        sb.cnt += 16
        tok = (sb.sem, sb.cnt)
        self.q[eng].append((w, lambda e, o=out_ap, i=in_ap: e.dma_start(out=o, in_=i), (sb.sem, 16)))
        dst.w = tok
        dst.r = []
        src.r.append(tok)
        return tok

    def fence(self, eng, bufs):
        toks = []
        for b in bufs:
            toks.append(b.w)
            toks.extend(b.r)
        w = self._waits(eng, toks)
        if w:
            self.q[eng].append((w, None, None))

    def emit(self):
        nc = self.nc
        with nc.Block() as block:
            def run(name):
                def body(eng):
                    for (w, fn, inc) in self.q[name]:
                        for (s, v) in w:
                            eng.wait_ge(s, v)
                        if fn is None:
                            continue
                        ins = fn(eng)
                        if inc is not None:
                            ins.then_inc(inc[0], inc[1])
                return body
            block.sync(run("sync"))
            block.scalar(run("scalar"))
            block.vector(run("vector"))
            block.gpsimd(run("gpsimd"))
            block.tensor(run("tensor"))


class Ring:
    def __init__(self, nc, es, name, n, shape, dt):
        self.t = [es.enter_context(nc.sbuf_tensor(f"sb_{name}{i}", shape, dt)) for i in range(n)]
        self.b = [Buf(f"{name}{i}") for i in range(n)]
        self.i = 0
        self.n = n

    def next(self):
        k = self.i % self.n
        self.i += 1
        return self.t[k], self.b[k]


def build(stop_after=99, dbg=None):
    nc = bass.Bass("TRN2", target_bir_lowering=False)
    es = ExitStack()
    P = Prog(nc, es)

    def din(name, shape, dt=F32):
        return nc.dram_tensor(name, list(shape), dt, kind="ExternalInput").ap()

    def dscr(name, shape, dt):
        return nc.dram_tensor(name, list(shape), dt).ap()

    x_d = din("x", [T, D])
    c_d = din("c", [KD, 128])
    ident_d = din("ident", [128, 128])
    out_d = nc.dram_tensor("out", [T, D], F32, kind="ExternalOutput").ap()

    XT = dscr("XT", [D, T], F32)
    XTb = Buf("XT")
    HT = dscr("HT", [D, T], BF16)
    HTb = Buf("HT")

    def sb(name, shape, dt):
        return es.enter_context(nc.sbuf_tensor("sb_" + name, list(shape), dt))

    w_ada_d = din("w_ada", [2, D, 6 * D])
    vecA_d = din("vecA", [128, 128])
    vecB_d = din("vecB", [2, 96, 128])
    vecC_d = din("vecC", [128, 128])
    vecD_d = din("vecD", [128, 128])
    ffn_gu_d = din("ffn_w_gu", [2, D, 2 * FH])
    ffn_dn_d = din("ffn_w_down", [2, FH, D])
    wbuf_d = Buf("wdram")

    HID = dscr("HID", [FH, T], BF16)
    HIDb = Buf("HID")

    ident = sb("ident", [128, 128], F32)
    identb = Buf("ident")
    P.dma("sync", ident[:], ident_d, Buf("identd"), identb)
    ones_bf = sb("ones_bf", [128, 128], BF16)
    onesb = Buf("ones")
    P.op("vector", lambda e: e.memset(ones_bf[:], 1.0), writes=[onesb])

    psum = [es.enter_context(nc.psum_tensor(f"ps{i}", [128, 512], F32)) for i in range(8)]
    psb = [Buf(f"ps{i}") for i in range(8)]
    pctr = [0]
    ps_n = [7]

    def next_ps():
        k = pctr[0] % ps_n[0]
        pctr[0] += 1
        return psum[k], psb[k]

    big = Ring(nc, es, "big", 2, [128, 2048], F32)
    xto = Ring(nc, es, "xto", 2, [128, KD, 128], F32)
    hbo = Ring(nc, es, "hbo", 2, [128, KD, 128], BF16)
    tmpf = Ring(nc, es, "tmpf", 4, [128, 512], F32)
    tmpb = Ring(nc, es, "tmpb", 3, [128, 512], BF16)
    wring = Ring(nc, es, "wr", 2, [128, KL, 512], BF16)
    act = sb("act", [128, KL, T], BF16)
    actb = Buf("act")

    colsA = sb("colsA", [128, 128], F32)
    colsB = sb("colsB", [128, 2, 96], F32)
    colsC = sb("colsC", [128, 128], F32)
    colsD = sb("colsD", [128, 128], F32)
    colsAb, colsBb, colsCb, colsDb = Buf("cA"), Buf("cB"), Buf("cC"), Buf("cD")
    vdb = Buf("vecd")

    def load_cols(src_ap, nrows, dst_ap, dstb):
        st, stb = big.next()
        P.dma("sync", st[:nrows, :128], src_ap, vdb, stb)
        ps, pb = next_ps()
        P.mm([lambda e: e.transpose(ps[:, :nrows], st[:nrows, :128], ident[:nrows, :nrows])],
             reads=[stb, identb], writes=[pb])
        P.op("vector", lambda e: e.tensor_copy(dst_ap, ps[:, :nrows]), reads=[pb], writes=[dstb])

    load_cols(vecA_d, 128, colsA[:, :], colsAb)
    load_cols(vecB_d[0], 96, colsB[:, 0, :], colsBb)
    load_cols(vecB_d[1], 96, colsB[:, 1, :], colsBb)
    load_cols(vecC_d, 128, colsC[:, :], colsCb)
    load_cols(vecD_d, 128, colsD[:, :], colsDb)
    condT = sb("condT", [128, KD], BF16)
    condb = Buf("cond")
    P.op("scalar", lambda e: e.activation(condT[:], colsA[:, 0:16], AF.Silu), reads=[colsAb], writes=[condb])

    mod = sb("mod", [128, 2, 96], F32)
    modb = Buf("mod")
    gs = sb("gs", [128, 2, 2, KD], F32)
    gsb = Buf("gs")
    ada_steps = []

    def ada_step(L, j):
        psm, pbm = psum[7], psb[7]
        wt, wb = wring.next()
        P.dma("gpsimd", wt[:, :KD, :],
              w_ada_d[L, :, j * 512:(j + 1) * 512].rearrange("(k p) n -> p k n", p=128), wbuf_d, wb)
        for m in range(4):
            col = j * 4 + m
            fns = []
            for k in range(KD):
                fns.append(lambda e, wt=wt, m=m, k=k, col=col, psm=psm: e.matmul(
                    psm[:, col:col + 1], lhsT=wt[:, k, m * 128:(m + 1) * 128], rhs=condT[:, k:k + 1],
                    start=(k == 0), stop=(k == KD - 1)))
            P.mm(fns, reads=[wb, condb], writes=[pbm])
        if j == 23:
            P.op("vector", lambda e, L=L, psm=psm: e.tensor_tensor(
                out=mod[:, L, :], in0=psm[:, 0:96], in1=colsB[:, L, :], op=ALU.add),
                reads=[pbm, colsBb], writes=[modb])
            for sub in range(2):
                sc0 = (sub * 3 + 1) * 16
                P.op("vector", lambda e, L=L, sub=sub, sc0=sc0: e.scalar_tensor_tensor(
                    out=gs[:, L, sub, :], in0=mod[:, L, sc0:sc0 + 16], scalar=1.0,
                    in1=colsA[:, 16 + (L * 2 + sub) * 16: 16 + (L * 2 + sub) * 16 + 16],
                    op0=ALU.add, op1=ALU.mult), reads=[modb, colsAb], writes=[gsb])

    for L in range(2):
        for j in range(24):
            ada_steps.append((L, j))

    def mcol(L, idx, k):
        return mod[:, L, idx * 16 + k: idx * 16 + k + 1]

    xdb = Buf("xd")
    XTv = XT.rearrange("(k p) t -> p k t", p=128)
    HTv = HT.rearrange("(k p) t -> p k t", p=128)
    def phase0_tile(tt):
        xt_, xb_ = big.next()
        P.dma("sync", xt_[:, :D], x_d[tt * 128:(tt + 1) * 128, :], xdb, xb_)
        ot, ob = xto.next()
        for k4 in range(KD // 4):
            ps, pb = next_ps()
            for j in range(4):
                k = k4 * 4 + j
                P.mm([lambda e, ps=ps, j=j, k=k, xt_=xt_: e.transpose(
                    ps[:, j * 128:(j + 1) * 128], xt_[:, k * 128:(k + 1) * 128], ident[:])],
                    reads=[xb_, identb], writes=[pb])
            if k4 % 2 == 0:
                P.op("vector", lambda e, ps=ps, ot=ot, k4=k4: e.tensor_copy(
                    ot[:, k4 * 4:(k4 + 1) * 4, :], ps[:].rearrange("p (j t) -> p j t", j=4)),
                    reads=[pb], writes=[ob])
            else:
                P.op("scalar", lambda e, ps=ps, ot=ot, k4=k4: e.copy(
                    ot[:, k4 * 4:(k4 + 1) * 4, :], ps[:].rearrange("p (j t) -> p j t", j=4)),
                    reads=[pb], writes=[ob])
        P.dma("sync", XTv[:, :, tt * 128:(tt + 1) * 128], ot[:], ob, XTb, sembuf=ob)

    for i, (L_, j_) in enumerate(ada_steps):
        ada_step(L_, j_)
        if i % 3 == 2:
            phase0_tile(i // 3)
    ps_n[0] = 8
    P.fence("sync", xto.b)

    def norm_phase(L, sub, to_act=False, q0=0, q1=T // 128, dst_fn=None):
        P.fence("sync", [XTb])
        for q in range(q0, q1):
            xt_, xb_ = big.next()
            xv = xt_[:].rearrange("p (k t) -> p k t", k=KD)
            P.dma("sync", xv, XTv[:, :, q * 128:(q + 1) * 128], XTb, xb_)
            ht, hb = hbo.next()
            P.op("scalar", lambda e, ht=ht, xv=xv: e.activation(ht[:], xv, AF.Square), reads=[xb_], writes=[hb])
            ps, pb = next_ps()
            fns = []
            for k in range(KD):
                fns.append(lambda e, ps=ps, ht=ht, k=k: e.matmul(
                    ps[:, :128], lhsT=ones_bf[:], rhs=ht[:, k, :], start=(k == 0), stop=(k == KD - 1)))
            P.mm(fns, reads=[hb, onesb], writes=[pb])
            rs, rb = tmpf.next()
            P.op("vector", lambda e, rs=rs, ps=ps: e.tensor_scalar(
                out=rs[:, :128], in0=ps[:, :128], scalar1=1.0 / D, scalar2=EPS, op0=ALU.mult, op1=ALU.add),
                reads=[pb], writes=[rb])
            P.op("scalar", lambda e, rs=rs: e.activation(rs[:, :128], rs[:, :128], AF.Sqrt),
                 reads=[rb], writes=[rb])
            P.op("vector", lambda e, rs=rs: e.reciprocal(rs[:, :128], rs[:, :128]),
                 reads=[rb], writes=[rb])
            P.op("vector", lambda e, xv=xv, rs=rs: e.tensor_tensor(
                out=xv, in0=xv, in1=rs[:, :128].unsqueeze(1).broadcast_to([128, KD, 128]), op=ALU.mult),
                reads=[xb_, rb], writes=[xb_])
            for k in range(KD):
                dst = (dst_fn(k, q) if dst_fn is not None else act[:, k, q * 128:(q + 1) * 128]) if to_act else ht[:, k, :]
                P.op("scalar", lambda e, xv=xv, k=k, dst=dst: e.activation(
                    dst, xv[:, k, :], AF.Identity, bias=mcol(L, sub * 3, k), scale=gs[:, L, sub, k:k + 1]),
                    reads=[xb_, modb, gsb], writes=[actb if to_act else hb])
            if not to_act:
                P.dma("sync", HTv[:, :, q * 128:(q + 1) * 128], ht[:], hb, HTb, sembuf=hb)
        P.fence("sync", hbo.b)

    def load_act(src_v, KT, srcb):
        P.fence("sync", [srcb])
        for k in range(KT):
            P.dma("sync", act[:, k, :], src_v[:, k, :], srcb, actb)

    xpend = {}

    def xt_prefetch(kk, tb):
        xt_, xb_ = tmpf.next()
        P.dma("sync", xt_[:], XTv[:, kk, tb * TB:(tb + 1) * TB], XTb, xb_)
        xpend[(kk, tb)] = (xt_, xb_)

    def gemm(steps, KT, epilogue, krange=None, pre=False):
        order = [(si, m, tb) for si in range(len(steps)) for m in range(4) for tb in range(NTB)]
        if pre:
            xt_prefetch(order[0][0] * 4 + order[0][1], order[0][2])
        idx = 0
        for si, wlist in enumerate(steps):
            wts = []
            for wap in wlist:
                wt, wb = wring.next()
                P.dma("gpsimd", wt[:, :KT, :], wap.rearrange("(k p) n -> p k n", p=128), wbuf_d, wb)
                wts.append((wt, wb))
            for m in range(4):
                for tb in range(NTB):
                    pss = []
                    for (wt, wb) in wts:
                        ps, pb = next_ps()
                        fns = []
                        k0, k1 = (0, KT - 1) if krange is None else krange(si, m)
                        for k in range(k0, k1 + 1):
                            fns.append(lambda e, ps=ps, wt=wt, m=m, k=k, tb=tb, k0=k0, k1=k1: e.matmul(
                                ps[:], lhsT=wt[:, k, m * 128:(m + 1) * 128],
                                rhs=act[:, k, tb * TB:(tb + 1) * TB], start=(k == k0), stop=(k == k1)))
                        P.mm(fns, reads=[wb, actb], writes=[pb])
                        pss.append((ps, pb))
                    idx += 1
                    if pre and idx < len(order):
                        nsi, nm, ntb = order[idx]
                        xt_prefetch(nsi * 4 + nm, ntb)
                    epilogue(si, m, tb, pss)

    def ffn(L):
        norm_phase(L, 1, to_act=True)
        HIDv = HID.rearrange("(k p) t -> p k t", p=128)

        def ep_swiglu(si, m, tb, pss):
            (pg, pgb), (pu, pub) = pss
            sg, sgb = tmpf.next()
            P.op("scalar", lambda e: e.activation(sg[:], pg[:], AF.Silu), reads=[pgb], writes=[sgb])
            ho, hob = tmpb.next()
            P.op("vector", lambda e: e.tensor_tensor(out=ho[:], in0=pu[:], in1=sg[:], op=ALU.mult),
                 reads=[pub, sgb], writes=[hob])
            P.dma("sync", HIDv[:, si * 4 + m, tb * TB:(tb + 1) * TB], ho[:], hob, HIDb, sembuf=hob)

        steps = [[ffn_gu_d[L, :, j * 512:(j + 1) * 512], ffn_gu_d[L, :, FH + j * 512: FH + (j + 1) * 512]]
                 for j in range(FH // 512)]
        gemm(steps, KD, ep_swiglu)
        P.fence("sync", tmpb.b)

        def ep_res(si, m, tb, pss):
            (ps, pb), = pss
            kk = si * 4 + m
            xt_, xb_ = xpend.pop((kk, tb))
            P.op("vector", lambda e: e.scalar_tensor_tensor(
                out=xt_[:], in0=ps[:], scalar=mcol(L, 5, kk), in1=xt_[:], op0=ALU.mult, op1=ALU.add),
                reads=[pb, xb_, modb], writes=[xb_])
            P.dma("sync", XTv[:, kk, tb * TB:(tb + 1) * TB], xt_[:], xb_, XTb, sembuf=xb_)

        for half in range(2):
            load_act(HIDv[:, half * KL:(half + 1) * KL, :], KL, HIDb)
            steps = [[ffn_dn_d[L, half * LW:(half + 1) * LW, j * 512:(j + 1) * 512]] for j in range(D // 512)]
            gemm(steps, KL, ep_res, pre=True)
            P.fence("sync", tmpf.b)


    lru_in_d = din("lru_w_in", [D, 2 * LW])
    lru_rg_d = din("lru_rg_dense", [LW, LW])
    lru_ig_d = din("lru_ig_dense", [LW, LW])
    lru_out_d = din("lru_w_out", [LW, D])
    GB = dscr("GB", [LW, T], BF16); GBb = Buf("GB")
    XB = dscr("XB", [LW, T], F32); XBb = Buf("XB")
    XC = dscr("XC", [LW, T], F32); XCb_ = Buf("XC")
    XCh = dscr("XCh", [LW, T], BF16); XChb = Buf("XCh")
    AA = dscr("AA", [LW, T], F32); AAb = Buf("AA")
    BT = dscr("BT", [LW, T], F32); BTb = Buf("BT")
    YL = dscr("YL", [LW, T], BF16); YLb = Buf("YL")
    GBv, XBv, XCv, XChv, AAv, BTv, YLv = [a.rearrange("(k p) t -> p k t", p=128) for a in (GB, XB, XC, XCh, AA, BT, YL)]

    def gelu_tile(src_ap, srcb, n, eng="vector"):
        t1, t1b = tmpf.next()
        P.op("scalar", lambda e: e.activation(t1[:, :n], src_ap, AF.Square), reads=[srcb], writes=[t1b])
        P.op(eng, lambda e: e.tensor_scalar(out=t1[:, :n], in0=t1[:, :n], scalar1=0.044715, scalar2=1.0,
                                            op0=ALU.mult, op1=ALU.add), reads=[t1b], writes=[t1b])
        P.op(eng, lambda e: e.tensor_tensor(out=t1[:, :n], in0=t1[:, :n], in1=src_ap, op=ALU.mult),
             reads=[t1b, srcb], writes=[t1b])
        P.op("scalar", lambda e: e.activation(t1[:, :n], t1[:, :n], AF.Sigmoid, scale=1.5957691216057308),
             reads=[t1b], writes=[t1b])
        ho, hob = tmpb.next()
        P.op(eng, lambda e: e.tensor_tensor(out=ho[:, :n], in0=t1[:, :n], in1=src_ap, op=ALU.mult),
             reads=[t1b, srcb], writes=[hob])
        return ho, hob

    def lru_mixer(L):
        norm_phase(L, 0, to_act=True)

        def ep_in(si, m, tb, pss):
            (ps, pb), = pss
            kk = si * 4 + m
            if kk < KL:
                ho, hob = gelu_tile(ps[:], pb, TB)
                P.dma("sync", GBv[:, kk, tb * TB:(tb + 1) * TB], ho[:], hob, GBb, sembuf=hob)
            else:
                xo, xob = tmpf.next()
                P.op("scalar", lambda e: e.copy(xo[:], ps[:]), reads=[pb], writes=[xob])
                P.dma("sync", XBv[:, kk - KL, tb * TB:(tb + 1) * TB], xo[:], xob, XBb, sembuf=xob)

        gemm([[lru_in_d[:, j * 512:(j + 1) * 512]] for j in range(2 * LW // 512)], KD, ep_in)
        P.fence("sync", tmpf.b + tmpb.b)

        spc = sb("spc", [128, KL], F32); spb = Buf("spc")
        ex = sb("spx", [128, KL], F32); exb = Buf("spx")
        P.op("scalar", lambda e: e.activation(ex[:], colsD[:, 44:66], AF.Exp, scale=-1.0), reads=[colsDb], writes=[exb])
        P.op("vector", lambda e: e.tensor_scalar(out=spc[:], in0=ex[:], scalar1=-0.25, scalar2=1.0 / 3.0,
                                                 op0=ALU.mult, op1=ALU.add), reads=[exb], writes=[spb])
        P.op("vector", lambda e: e.tensor_tensor(out=spc[:], in0=spc[:], in1=ex[:], op=ALU.mult), reads=[exb, spb], writes=[spb])
        P.op("vector", lambda e: e.tensor_scalar(out=spc[:], in0=spc[:], scalar1=-0.5, scalar2=None, op0=ALU.add),
             reads=[spb], writes=[spb])
        P.op("vector", lambda e: e.tensor_tensor(out=spc[:], in0=spc[:], in1=ex[:], op=ALU.mult), reads=[exb, spb], writes=[spb])
        P.op("vector", lambda e: e.tensor_scalar(out=spc[:], in0=spc[:], scalar1=1.0, scalar2=None, op0=ALU.add),
             reads=[spb], writes=[spb])
        P.op("vector", lambda e: e.tensor_tensor(out=spc[:], in0=spc[:], in1=ex[:], op=ALU.mult), reads=[exb, spb], writes=[spb])
        P.op("vector", lambda e: e.tensor_scalar(out=spc[:], in0=spc[:], scalar1=-8.0, scalar2=None, op0=ALU.mult),
             reads=[spb], writes=[spb])

        for kt in range(KL):
            xt_, xb_ = big.next()
            P.dma("sync", xt_[:, :T], XBv[:, kt, :], XBb, xb_)
            ot, ob = xto.next()
            ov = ot[:].rearrange("p k t -> p (k t)")
            P.op("vector", lambda e, xt_=xt_, ov=ov, kt=kt: e.tensor_scalar(
                out=ov, in0=xt_[:, :T], scalar1=colsC[:, 3 * KL + kt: 3 * KL + kt + 1],
                scalar2=colsC[:, 88 + kt: 88 + kt + 1], op0=ALU.mult, op1=ALU.add),
                reads=[xb_, colsCb], writes=[ob])
            for sh in (1, 2, 3):
                P.op("vector", lambda e, xt_=xt_, ov=ov, kt=kt, sh=sh: e.scalar_tensor_tensor(
                    out=ov[:, sh:], in0=xt_[:, :T - sh], scalar=colsC[:, (3 - sh) * KL + kt: (3 - sh) * KL + kt + 1],
                    in1=ov[:, sh:], op0=ALU.mult, op1=ALU.add), reads=[xb_, colsCb, ob], writes=[ob])
            hb16, hb16b = hbo.next()
            hv = hb16[:].rearrange("p k t -> p (k t)")
            P.op("scalar", lambda e, hv=hv, ov=ov: e.copy(hv, ov), reads=[ob], writes=[hb16b])
            P.dma("sync", XCv[:, kt, :], ov, ob, XCb_, sembuf=ob)
            P.dma("sync", XChv[:, kt, :], hv, hb16b, XChb, sembuf=hb16b)
        P.fence("sync", xto.b + hbo.b)

        load_act(XChv, KL, XChb)

        def kr(si, m):
            kk = si * 4 + m
            h0 = (128 * kk) // 176
            h1 = (128 * kk + 127) // 176
            return (176 * h0) // 128, min(KL - 1, (176 * h1 + 175) // 128)

        gst = {}

        def ep_gate(si, m, tb, pss):
            (pr, prb), (pi, pib) = pss
            kk = si * 4 + m
            if tb == 0:
                gst["a"] = big.next()
                gst["b"] = xto.next()
            at, atb = gst["a"]
            bt, btb = gst["b"]
            bv = bt[:].rearrange("p k t -> p (k t)")
            sl = slice(tb * TB, (tb + 1) * TB)
            r_, rb_ = tmpf.next()
            P.op("scalar", lambda e: e.activation(r_[:], pr[:], AF.Sigmoid, bias=colsD[:, kk:kk + 1], scale=1.0),
                 reads=[prb, colsDb], writes=[rb_])
            i_, ib_ = tmpf.next()
            P.op("scalar", lambda e: e.activation(i_[:], pi[:], AF.Sigmoid, bias=colsD[:, 22 + kk:22 + kk + 1], scale=1.0),
                 reads=[pib, colsDb], writes=[ib_])
            xc_, xcb = tmpf.next()
            P.dma("sync", xc_[:], XCv[:, kk, tb * TB:(tb + 1) * TB], XCb_, xcb)
            P.op("scalar", lambda e: e.activation(at[:, sl], r_[:], AF.Exp, scale=spc[:, kk:kk + 1]),
                 reads=[rb_, spb], writes=[atb])
            P.op("vector", lambda e: e.tensor_tensor(out=i_[:], in0=i_[:], in1=xc_[:], op=ALU.mult),
                 reads=[ib_, xcb], writes=[ib_])
            P.op("vector", lambda e: e.tensor_tensor(out=xc_[:], in0=at[:, sl], in1=at[:, sl], op=ALU.mult),
                 reads=[atb, xcb], writes=[xcb])
            P.op("vector", lambda e: e.tensor_scalar(out=xc_[:], in0=xc_[:], scalar1=-1.0, scalar2=1.0,
                                                     op0=ALU.mult, op1=ALU.add), reads=[xcb], writes=[xcb])
            P.op("scalar", lambda e: e.activation(xc_[:], xc_[:], AF.Sqrt), reads=[xcb], writes=[xcb])
            P.op("vector", lambda e: e.tensor_tensor(out=bv[:, sl], in0=i_[:], in1=xc_[:], op=ALU.mult),
                 reads=[ib_, xcb], writes=[btb])
            if tb == NTB - 1:
                gt, gtb = hbo.next()
                gv = gt[:].rearrange("p k t -> p (k t)")
                P.dma("sync", gv, GBv[:, kk, :], GBb, gtb)
                P.op("vector", lambda e: e.tensor_tensor_scan(
                    out=bv, data0=at[:, :T], data1=bv, initial=0.0, op0=ALU.mult, op1=ALU.add),
                    reads=[atb, btb], writes=[btb])
                P.op("vector", lambda e: e.tensor_tensor(out=gv, in0=bv, in1=gv, op=ALU.mult),
                     reads=[btb, gtb], writes=[gtb])
                P.dma("sync", YLv[:, kk, :], gv, gtb, YLb, sembuf=gtb)

        steps = [[lru_rg_d[:, j * 512: min(LW, (j + 1) * 512)], lru_ig_d[:, j * 512: min(LW, (j + 1) * 512)]]
                 for j in range((LW + 511) // 512)]
        gemm_ragged(steps, KL, ep_gate, kr)
        P.fence("sync", tmpf.b + hbo.b)

        load_act(YLv, KL, YLb)

        def ep_res1(si, m, tb, pss):
            (ps, pb), = pss
            kk = si * 4 + m
            xt_, xb_ = xpend.pop((kk, tb))
            P.op("vector", lambda e: e.scalar_tensor_tensor(
                out=xt_[:], in0=ps[:], scalar=mcol(L, 2, kk), in1=xt_[:], op0=ALU.mult, op1=ALU.add),
                reads=[pb, xb_, modb], writes=[xb_])
            P.dma("sync", XTv[:, kk, tb * TB:(tb + 1) * TB], xt_[:], xb_, XTb, sembuf=xb_)

        gemm([[lru_out_d[:, j * 512:(j + 1) * 512]] for j in range(D // 512)], KL, ep_res1, pre=True)
        P.fence("sync", tmpf.b)

    def gemm_ragged(steps, KT, epilogue, krange):
        for si, wlist in enumerate(steps):
            wts = []
            ncol = wlist[0].shape[1]
            for wap in wlist:
                wt, wb = wring.next()
                P.dma("gpsimd", wt[:, :KT, :ncol], wap.rearrange("(k p) n -> p k n", p=128), wbuf_d, wb)
                wts.append((wt, wb))
            for m in range(ncol // 128):
                for tb in range(NTB):
                    pss = []
                    for (wt, wb) in wts:
                        ps, pb = next_ps()
                        fns = []
                        k0, k1 = krange(si, m)
                        for k in range(k0, k1 + 1):
                            fns.append(lambda e, ps=ps, wt=wt, m=m, k=k, tb=tb, k0=k0, k1=k1: e.matmul(
                                ps[:], lhsT=wt[:, k, m * 128:(m + 1) * 128],
                                rhs=act[:, k, tb * TB:(tb + 1) * TB], start=(k == k0), stop=(k == k1)))
                        P.mm(fns, reads=[wb, actb], writes=[pb])
                        pss.append((ps, pb))
                    epilogue(si, m, tb, pss)

    if dbg == "LRU":
        lru_mixer(1)

    PI = 3.141592653589793
    s5_in_d = din("s5_w_in", [D, D])
    s5_glu_d = din("s5_w_glu", [D, 2 * D])
    lamre_d = din("s5_lam_re", [128, 64]); lamim_d = din("s5_lam_im", [128, 64]); logdt_d = din("s5_log_dt", [128, 1])
    bre_d = din("s5_b_re", [128, 1024]); bim_d = din("s5_b_im", [128, 1024])
    cre_d = din("s5_c_re", [128, 1024]); cim_d = din("s5_c_im", [128, 1024])
    s5d_d = din("s5_d_row", [1, D])
    WST = dscr("WST", [128, 128, 128], BF16); WINT = dscr("WINT", [128, 128, 128], BF16)
    WOR = dscr("WOR", [128, 64, 128], BF16); WOI = dscr("WOI", [128, 64, 128], BF16)
    WSTb, WINTb, WORb, WOIb = Buf("WST"), Buf("WINT"), Buf("WOR"), Buf("WOI")
    YT = dscr("YT", [D, T], BF16); YTb = Buf("YT")
    YTv = YT.rearrange("(k p) t -> p k t", p=128)
    identh = sb("identh", [128, 128], BF16); identhb = Buf("identh")
    P.op("vector", lambda e: e.tensor_copy(identh[:], ident[:]), reads=[identb], writes=[identhb])
    cA4 = sb("cA4", [64, 2, 2, 128], F32); cAb = Buf("cA")
    cA1 = cA4[:, 0, :, :]; cA2 = cA4[:, 1, :, :]
    vpb = [Buf("vst0"), Buf("vst1")]
    dtab = sb("dtab", [128, D], F32); dtabb = Buf("dtab")

    def V(fn, r, w):
        return P.op("vector", fn, reads=r, writes=w)

    def A(fn, r, w):
        return P.op("scalar", fn, reads=r, writes=w)

    def tt(out, a, b, op):
        return lambda e: e.tensor_tensor(out=out, in0=a, in1=b, op=op)

    def s5_prep():
        nat = sb("s5nat", [128, 130], F32); natb = Buf("s5nat")
        P.dma("sync", nat[:, 0:64], lamre_d, vdb, natb)
        P.dma("sync", nat[:, 64:128], lamim_d, vdb, natb)
        P.dma("sync", nat[:, 128:129], logdt_d, vdb, natb)
        P.dma("sync", dtab[:], s5d_d.partition_broadcast(128), vdb, dtabb)
        Lr = sb("s5Lr", [128, 9, 64], F32); Li = sb("s5Li", [128, 9, 64], F32); Lb = Buf("s5L")
        sm = sb("s5sm", [128, 12, 64], F32); smb = Buf("s5sm")
        lr = nat[:, 0:64]; li = nat[:, 64:128]
        dtc = sm[:, 11, 0:1]
        A(lambda e: e.activation(dtc, nat[:, 128:129], AF.Exp), [natb], [smb])
        lrdt, lidt, mag, sa, ca, t1, t2, fr, fi, nr, den = [sm[:, i, :] for i in range(11)]
        V(lambda e: e.tensor_scalar(out=lrdt, in0=lr, scalar1=dtc, scalar2=None, op0=ALU.mult), [natb, smb], [smb])
        V(lambda e: e.tensor_scalar(out=lidt, in0=li, scalar1=dtc, scalar2=None, op0=ALU.mult), [natb, smb], [smb])
        A(lambda e: e.activation(mag, lrdt, AF.Exp), [smb], [smb])
        A(lambda e: e.activation(sa, lidt, AF.Sin, scale=1.0 / 16.0), [smb], [smb])
        V(lambda e: e.tensor_scalar(out=ca, in0=lidt, scalar1=1.0 / 16.0, scalar2=PI / 2, op0=ALU.mult, op1=ALU.add), [smb], [smb])
        A(lambda e: e.activation(ca, ca, AF.Sin), [smb], [smb])
        for _ in range(4):
            V(tt(t1, ca, ca, ALU.mult), [smb], [smb])
            V(tt(t2, sa, sa, ALU.mult), [smb], [smb])
            V(tt(sa, ca, sa, ALU.mult), [smb], [smb])
            V(lambda e: e.tensor_scalar(out=sa, in0=sa, scalar1=2.0, scalar2=None, op0=ALU.mult), [smb], [smb])
            V(tt(ca, t1, t2, ALU.subtract), [smb], [smb])
        V(lambda e: e.memset(Lr[:, 0, :], 1.0), [], [Lb])
        V(lambda e: e.memset(Li[:, 0, :], 0.0), [], [Lb])
        V(tt(Lr[:, 1, :], mag, ca, ALU.mult), [smb], [Lb])
        V(tt(Li[:, 1, :], mag, sa, ALU.mult), [smb], [Lb])
        for k in range(2, 9):
            V(tt(t1, Lr[:, k - 1, :], Lr[:, 1, :], ALU.mult), [Lb], [smb])
            V(tt(t2, Li[:, k - 1, :], Li[:, 1, :], ALU.mult), [Lb], [smb])
            V(tt(Lr[:, k, :], t1, t2, ALU.subtract), [smb], [Lb])
            V(tt(t1, Lr[:, k - 1, :], Li[:, 1, :], ALU.mult), [Lb], [smb])
            V(tt(t2, Li[:, k - 1, :], Lr[:, 1, :], ALU.mult), [Lb], [smb])
            V(tt(Li[:, k, :], t1, t2, ALU.add), [smb], [Lb])
        V(lambda e: e.tensor_scalar(out=nr, in0=Lr[:, 1, :], scalar1=-1.0, scalar2=None, op0=ALU.add), [Lb], [smb])
        V(tt(den, lr, lr, ALU.mult), [natb], [smb])
        V(tt(t1, li, li, ALU.mult), [natb], [smb])
        V(tt(den, den, t1, ALU.add), [smb], [smb])
        V(lambda e: e.reciprocal(den, den), [smb], [smb])
        V(tt(t1, nr, lr, ALU.mult), [smb, natb], [smb])
        V(tt(t2, Li[:, 1, :], li, ALU.mult), [Lb, natb], [smb])
        V(tt(fr, t1, t2, ALU.add), [smb], [smb])
        V(tt(fr, fr, den, ALU.mult), [smb], [smb])
        V(tt(t1, Li[:, 1, :], lr, ALU.mult), [Lb, natb], [smb])
        V(tt(t2, nr, li, ALU.mult), [smb, natb], [smb])
        V(tt(fi, t1, t2, ALU.subtract), [smb], [smb])
        V(tt(fi, fi, den, ALU.mult), [smb], [smb])
        for (src, dst0, dst1, neg) in ((Lr[:, 8, :], cA1[:, 0, :], cA1[:, 1, :], False), (Li[:, 8, :], cA2[:, 1, :], cA2[:, 0, :], True)):
            ps, pb = next_ps()
            P.mm([lambda e, ps=ps, src=src: e.transpose(ps[0:64, 0:128], src, ident[:])], reads=[Lb, identb], writes=[pb])
            V(lambda e, ps=ps, dst0=dst0: e.tensor_copy(dst0, ps[0:64, 0:128]), [pb], [cAb])
            if neg:
                V(lambda e, ps=ps, dst1=dst1: e.tensor_scalar(out=dst1, in0=ps[0:64, 0:128], scalar1=-1.0, scalar2=None, op0=ALU.mult), [pb], [cAb])
            else:
                V(lambda e, ps=ps, dst1=dst1: e.tensor_copy(dst1, ps[0:64, 0:128]), [pb], [cAb])
        s5_prep.sm = sm; s5_prep.smb = smb

        w0, w0b = wring.t[0], wring.b[0]
        w1, w1b = wring.t[1], wring.b[1]
        f32v = w1[:].rearrange("p k n -> p (k n)").bitcast(F32)
        bre, bim, bbre, bbim, tA = [f32v[:, i * 1024:(i + 1) * 1024] for i in range(5)]
        g0, g0b = big.t[0], big.b[0]
        g1_, g1b = big.t[1], big.b[1]
        cre, cim = g0[:, 0:1024], g0[:, 1024:2048]
        tB, tC = g1_[:, 0:1024], g1_[:, 1024:2048]
        w1aux = Buf("w1aux")
        P.dma("sync", bre, bre_d, vdb, w1b, sembuf=w1aux)
        P.dma("sync", bim, bim_d, vdb, w1b, sembuf=w1aux)
        P.dma("sync", cre, cre_d, vdb, g0b)
        P.dma("sync", cim, cim_d, vdb, g0b)
        actf = act[:].rearrange("p k t -> p (k t)")
        stage = actf[:, 0:16384].rearrange("p (g j) -> p g j", j=128)
        Mre = actf[:, 16384:24576].rearrange("p (q s c) -> p q s c", q=64, s=8)
        Mim = actf[:, 24576:32768].rearrange("p (q s c) -> p q s c", q=64, s=8)
        Nre = actf[:, 32768:40960].rearrange("p (t o q) -> p t o q", t=8, o=16)
        Nim = w0[:].rearrange("p k n -> p (k n)")[:, 0:8192].rearrange("p (t o q) -> p t o q", t=8, o=16)
        Krev = xto.t[0][:].rearrange("p k t -> p (k t)").bitcast(BF16)[:, 0:4096].rearrange("p (s o c) -> p s o c", s=16, o=16)
        KT = xto.t[1][:].rearrange("p k t -> p (k t)").bitcast(BF16)[:, 0:2048].rearrange("p (o s c) -> p o s c", o=16, s=8)
        krb, ktb = xto.b[0], xto.b[1]
        b3 = lambda a: a.rearrange("p (q c) -> p q c", c=16)
        bc3 = lambda a: a.unsqueeze(2).broadcast_to([128, 64, 16])
        V(tt(b3(tA), b3(bre), bc3(fr), ALU.mult), [w1b, smb], [w1b])
        V(tt(b3(tB), b3(bim), bc3(fi), ALU.mult), [w1b, smb], [g1b])
        V(tt(bbre, tA, tB, ALU.subtract), [w1b, g1b], [w1b])
        V(tt(b3(tA), b3(bim), bc3(fr), ALU.mult), [w1b, smb], [w1b])
        V(tt(b3(tB), b3(bre), bc3(fi), ALU.mult), [w1b, smb], [g1b])
        V(tt(bbim, tA, tB, ALU.add), [w1b, g1b], [w1b])
        for s_ in range(8):
            k = 7 - s_
            V(tt(b3(tA), b3(bbre), bc3(Lr[:, k, :]), ALU.mult), [w1b, Lb], [w1b])
            V(tt(b3(tB), b3(bbim), bc3(Li[:, k, :]), ALU.mult), [w1b, Lb], [g1b])
            V(tt(Mre[:, :, s_, :], b3(tA), b3(tB), ALU.subtract), [w1b, g1b], [actb])
            V(tt(b3(tA), b3(bbim), bc3(Lr[:, k, :]), ALU.mult), [w1b, Lb], [w1b])
            V(tt(b3(tB), b3(bbre), bc3(Li[:, k, :]), ALU.mult), [w1b, Lb], [g1b])
            V(tt(Mim[:, :, s_, :], b3(tA), b3(tB), ALU.add), [w1b, g1b], [actb])
        V(lambda e: e.memset(Krev, 0.0), [], [krb])
        c3 = lambda a: a.rearrange("p (o q) -> p o q", q=64)
        bo3 = lambda a: a.unsqueeze(1).broadcast_to([128, 16, 64])
        nre_f = tC.rearrange("p (o q) -> p o q", q=64)
        nim_f = f32v[:, 5120:5632]
        nimb = hbo.b[1]
        nim_f = hbo.t[1][:].rearrange("p k t -> p (k t)").bitcast(F32).rearrange("p (o q) -> p o q", q=64)
        kred = sb("kred", [128, 16], F32); kredb = Buf("kred")
        bbreT = bbre.rearrange("p (q c) -> p c q", c=16)
        bbimT = bbim.rearrange("p (q c) -> p c q", c=16)
        for k in range(9):
            V(tt(c3(tA), c3(cre), bo3(Lr[:, k, :]), ALU.mult), [g0b, Lb], [w1b])
            V(tt(c3(tB), c3(cim), bo3(Li[:, k, :]), ALU.mult), [g0b, Lb], [g1b])
            V(tt(nre_f, c3(tA), c3(tB), ALU.subtract), [w1b, g1b], [g1b])
            V(tt(c3(tA), c3(cre), bo3(Li[:, k, :]), ALU.mult), [g0b, Lb], [w1b])
            V(tt(c3(tB), c3(cim), bo3(Lr[:, k, :]), ALU.mult), [g0b, Lb], [g1b])
            V(tt(nim_f, c3(tA), c3(tB), ALU.add), [w1b, g1b], [nimb])
            if k >= 1:
                V(lambda e, k=k: e.tensor_copy(Nre[:, k - 1, :, :], nre_f), [g1b], [actb])
                V(lambda e, k=k: e.tensor_scalar(out=Nim[:, k - 1, :, :], in0=nim_f, scalar1=-1.0, scalar2=None, op0=ALU.mult), [nimb], [w0b])
            if k <= 7:
                for o in range(16):
                    V(tt(tA.rearrange("p (c q) -> p c q", q=64), bbreT, nre_f[:, o, :].unsqueeze(1).broadcast_to([128, 16, 64]), ALU.mult), [w1b, g1b], [w1b])
                    V(tt(tB.rearrange("p (c q) -> p c q", q=64), bbimT, nim_f[:, o, :].unsqueeze(1).broadcast_to([128, 16, 64]), ALU.mult), [w1b, nimb, g1b], [g1b])
                    V(tt(tA, tA, tB, ALU.subtract), [w1b, g1b], [w1b])
                    V(lambda e, k=k, o=o: e.tensor_reduce(out=kred[:], in_=tA.rearrange("p (c q) -> p c q", q=64),
                                                            axis=AX.X, op=ALU.add), [w1b], [kredb])
                    V(lambda e, k=k, o=o: e.tensor_copy(Krev[:, 7 - k, o, :], kred[:]), [kredb], [krb])

        def family(n_in, src_fn, src_bufs, dram, dramb, rows):
            for j4 in range(32):
                ps, pb = next_ps()
                psh = ps[:].bitcast(BF16)
                for q in range(4):
                    j = j4 * 4 + q
                    P.mm([lambda e, psh=psh, q=q, j=j: e.transpose(psh[0:n_in, q * 128:(q + 1) * 128], src_fn(j), identh[:])],
                         reads=src_bufs + [identhb], writes=[pb])
                cp = (lambda e, psh=psh, j4=j4: e.tensor_copy(stage[0:n_in, :, j4 * 4:(j4 + 1) * 4],
                                                              psh[0:n_in, 0:512].rearrange("p (j g) -> p g j", j=4)))
                if j4 % 2 == 0:
                    V(cp, [pb], [actb])
                else:
                    A(lambda e, psh=psh, j4=j4: e.copy(stage[0:n_in, :, j4 * 4:(j4 + 1) * 4],
                                                       psh[0:n_in, 0:512].rearrange("p (j g) -> p g j", j=4)), [pb], [actb])
            P.dma("sync", dram.rearrange("g r j -> r g j"), stage[0:rows, :, :], actb, dramb, sembuf=actb)

        family(128, lambda j: (Mre if j < 64 else Mim)[:, j % 64, :, :].rearrange("p s c -> p (s c)"), [actb], WST, WSTb, 128)
        family(64, lambda j: Nre[:, j // 16, j % 16, :], [actb], WOR, WORb, 64)
        family(64, lambda j: Nim[:, j // 16, j % 16, :], [w0b], WOI, WOIb, 64)
        for t_ in range(8):
            V(lambda e, t_=t_: e.tensor_copy(KT, Krev[:, 7 - t_: 15 - t_, :, :].rearrange("p s o c -> p o s c")), [krb], [ktb])
            for j4 in range(4):
                ps, pb = next_ps()
                psh = ps[:].bitcast(BF16)
                for q in range(4):
                    o = j4 * 4 + q
                    P.mm([lambda e, psh=psh, q=q, o=o: e.transpose(psh[:, q * 128:(q + 1) * 128],
                                                                   KT[:, o, :, :].rearrange("p s c -> p (s c)"), identh[:])],
                         reads=[ktb, identhb], writes=[pb])
                jb = t_ * 16 + j4 * 4
                V(lambda e, psh=psh, jb=jb: e.tensor_copy(stage[:, :, jb:jb + 4],
                                                          psh[:, 0:512].rearrange("p (j g) -> p g j", j=4)), [pb], [actb])
        P.dma("sync", WINT.rearrange("g r j -> r g j"), stage[:, :, :], actb, WINTb, sembuf=actb)
        P.fence("sync", [actb])

    def s5_main():
        actf = act[:].rearrange("p k t -> p (k t)")
        hsb = actf[:, 0:16384].rearrange("p (k t) -> p k t", k=KD)
        wsec = actf[:, 16384:32768].rearrange("p (w g j) -> p w g j", w=4, g=32)
        wsecb = Buf("wsec")
        sp_ = actf[0:64, 32768:40960].rearrange("p (r g c) -> p r g c", r=2, g=32)
        spb_ = Buf("sprev")
        xb_ = actf[:, 40960:45056].rearrange("p (g c) -> p g c", c=128)
        xbb = Buf("xblk")
        ub = xto.t[0][:].rearrange("p k t -> p (k t)").bitcast(BF16).rearrange("p (g s c) -> p g s c", g=32, s=8)
        ubb = xto.b[0]
        ym = xto.t[1][:].rearrange("p k t -> p (k t)").bitcast(BF16).rearrange("p (t c) -> p t c", c=512)
        ymb = xto.b[1]
        yos = [hbo.t[i][:].rearrange("p k t -> p (k t)").rearrange("p (a t) -> p a t", a=2) for i in range(2)]
        yobs = [hbo.b[0], hbo.b[1]]
        slr = [big.t[ri][:].bitcast(BF16)[0:64, :].rearrange("p (g c) -> p g c", c=128) for ri in range(2)]
        slbs = [big.b[0], big.b[1]]
        tsc = sb("tsc", [64, 2, 2, 32], F32)
        usc = sb("usc", [64, 2, 32], F32)
        tscb = Buf("tsc")
        uscb = Buf("usc")
        vst = s5_prep.sm[0:64, :, :].rearrange("p a b -> p (a b)").rearrange("p (j q s g) -> p j q s g", j=4, q=2, s=3)
        V(lambda e: e.memset(vst, 0.0), [], [s5_prep.smb, vpb[0], vpb[1]])
        for SBi in range(2):
            norm_phase(0, 0, to_act=True, q0=SBi * 8, q1=SBi * 8 + 8,
                       dst_fn=lambda k, q, SBi=SBi: hsb[:, k, (q - SBi * 8) * 128:(q - SBi * 8 + 1) * 128])
            for j in range(4):
                gb = j * 32
                wt, wb = wring.next()
                P.dma("gpsimd", wt[:, :KD, :], s5_in_d[:, j * 512:(j + 1) * 512].rearrange("(k p) n -> p k n", p=128), wbuf_d, wb)
                P.dma("sync", wsec[:, 0, :, :], WST[gb:gb + 32].rearrange("g r j -> r g j"), WSTb, wsecb)
                P.dma("sync", wsec[:, 1, :, :], WINT[gb:gb + 32].rearrange("g r j -> r g j"), WINTb, wsecb)
                P.dma("sync", wsec[0:64, 2, :, :], WOR[gb:gb + 32].rearrange("g r j -> r g j"), WORb, wsecb)
                P.dma("sync", wsec[0:64, 3, :, :], WOI[gb:gb + 32].rearrange("g r j -> r g j"), WOIb, wsecb)
                for s_ in range(8):
                    ps, pb = next_ps()
                    fns = []
                    for k in range(KD):
                        fns.append(lambda e, ps=ps, k=k, s_=s_, wt=wt: e.matmul(
                            ps[:], lhsT=hsb[:, k, s_:1024:8], rhs=wt[:, k, :],
                            start=(k == 0), stop=(k == KD - 1)))
                    P.mm(fns, reads=[actb, wb], writes=[pb])
                    src = ps[:].rearrange("p (g c) -> p g c", c=16)
                    A(lambda e, s_=s_, src=src: e.copy(ub[:, :, s_, :], src), [pb], [ubb])
                for g4 in range(8):
                    ps, pb = next_ps()
                    psh = ps[:].bitcast(BF16)
                    for q in range(4):
                        gl = g4 * 4 + q
                        P.mm([lambda e, psh=psh, q=q, gl=gl: e.transpose(
                            psh[:, q * 128:(q + 1) * 128],
                            ub[:, gl, :, :].rearrange("p s c -> p (s c)"), identh[:])],
                            reads=[ubb, identhb], writes=[pb])
                    A(lambda e, psh=psh, g4=g4: e.copy(
                        xb_[:, g4 * 4:(g4 + 1) * 4, :], psh[:, 0:512].rearrange("p (g c) -> p g c", g=4)), [pb], [xbb])
                for g2 in range(16):
                    ps, pb = next_ps()
                    for q in range(2):
                        gl = g2 * 2 + q
                        for ri in range(2):
                            P.mm([lambda e, ps=ps, q=q, ri=ri, gl=gl: e.matmul(
                                ps[0:64, (q * 2 + ri) * 128:(q * 2 + ri + 1) * 128],
                                lhsT=wsec[:, 0, gl, ri * 64:(ri + 1) * 64], rhs=xb_[:, gl, :], start=True, stop=True)],
                                reads=[wsecb, xbb], writes=[pb])
                    for ri in range(2):
                        A(lambda e, ps=ps, g2=g2, ri=ri: e.copy(
                            slr[ri][:, g2 * 2:(g2 + 1) * 2, :],
                            ps[0:64, :].rearrange("p (q r c) -> p r q c", q=2, r=2)[:, ri, :, :]), [pb], [slbs[ri]])
                c4 = cA4[:, :, :, gb:gb + 32]
                for c_ in range(128):
                    rd, wr = c_ % 2, 1 - (c_ % 2)
                    v3 = vst[:, j, rd, :, :]
                    w3 = vst[:, j, wr, :, :]
                    win = bass.AP(v3.tensor, v3.offset, [list(v3.ap[0]), [32, 2], [32, 2], [1, 32]])
                    w02 = bass.AP(w3.tensor, w3.offset, [list(w3.ap[0]), [64, 2], [1, 32]])
                    V(tt(tsc[:], c4, win, ALU.mult), [cAb, vpb[rd]], [tscb])
                    V(tt(usc[:], tsc[:, 0, :, :], tsc[:, 1, :, :], ALU.add), [tscb], [uscb])
                    A(lambda e, c_=c_, v3=v3: e.copy(sp_[:, :, :, c_], v3[:, 0:2, :]), [vpb[rd]], [spb_])
                    V(tt(w02, usc[:, 0, :].unsqueeze(1).broadcast_to([64, 2, 32]),
                         slr[0][:, :, c_].unsqueeze(1).broadcast_to([64, 2, 32]), ALU.add), [uscb, slbs[0]], [vpb[wr]])
                    V(tt(w3[:, 1, :], usc[:, 1, :], slr[1][:, :, c_], ALU.add), [uscb, slbs[1]], [vpb[wr]])
                for g4 in range(8):
                    ps, pb = next_ps()
                    for q in range(4):
                        gl = g4 * 4 + q
                        P.mm([lambda e, ps=ps, q=q, gl=gl: e.matmul(
                                ps[:, q * 128:(q + 1) * 128], lhsT=xb_[:, gl, :], rhs=wsec[:, 1, gl, :], start=True, stop=False),
                              lambda e, ps=ps, q=q, gl=gl: e.matmul(
                                ps[:, q * 128:(q + 1) * 128], lhsT=sp_[:, 0, gl, :], rhs=wsec[0:64, 2, gl, :], start=False, stop=False),
                              lambda e, ps=ps, q=q, gl=gl: e.matmul(
                                ps[:, q * 128:(q + 1) * 128], lhsT=sp_[:, 1, gl, :], rhs=wsec[0:64, 3, gl, :], start=False, stop=True)],
                             reads=[xbb, wsecb, spb_], writes=[pb])
                    yf_, yfb = tmpf.next()
                    yf = yf_[:].rearrange("p (g t c) -> p g t c", g=4, t=8)
                    ch0 = (gb + g4 * 4) * 16
                    P.op("gpsimd", tt(yf, ub[:, g4 * 4: g4 * 4 + 4, :, :],
                         dtab[:, ch0:ch0 + 64].rearrange("p (g c) -> p g c", c=16).unsqueeze(2).broadcast_to([128, 4, 8, 16]),
                         ALU.mult), reads=[ubb, dtabb], writes=[yfb])
                    V(tt(yf, yf, ps[:].rearrange("p (g t c) -> p g t c", g=4, t=8), ALU.add), [yfb, pb], [yfb])
                    ho, hob = gelu_tile(yf_[:], yfb, 512, eng="gpsimd")
                    P.op("gpsimd", lambda e, ho=ho, g4=g4: e.tensor_copy(
                        ym[:, :, g4 * 64:(g4 + 1) * 64].rearrange("p t (g c) -> p g t c", c=16),
                        ho[:, :512].rearrange("p (g t c) -> p g t c", g=4, t=8)), reads=[hob], writes=[ymb])
                for t4 in range(8):
                    ps, pb = next_ps()
                    psh = ps[:].bitcast(BF16)
                    for q in range(4):
                        idx = t4 * 4 + q
                        t_, ct = idx // 4, idx % 4
                        P.mm([lambda e, psh=psh, q=q, t_=t_, ct=ct: e.transpose(
                            psh[:, q * 128:(q + 1) * 128], ym[:, t_, ct * 128:(ct + 1) * 128], identh[:])],
                            reads=[ymb, identhb], writes=[pb])
                    for q in range(4):
                        idx = t4 * 4 + q
                        t_, ct = idx // 4, idx % 4
                        A(lambda e, psh=psh, q=q, t_=t_, ct=ct: e.copy(
                            yos[ct // 2][:, ct % 2, t_:1024:8], psh[:, q * 128:(q + 1) * 128]), [pb], [yobs[ct // 2]])
                kt0 = gb // 8
                for hh in range(2):
                    P.dma("sync", YTv[:, kt0 + 2 * hh: kt0 + 2 * hh + 2, SBi * 1024:(SBi + 1) * 1024], yos[hh], yobs[hh], YTb, sembuf=yobs[hh])
        P.fence("sync", hbo.b)
        load_act(YTv, KD, YTb)

        def ep_glu(si, m, tb, pss):
            (pv, pvb), (pg, pgb) = pss
            kk = si * 4 + m
            sg, sgb = tmpf.next()
            A(lambda e: e.activation(sg[:], pg[:], AF.Sigmoid), [pgb], [sgb])
            V(tt(sg[:], sg[:], pv[:], ALU.mult), [sgb, pvb], [sgb])
            xt_, xb2 = xpend.pop((kk, tb))
            V(lambda e: e.scalar_tensor_tensor(out=xt_[:], in0=sg[:], scalar=mcol(0, 2, kk), in1=xt_[:],
                                               op0=ALU.mult, op1=ALU.add), [sgb, xb2, modb], [xb2])
            P.dma("sync", XTv[:, kk, tb * TB:(tb + 1) * TB], xt_[:], xb2, XTb, sembuf=xb2)

        gemm([[s5_glu_d[:, jj * 512:(jj + 1) * 512], s5_glu_d[:, D + jj * 512: D + (jj + 1) * 512]] for jj in range(4)], KD, ep_glu, pre=True)
        P.fence("sync", tmpf.b)

    def final_phase():
        P.fence("sync", [XTb])
        for q in range(T // 128):
            xt_, xb_ = big.next()
            xv = xt_[:].rearrange("p (k t) -> p k t", k=KD)
            P.dma("sync", xv, XTv[:, :, q * 128:(q + 1) * 128], XTb, xb_)
            ht, hb = hbo.next()
            A(lambda e, ht=ht, xv=xv: e.activation(ht[:], xv, AF.Square), [xb_], [hb])
            ps, pb = next_ps()
            fns = []
            for k in range(KD):
                fns.append(lambda e, ps=ps, ht=ht, k=k: e.matmul(
                    ps[:, :128], lhsT=ones_bf[:], rhs=ht[:, k, :], start=(k == 0), stop=(k == KD - 1)))
            P.mm(fns, reads=[hb, onesb], writes=[pb])
            rs, rb = tmpf.next()
            V(lambda e, rs=rs, ps=ps: e.tensor_scalar(out=rs[:, :128], in0=ps[:, :128], scalar1=1.0 / D, scalar2=EPS,
                                                      op0=ALU.mult, op1=ALU.add), [pb], [rb])
            A(lambda e, rs=rs: e.activation(rs[:, :128], rs[:, :128], AF.Sqrt), [rb], [rb])
            V(lambda e, rs=rs: e.reciprocal(rs[:, :128], rs[:, :128]), [rb], [rb])
            for k in range(KD):
                V(lambda e, xv=xv, k=k, rs=rs: e.scalar_tensor_tensor(
                    out=xv[:, k, :], in0=xv[:, k, :], scalar=colsA[:, 80 + k:81 + k], in1=rs[:, :128],
                    op0=ALU.mult, op1=ALU.mult), [xb_, rb, colsAb], [xb_])
            ot, ob = xto.next()
            otv = ot[:].rearrange("p k t -> p (k t)")
            for k4 in range(4):
                ps2, pb2 = next_ps()
                for jq in range(4):
                    k = k4 * 4 + jq
                    P.mm([lambda e, ps2=ps2, jq=jq, k=k, xv=xv: e.transpose(
                        ps2[:, jq * 128:(jq + 1) * 128], xv[:, k, :], ident[:])], reads=[xb_, identb], writes=[pb2])
                if k4 % 2 == 0:
                    V(lambda e, ps2=ps2, k4=k4, otv=otv: e.tensor_copy(otv[:, k4 * 512:(k4 + 1) * 512], ps2[:]), [pb2], [ob])
                else:
                    A(lambda e, ps2=ps2, k4=k4, otv=otv: e.copy(otv[:, k4 * 512:(k4 + 1) * 512], ps2[:]), [pb2], [ob])
            P.dma("sync", out_d[q * 128:(q + 1) * 128, :], otv, ob, Buf("outd"), sembuf=ob)
        P.fence("sync", xto.b)

    if dbg == "S5":
        s5_prep()
        s5_main()
    snaps = []

    def snap(name):
        d_ = nc.dram_tensor(name, [D, T], F32, kind="ExternalOutput").ap()
        P.fence("sync", [XTb] + tmpf.b)
        b_ = Buf(name)
        P.dma("sync", d_, XT, XTb, b_)
        snaps.append(b_)

    if dbg is None or dbg == "ALL":
        s5_prep()
        s5_main()
        if dbg: snap("dbg0")
        ffn(0)
        if dbg: snap("dbg1")
        lru_mixer(1)
        if dbg: snap("dbg2")
        ffn(1)
        if dbg: snap("dbg3")
        final_phase()
        P.fence("sync", snaps)

    if dbg == "FFN0":
        ffn(0)

    if dbg in ("XT", "FFN0", "LRU", "S5"):
        dbg_d = nc.dram_tensor("dbg", [D, T], F32, kind="ExternalOutput").ap()
        P.fence("sync", [XTb] + tmpf.b)
        dbgb = Buf("dbgb")
        P.dma("sync", dbg_d, XT, XTb, dbgb)
        P.fence("sync", [dbgb])
    P.emit()
    return nc, es


def kernel(**inputs):
    dbg = os.environ.get("KDBG")
    nc, es = build(dbg=dbg)
    f = lambda a: np.ascontiguousarray(a, dtype=np.float32)
    x = f(inputs["x"]); c = f(inputs["c"])
    ng = f(inputs["norm_g"]).reshape(64, 128)
    fg = f(inputs["final_g"]).reshape(16, 128)
    sd = f(inputs["s5_d"]).reshape(16, 128)
    vecB = f(inputs["b_ada"]).reshape(2, 96, 128)
    vecC = np.zeros((128, 128), np.float32)
    vecC[0:88] = f(inputs["lru_conv_w"]).reshape(88, 128)
    vecC[88:110] = f(inputs["lru_conv_b"]).reshape(22, 128)
    vecD = np.zeros((128, 128), np.float32)
    vecD[0:22] = f(inputs["lru_b_rg"]).reshape(22, 128)
    vecD[22:44] = f(inputs["lru_b_ig"]).reshape(22, 128)
    vecD[44:66] = f(inputs["lru_lam"]).reshape(22, 128)
    def dense_bd(w):
        w = f(w)[0]
        o = np.zeros((LW, LW), np.float32)
        for h in range(16):
            o[h * 176:(h + 1) * 176, h * 176:(h + 1) * 176] = w[h]
        return o
    rgd = dense_bd(inputs["lru_w_rg"]); igd = dense_bd(inputs["lru_w_ig"])
    in_maps = []
    for b in range(NCORES):
        vecA = np.zeros((128, 128), np.float32)
        vecA[0:16] = c[b].reshape(16, 128)
        vecA[16:80] = ng
        vecA[80:96] = fg
        vecA[96:112] = sd
        in_maps.append({"x": x[b], "c": c[b].reshape(KD, 128), "ident": np.eye(128, dtype=np.float32),
                        "w_ada": f(inputs["w_ada"]), "vecA": vecA, "vecB": vecB, "vecC": vecC, "vecD": vecD,
                        "ffn_w_gu": f(inputs["ffn_w_gu"]), "ffn_w_down": f(inputs["ffn_w_down"]),
                        "lru_w_in": f(inputs["lru_w_in"])[0], "lru_rg_dense": rgd, "lru_ig_dense": igd,
                        "lru_w_out": f(inputs["lru_w_out"])[0],
                        "s5_w_in": f(inputs["s5_w_in"])[0], "s5_w_glu": f(inputs["s5_w_glu"])[0],
                        "s5_lam_re": f(inputs["s5_lam_re"])[0], "s5_lam_im": f(inputs["s5_lam_im"])[0],
                        "s5_log_dt": f(inputs["s5_log_dt"]).reshape(128, 1),
                        "s5_b_re": f(inputs["s5_b_re"]).reshape(128, 1024), "s5_b_im": f(inputs["s5_b_im"]).reshape(128, 1024),
                        "s5_c_re": f(inputs["s5_c_re"]).reshape(128, 1024), "s5_c_im": f(inputs["s5_c_im"]).reshape(128, 1024),
                        "s5_d_row": f(inputs["s5_d"]).reshape(1, D)})
    res = run_bass_kernel_spmd(nc, in_maps, core_ids=list(range(NCORES)))
    es.close()
    if dbg == "ALL":
        return [{k: r[k] for k in ("dbg0", "dbg1", "dbg2", "dbg3", "out")} for r in res.results]
    if dbg:
        return [r["dbg"] for r in res.results]
    return np.stack([r["out"] for r in res.results], axis=0)
```
